# Optimizing a Trainium2 kernel written in Bass

```python
import jax, jax.numpy as jnp
from jax import lax
import numpy as np

D_MODEL = 1024
BATCH = 8
SEQ = 2048
DEPTH = 2

D_FF = 2816
NORM_EPS = 1e-6
LRU_WIDTH = 512
LRU_HEADS = 8
LRU_HEAD_DIM = LRU_WIDTH // LRU_HEADS
LRU_CONV_WIDTH = 4
LRU_C = 8.0
MLA_HEADS = 8
QK_NOPE_DIM = 64
QK_ROPE_DIM = 32
V_HEAD_DIM = 64
Q_LORA_RANK = 384
KV_LORA_RANK = 256
ROPE_THETA = 10000.0
Q_BLOCK = 128
CONV_CH = 512
CONV_WIDTH = 31
N_BRANCH = 3
IN_A = 2 * LRU_WIDTH
IN_B = Q_LORA_RANK + KV_LORA_RANK + QK_ROPE_DIM
IN_C = 2 * CONV_CH
IN_G = N_BRANCH * D_MODEL
D_IN = IN_A + IN_B + IN_C + IN_G
MAX_POS_OFFSET = 4096

kernel_name = 'hybrid_rglru_mla_conformer_macaron'


def rms_norm(x, g):
    xf = x.astype(jnp.float32)
    y = xf * lax.rsqrt(jnp.mean(xf * xf, axis=-1, keepdims=True) + NORM_EPS)
    return (y * g.astype(jnp.float32)).astype(x.dtype)


def layer_norm(x, g, b):
    xf = x.astype(jnp.float32)
    mu = jnp.mean(xf, axis=-1, keepdims=True)
    var = jnp.mean(jnp.square(xf - mu), axis=-1, keepdims=True)
    y = (xf - mu) * lax.rsqrt(var + NORM_EPS)
    return (y * g.astype(jnp.float32) + b.astype(jnp.float32)).astype(x.dtype)


def swiglu(x, w1, w2):
    gu = x @ w1
    g, u = gu[..., :D_FF], gu[..., D_FF:]
    return (jax.nn.silu(g) * u) @ w2


def causal_depthwise_conv(x, w, b):
    k = w.shape[0]
    y = lax.conv_general_dilated(
        x, w[:, None, :].astype(x.dtype), window_strides=(1,), padding=[(k - 1, 0)],
        dimension_numbers=('NWC', 'WIO', 'NWC'), feature_group_count=x.shape[-1])
    return y + b.astype(x.dtype)


def rg_lru(x, w_gate, b_gate, lam):
    b, s, w = x.shape
    xf = x.astype(jnp.float32)
    xh = xf.reshape(b, s, LRU_HEADS, LRU_HEAD_DIM)
    gates = jnp.einsum('bshd,hde->bshe', xh, w_gate.astype(jnp.float32)) + b_gate.astype(jnp.float32)
    r = jax.nn.sigmoid(gates[..., :LRU_HEAD_DIM]).reshape(b, s, w)
    i = jax.nn.sigmoid(gates[..., LRU_HEAD_DIM:]).reshape(b, s, w)
    log_a = -LRU_C * r * jax.nn.softplus(-lam.astype(jnp.float32))
    a = jnp.exp(log_a)
    u = jnp.sqrt(-jnp.expm1(2.0 * log_a)) * (i * xf)

    def combine(left, right):
        a_l, h_l = left
        a_r, h_r = right
        return a_l * a_r, a_r * h_l + h_r

    _, h = lax.associative_scan(combine, (a, u), axis=1)
    return h.astype(x.dtype)


def rope_tables(positions):
    inv_freq = ROPE_THETA ** (-jnp.arange(0, QK_ROPE_DIM, 2, dtype=jnp.float32) / QK_ROPE_DIM)
    ang = positions.astype(jnp.float32)[..., None] * inv_freq
    return jnp.cos(ang), jnp.sin(ang)


def apply_rope(x, cos, sin):
    half = x.shape[-1] // 2
    x1, x2 = x[..., :half], x[..., half:]
    cos = cos.astype(x.dtype)
    sin = sin.astype(x.dtype)
    return jnp.concatenate([x1 * cos - x2 * sin, x2 * cos + x1 * sin], axis=-1)


def mla_branch(cq, ckv, kpe, positions, q_norm, w_uq, kv_norm, w_ukv, w_o):
    b, s, _ = cq.shape
    q = (rms_norm(cq, q_norm) @ w_uq).reshape(b, s, MLA_HEADS, QK_NOPE_DIM + QK_ROPE_DIM)
    q_nope, q_pe = q[..., :QK_NOPE_DIM], q[..., QK_NOPE_DIM:]
    kv = (rms_norm(ckv, kv_norm) @ w_ukv).reshape(b, s, MLA_HEADS, QK_NOPE_DIM + V_HEAD_DIM)
    k_nope, v = kv[..., :QK_NOPE_DIM], kv[..., QK_NOPE_DIM:]
    cos, sin = rope_tables(positions)
    q_pe = apply_rope(q_pe, cos[:, :, None, :], sin[:, :, None, :])
    k_pe = apply_rope(kpe, cos, sin)
    n_blk = s // Q_BLOCK
    qn_blocks = q_nope.reshape(b, n_blk, Q_BLOCK, MLA_HEADS, QK_NOPE_DIM).swapaxes(0, 1)
    qp_blocks = q_pe.reshape(b, n_blk, Q_BLOCK, MLA_HEADS, QK_ROPE_DIM).swapaxes(0, 1)
    scale = (QK_NOPE_DIM + QK_ROPE_DIM) ** -0.5
    key_idx = jnp.arange(s)

    def attend(args):
        qn, qp, blk = args
        sc = (jnp.einsum('bqhd,bkhd->bhqk', qn, k_nope, preferred_element_type=jnp.float32)
              + jnp.einsum('bqhr,bkr->bhqk', qp, k_pe, preferred_element_type=jnp.float32)) * scale
        q_idx = blk * Q_BLOCK + jnp.arange(Q_BLOCK)
        mask = key_idx[None, :] <= q_idx[:, None]
        sc = jnp.where(mask, sc, jnp.finfo(jnp.float32).min)
        p = jax.nn.softmax(sc, axis=-1).astype(v.dtype)
        return jnp.einsum('bhqk,bkhd->bqhd', p, v)

    o = lax.map(attend, (qn_blocks, qp_blocks, jnp.arange(n_blk)))
    o = o.swapaxes(0, 1).reshape(b, s, MLA_HEADS * V_HEAD_DIM)
    return o @ w_o


def conformer_conv_branch(pc, dw_w, dw_b, ln_g, ln_b, w_pw, b_pw):
    c = pc[..., :CONV_CH] * jax.nn.sigmoid(pc[..., CONV_CH:])
    c = causal_depthwise_conv(c, dw_w, dw_b)
    c = jax.nn.silu(layer_norm(c, ln_g, ln_b))
    return c @ w_pw + b_pw


def hybrid_mixer(h, positions, w_in, b_in, lru_conv_w, lru_conv_b, lru_w_gate, lru_b_gate,
                 lru_lambda, lru_w_out, q_norm, w_uq, kv_norm, w_ukv, mla_w_o,
                 conv_dw_w, conv_dw_b, conv_ln_g, conv_ln_b, conv_w_out, conv_b_out, w_out):
    proj = h @ w_in + b_in
    o1, o2, o3 = IN_A, IN_A + IN_B, IN_A + IN_B + IN_C
    pa, pb, pc, pg = proj[..., :o1], proj[..., o1:o2], proj[..., o2:o3], proj[..., o3:]
    xa = causal_depthwise_conv(pa[..., :LRU_WIDTH], lru_conv_w, lru_conv_b)
    y_a = (rg_lru(xa, lru_w_gate, lru_b_gate, lru_lambda) * jax.nn.gelu(pa[..., LRU_WIDTH:])) @ lru_w_out
    cq = pb[..., :Q_LORA_RANK]
    ckv = pb[..., Q_LORA_RANK:Q_LORA_RANK + KV_LORA_RANK]
    kpe = pb[..., Q_LORA_RANK + KV_LORA_RANK:]
    y_b = mla_branch(cq, ckv, kpe, positions, q_norm, w_uq, kv_norm, w_ukv, mla_w_o)
    y_c = conformer_conv_branch(pc, conv_dw_w, conv_dw_b, conv_ln_g, conv_ln_b, conv_w_out, conv_b_out)
    gates = jax.nn.sigmoid(pg.astype(jnp.float32)).astype(h.dtype)
    gates = gates.reshape(*pg.shape[:-1], N_BRANCH, D_MODEL)
    merged = gates[..., 0, :] * y_a + gates[..., 1, :] * y_b + gates[..., 2, :] * y_c
    return merged @ w_out


def setup_inputs(seed: int = 0) -> dict:
    key = jax.random.key(seed)
    ks = iter(jax.random.split(key, 48))
    L = DEPTH

    def w(shape, fan_in):
        return jax.random.normal(next(ks), shape, jnp.float32) * fan_in ** -0.5

    def gain(shape):
        return 1.0 + 0.02 * jax.random.normal(next(ks), shape, jnp.float32)

    def bias(shape):
        return 0.02 * jax.random.normal(next(ks), shape, jnp.float32)

    x = jax.random.normal(next(ks), (BATCH, SEQ, D_MODEL), jnp.float32)
    offsets = jax.random.randint(next(ks), (BATCH, 1), 0, MAX_POS_OFFSET, dtype=jnp.int32)
    positions = offsets + jnp.arange(SEQ, dtype=jnp.int32)[None, :]
    u = jax.random.uniform(next(ks), (L, LRU_WIDTH), jnp.float32, 0.9, 0.999)
    a0 = u ** (1.0 / LRU_C)
    lru_lambda = jnp.log(a0) - jnp.log1p(-a0)
    return {
        'x': x,
        'positions': positions,
        'ffn1_norm': gain((L, D_MODEL)),
        'ffn1_w1': w((L, D_MODEL, 2 * D_FF), D_MODEL),
        'ffn1_w2': w((L, D_FF, D_MODEL), D_FF),
        'mix_norm': gain((L, D_MODEL)),
        'w_in': w((L, D_MODEL, D_IN), D_MODEL),
        'b_in': bias((L, D_IN)),
        'lru_conv_w': w((L, LRU_CONV_WIDTH, LRU_WIDTH), LRU_CONV_WIDTH),
        'lru_conv_b': bias((L, LRU_WIDTH)),
        'lru_w_gate': w((L, LRU_HEADS, LRU_HEAD_DIM, 2 * LRU_HEAD_DIM), LRU_HEAD_DIM),
        'lru_b_gate': bias((L, LRU_HEADS, 2 * LRU_HEAD_DIM)),
        'lru_lambda': lru_lambda,
        'lru_w_out': w((L, LRU_WIDTH, D_MODEL), LRU_WIDTH),
        'q_norm': gain((L, Q_LORA_RANK)),
        'w_uq': w((L, Q_LORA_RANK, MLA_HEADS * (QK_NOPE_DIM + QK_ROPE_DIM)), Q_LORA_RANK),
        'kv_norm': gain((L, KV_LORA_RANK)),
        'w_ukv': w((L, KV_LORA_RANK, MLA_HEADS * (QK_NOPE_DIM + V_HEAD_DIM)), KV_LORA_RANK),
        'mla_w_o': w((L, MLA_HEADS * V_HEAD_DIM, D_MODEL), MLA_HEADS * V_HEAD_DIM),
        'conv_dw_w': w((L, CONV_WIDTH, CONV_CH), CONV_WIDTH),
        'conv_dw_b': bias((L, CONV_CH)),
        'conv_ln_g': gain((L, CONV_CH)),
        'conv_ln_b': bias((L, CONV_CH)),
        'conv_w_out': w((L, CONV_CH, D_MODEL), CONV_CH),
        'conv_b_out': bias((L, D_MODEL)),
        'w_out': w((L, D_MODEL, D_MODEL), D_MODEL),
        'ffn2_norm': gain((L, D_MODEL)),
        'ffn2_w1': w((L, D_MODEL, 2 * D_FF), D_MODEL),
        'ffn2_w2': w((L, D_FF, D_MODEL), D_FF),
        'final_norm': gain((D_MODEL,)),
    }


def reference(x, positions, ffn1_norm, ffn1_w1, ffn1_w2, mix_norm, w_in, b_in,
              lru_conv_w, lru_conv_b, lru_w_gate, lru_b_gate, lru_lambda, lru_w_out,
              q_norm, w_uq, kv_norm, w_ukv, mla_w_o,
              conv_dw_w, conv_dw_b, conv_ln_g, conv_ln_b, conv_w_out, conv_b_out,
              w_out, ffn2_norm, ffn2_w1, ffn2_w2, final_norm):
    for l in range(DEPTH):
        x = x + 0.5 * swiglu(rms_norm(x, ffn1_norm[l]), ffn1_w1[l], ffn1_w2[l])
        x = x + hybrid_mixer(
            rms_norm(x, mix_norm[l]), positions, w_in[l], b_in[l],
            lru_conv_w[l], lru_conv_b[l], lru_w_gate[l], lru_b_gate[l], lru_lambda[l], lru_w_out[l],
            q_norm[l], w_uq[l], kv_norm[l], w_ukv[l], mla_w_o[l],
            conv_dw_w[l], conv_dw_b[l], conv_ln_g[l], conv_ln_b[l], conv_w_out[l], conv_b_out[l],
            w_out[l])
        x = x + 0.5 * swiglu(rms_norm(x, ffn2_norm[l]), ffn2_w1[l], ffn2_w2[l])
    return rms_norm(x, final_norm)
```

```python
import numpy as np
import concourse.bass as bass
import concourse.mybir as mybir
from concourse.bass_utils import run_bass_kernel_spmd

F32 = mybir.dt.float32
BF16 = mybir.dt.bfloat16
I32 = mybir.dt.int32
AF = mybir.ActivationFunctionType
ALU = mybir.AluOpType
AX = mybir.AxisListType

D = 1024
S = 2048
DFF = 2816
DEPTH = 2
EPS = 1e-6
NKC = D // 128
NFC = DFF // 128
TC = 512
NTC = S // TC


class _Op:
    __slots__ = ("eng", "fn", "idx", "waits", "signal", "sem", "val", "dma", "semkey")

    def __init__(self, eng, fn, dma, semkey):
        self.eng = eng
        self.fn = fn
        self.dma = dma
        self.semkey = semkey
        self.waits = []
        self.signal = dma
        self.sem = None
        self.val = 0
        self.idx = 0


class _Rec:
    def __getattr__(self, name):
        return lambda *a, **k: (name, a, k)


_REC = _Rec()


class Sched:
    ENGS = ("pe", "act", "dve", "pool", "sp")

    def __init__(self):
        self.ops = {e: [] for e in self.ENGS}
        self.last_w = {}
        self.readers = {}
        self.waited = {e: {} for e in self.ENGS}
        self.dma_seq = {}
        self.all_dma = []

    def add(self, eng, fn, reads=(), writes=(), dma=False, semkey=None, extra=()):
        if fn is not None:
            name_, a_, k_ = fn(_REC)
            fn = (lambda e, name_=name_, a_=a_, k_=k_: getattr(e, name_)(*a_, **k_))
        op = _Op(eng, fn, dma, semkey)
        op.idx = len(self.ops[eng])
        if dma:
            assert semkey is not None
        deps = list(extra)
        for r in reads:
            w = self.last_w.get(r)
            if w is not None:
                deps.append(w)
        for wtok in writes:
            w = self.last_w.get(wtok)
            if w is not None:
                deps.append(w)
            deps.extend(self.readers.get(wtok, ()))
        for p in deps:
            if p is op:
                continue
            if p.dma:
                key = ("dma", p.semkey)
                pos = p.val
            else:
                if p.eng == "pe" and eng == "pe":
                    continue
                key = p.eng
                pos = p.idx + 1
            if self.waited[eng].get(key, 0) >= pos:
                continue
            self.waited[eng][key] = pos
            p.signal = True
            op.waits.append(p)
        if dma:
            n = self.dma_seq.get(semkey, 0) + 1
            self.dma_seq[semkey] = n
            op.val = n
            self.all_dma.append(op)
        for r in reads:
            self.readers.setdefault(r, []).append(op)
        for wtok in writes:
            self.last_w[wtok] = op
            self.readers[wtok] = []
        self.ops[eng].append(op)
        return op

    def barrier(self):
        comp = ("pe", "act", "dve", "pool")
        last = {}
        for e in comp:
            for op in reversed(self.ops[e]):
                if not op.dma and op.fn is not None:
                    last[e] = op
                    break
        for e in comp:
            self.add(e, None, extra=[last[f] for f in comp if f != e and f in last])

    def emit(self, nc, final_wait_eng="sp"):
        from contextlib import ExitStack
        with ExitStack() as es:
            esem = {e: es.enter_context(nc.semaphore("s_" + e)) for e in self.ENGS}
            dsem = {}
            for k in self.dma_seq:
                dsem[k] = es.enter_context(nc.semaphore("d_%s" % (len(dsem),)))
            for e in self.ENGS:
                c = 0
                for op in self.ops[e]:
                    if op.dma:
                        op.sem = dsem[op.semkey]
                        op.val = op.val * 16
                    elif op.signal:
                        assert op.fn is not None
                        c += 1
                        op.sem = esem[e]
                        op.val = c
            block = es.enter_context(nc.Block())

            def run(e, engobj, extra_final=False):
                for op in self.ops[e]:
                    for p in op.waits:
                        engobj.wait_ge(p.sem, p.val)
                    if op.fn is None:
                        continue
                    ins = op.fn(engobj)
                    if op.dma:
                        ins.then_inc(op.sem, 16)
                    elif op.signal:
                        ins.then_inc(op.sem, 1)
                if extra_final:
                    for k, n in self.dma_seq.items():
                        engobj.wait_ge(dsem[k], 16 * n)

            @block.tensor
            def _(eng):
                run("pe", eng)

            @block.scalar
            def _(eng):
                run("act", eng)

            @block.vector
            def _(eng):
                run("dve", eng)

            @block.gpsimd
            def _(eng):
                run("pool", eng)

            @block.sync
            def _(eng):
                run("sp", eng, extra_final=True)


_VC = {}
def _vc_build():
    c = 0
    def put(name, n):
        nonlocal c
        _VC[name] = c
        c += n
    put("ffn1_norm", 8); put("mix_norm", 8); put("ffn2_norm", 8)
    put("b_xa", 4); put("b_ga", 4); put("b_cq", 3); put("b_ckv", 2); put("b_kpe", 2)
    put("b_cv", 4); put("b_cg", 4); put("b_G", 24)
    put("lru_cw", 16); put("lru_cb", 4); put("b_r", 4); put("b_i", 4); put("lam", 4)
    put("q_norm", 3); put("kv_norm", 2); put("dw_w", 124); put("dw_b", 4)
    put("ln_g", 4); put("ln_b", 4); put("cb_out", 8)
    return c
NVEC = _vc_build()
NGL = 16

SCALE = 96.0 ** -0.5
GELU_K = 1.5957691216057308
PI = 3.141592653589793


class Builder:
    def __init__(self, S=2048, L=DEPTH):
        self.nc = bass.Bass("TRN2", target_bir_lowering=False)
        self.s = Sched()
        self.S = S
        self.L = L
        self.NTC = S // TC
        self.dram_in = {}
        self.psum_rr = 0
        self.wp_rr = 0

    def din(self, name, shape, dt=F32):
        t = self.nc.dram_tensor(name, list(shape), dt, kind="ExternalInput")
        self.dram_in[name] = t
        return t

    def sb(self, name, shape, dt):
        return self.nc.alloc_sbuf_tensor(name, list(shape), dt)

    def setup(self):
        nc, s, S = self.nc, self.s, self.S
        self.x = self.sb("x", [128, NKC, S], F32)
        self.psum = [nc.alloc_psum_tensor("ps%d" % i, [128, 512], F32) for i in range(8)]
        self.wp = [self.sb("wp%d" % i, [128, 3072], BF16) for i in range(4)]
        self.ftmp = [self.sb("ft%d" % i, [128, 544], F32) for i in range(7)]
        self.btmp = [self.sb("bt%d" % i, [128, 512], BF16) for i in range(5)]
        self.ffree = list(range(7))
        self.bfree = list(range(5))
        NB = S // 128
        mix_elems = 8 * S + NB * 8 * 65 + 4096 + 6 * 2048 + 4096 + 1536 + 1024 + 64
        ffn_elems = 8 * 1024 + 22 * 1024
        self.arena = self.sb("arena", [128, max(mix_elems, ffn_elems)], BF16)
        a = self.arena
        self.h = a[:, 0:8192].rearrange("p (k t) -> p k t", k=8)
        self.mid = a[:, 8192:8192 + 22528].rearrange("p (f t) -> p f t", f=22)
        o = 0
        def carve(n):
            nonlocal o
            v = a[:, o:o + n]
            o += n
            return v
        self.K = carve(8 * S).rearrange("p (h t) -> p h t", h=8)
        self.V = carve(NB * 8 * 65).rearrange("p (b h d) -> p b h d", b=NB, h=8)
        self.hm = carve(4096).rearrange("p (k t) -> p k t", k=8)
        self.m_a = carve(2048).rearrange("p (k t) -> p k t", k=4)
        self.cs = carve(2048).rearrange("p (k t) -> p k t", k=4)
        self.O_sb = carve(2048).rearrange("p (q f) -> p q f", q=4)
        self.OT = carve(2048).rearrange("p (k t) -> p k t", k=4)
        self.cc = carve(2048).rearrange("p (k t) -> p k t", k=4)
        self.merged = carve(4096).rearrange("p (k t) -> p k t", k=8)
        self.cqn = carve(1536).rearrange("p (k t) -> p k t", k=3)
        self.ckvn = carve(1024).rearrange("p (k t) -> p k t", k=2)
        self.ones_bf = self.sb("ones_bf", [128, 128], BF16)
        self.ident = self.sb("ident", [128, 128], BF16)
        self.tri = self.sb("tri", [128, 128], BF16)
        self.epsb = self.sb("epsb", [128, 1], F32)
        self.vecs = [self.sb("svecs%d" % l, [128, NVEC], F32) for l in range(self.L)]
        self.glob = self.sb("glob", [128, NGL], F32)
        self.gw = self.sb("gw", [128, 8, 128], BF16)
        self.halo_a = self.sb("halo_a", [128, 4, 3], F32)
        self.halo_c = self.sb("halo_c", [128, 4, 30], F32)
        self.state = self.sb("state", [128, 4], F32)
        self.nsp = self.sb("nsp", [128, 8], F32)
        self.rec = self.sb("rec", [128, 4], F32)
        s.add("pool", lambda e: e.memset(self.ones_bf[:], 1.0), writes=[("ones_bf",)])
        s.add("pool", lambda e: e.memset(self.epsb[:], EPS), writes=[("epsb",)])

    def load_consts(self):
        s = self.s
        dr_ident = self.din("c_ident", [128, 128])
        dr_tri = self.din("c_tri", [128, 128])
        dr_glob = self.din("c_glob", [128, NGL])
        self.dr_pos = self.din("pos_rep", [96, self.S], I32)
        s.add("pool", lambda e: e.dma_start(out=self.ident[:, :], in_=dr_ident[:, :]),
              writes=[("ident",)], dma=True, semkey=("c", 0))
        s.add("pool", lambda e: e.dma_start(out=self.tri[:, :], in_=dr_tri[:, :]),
              writes=[("tri",)], dma=True, semkey=("c", 1))
        s.add("sp", lambda e: e.dma_start(out=self.glob[:, :], in_=dr_glob[:, :]),
              writes=[("glob",)], dma=True, semkey=("c", 2))
        self.dr_vecs = []
        for l in range(self.L):
            dv = self.din("vecs%d" % l, [128, NVEC])
            self.dr_vecs.append(dv)
            s.add("sp", lambda e, l=l, dv=dv: e.dma_start(out=self.vecs[l][:, :], in_=dv[:, :]),
                  writes=[("vecs", l)], dma=True, semkey=("vecs", l))

    def vcol(self, l, name, j=0, n=1):
        c = _VC[name] + j
        return self.vecs[l][:, c:c + n]

    def ft(self):
        i = self.ffree.pop(0)
        return i, self.ftmp[i], ("ft", i)

    def ffr(self, i):
        self.ffree.append(i)

    def bt(self):
        i = self.bfree.pop(0)
        return i, self.btmp[i], ("bt", i)

    def bfr(self, i):
        self.bfree.append(i)

    def next_psum(self):
        i = self.psum_rr
        self.psum_rr = (i + 1) % 6
        return i

    def wload(self, dmas):
        s = self.s
        b = self.wp_rr
        self.wp_rr = (b + 1) % 4
        buf = self.wp[b]
        for n, (dstf, src) in enumerate(dmas):
            s.add("pool", lambda e, dstf=dstf, src=src, buf=buf: e.dma_start(out=dstf(buf), in_=src),
                  writes=[("wp", b, n)], dma=True, semkey=("wp", b, n))
        return buf, [("wp", b, n) for n in range(len(dmas))]

    def load_x(self, xT):
        s = self.s
        for c in range(NKC):
            s.add("sp", lambda e, c=c: e.dma_start(out=self.x[:, c, :], in_=xT[c * 128:(c + 1) * 128, :]),
                  writes=[("x", c, t) for t in range(self.NTC)], dma=True, semkey=("xload", c))

    def store_x(self, outT):
        s = self.s
        for c in range(NKC):
            s.add("sp", lambda e, c=c: e.dma_start(out=outT[c * 128:(c + 1) * 128, :], in_=self.x[:, c, :]),
                  reads=[("x", c, t) for t in range(self.NTC)], dma=True, semkey=("xstore",))

    def rstd_from(self, srcs, src_toks, n_feat):
        s = self.s
        pb = self.next_psum()
        ps = self.psum[pb]
        nk = len(srcs)
        for k, (ap, tok) in enumerate(zip(srcs, src_toks)):
            bi, bq, btok = self.bt()
            s.add("act", lambda e, ap=ap, bq=bq: e.activation(out=bq[:, :], in_=ap, func=AF.Square),
                  reads=[tok], writes=[btok])
            s.add("pe", lambda e, k=k, bq=bq: e.matmul(ps[:, :], lhsT=self.ones_bf[:, :], rhs=bq[:, :],
                                                        start=(k == 0), stop=(k == nk - 1)),
                  reads=[btok, ("ones_bf",)], writes=[("ps", pb)])
            self.bfr(bi)
        fi, r, rtok = self.ft()
        s.add("act", lambda e: e.activation(out=r[:, 0:TC], in_=ps[:, :], func=AF.Sqrt,
                                            bias=self.epsb[:, 0:1], scale=1.0 / n_feat),
              reads=[("ps", pb), ("epsb",)], writes=[rtok])
        s.add("dve", lambda e: e.reciprocal(out=r[:, 0:TC], in_=r[:, 0:TC]), reads=[rtok], writes=[rtok])
        return fi, r, rtok

    def rmsnorm_chunk(self, tg, gain, gtok, h_out, hcol0, htoks):
        s = self.s
        t0 = tg * TC
        fi, r, rtok = self.rstd_from([self.x[:, k, t0:t0 + TC] for k in range(NKC)],
                                     [("x", k, tg) for k in range(NKC)], D)
        for k in range(NKC):
            s.add("dve", lambda e, k=k: e.scalar_tensor_tensor(
                out=h_out[:, k, hcol0:hcol0 + TC], in0=self.x[:, k, t0:t0 + TC], scalar=gain[:, k:k + 1],
                in1=r[:, 0:TC], op0=ALU.mult, op1=ALU.mult),
                reads=[("x", k, tg), rtok, gtok], writes=[htoks[k]])
        self.ffr(fi)

    def ffn(self, l, which, w1r, w2):
        s = self.s
        TH = min(1024, self.S)
        nloc = TH // TC
        gain = self.vcol(l, which, 0, 8)
        gtok = ("vecs", l)
        w1v = w1r.rearrange("(kc p) f n -> p kc f n", p=128)
        w2v = w2.rearrange("(fc p) n -> p fc n", p=128)
        for half in range(self.S // TH):
            for tcl in range(nloc):
                tg = half * nloc + tcl
                self.rmsnorm_chunk(tg, gain, gtok, self.h, tcl * TC, [("h", tcl)] * 8)
            for f in range(NFC):
                buf, wt = self.wload([(lambda b: b[:, 0:2048].rearrange("p (k n) -> p k n", k=8), w1v[:, :, f, :])])
                wv = buf[:, 0:2048].rearrange("p (k n) -> p k n", k=8)
                for tcl in range(nloc):
                    pg = self.next_psum()
                    pu = self.next_psum()
                    for k in range(NKC):
                        s.add("pe", lambda e, k=k, wv=wv, pg=pg, tcl=tcl: e.matmul(
                            self.psum[pg][:, :], lhsT=wv[:, k, 0:128], rhs=self.h[:, k, tcl * TC:(tcl + 1) * TC],
                            start=(k == 0), stop=(k == NKC - 1)),
                            reads=wt + [("h", tcl)], writes=[("ps", pg)])
                    for k in range(NKC):
                        s.add("pe", lambda e, k=k, wv=wv, pu=pu, tcl=tcl: e.matmul(
                            self.psum[pu][:, :], lhsT=wv[:, k, 128:256], rhs=self.h[:, k, tcl * TC:(tcl + 1) * TC],
                            start=(k == 0), stop=(k == NKC - 1)),
                            reads=wt + [("h", tcl)], writes=[("ps", pu)])
                    fi, sg, sgt = self.ft()
                    s.add("act", lambda e, sg=sg, pg=pg: e.activation(out=sg[:, 0:TC], in_=self.psum[pg][:, :], func=AF.Silu),
                          reads=[("ps", pg)], writes=[sgt])
                    s.add("dve", lambda e, sg=sg, pu=pu, f=f, tcl=tcl: e.tensor_tensor(
                        out=self.mid[:, f, tcl * TC:(tcl + 1) * TC], in0=sg[:, 0:TC], in1=self.psum[pu][:, :], op=ALU.mult),
                        reads=[sgt, ("ps", pu)], writes=[("mid", f, tcl)])
                    self.ffr(fi)
            for o in range(NKC):
                buf, wt = self.wload([(lambda b: b[:, 0:2816].rearrange("p (f n) -> p f n", f=22), w2v[:, :, o * 128:(o + 1) * 128])])
                wv = buf[:, 0:2816].rearrange("p (f n) -> p f n", f=22)
                for tcl in range(nloc):
                    tg = half * nloc + tcl
                    py = self.next_psum()
                    for f in range(NFC):
                        s.add("pe", lambda e, f=f, wv=wv, py=py, tcl=tcl: e.matmul(
                            self.psum[py][:, :], lhsT=wv[:, f, :], rhs=self.mid[:, f, tcl * TC:(tcl + 1) * TC],
                            start=(f == 0), stop=(f == NFC - 1)),
                            reads=wt + [("mid", f, tcl)], writes=[("ps", py)])
                    s.add("dve", lambda e, o=o, py=py, tg=tg: e.scalar_tensor_tensor(
                        out=self.x[:, o, tg * TC:(tg + 1) * TC], in0=self.psum[py][:, :], scalar=0.5,
                        in1=self.x[:, o, tg * TC:(tg + 1) * TC], op0=ALU.mult, op1=ALU.add),
                        reads=[("ps", py), ("x", o, tg)], writes=[("x", o, tg)])

    def final_norm(self):
        for tg in range(self.NTC):
            self.rmsnorm_chunk(tg, self.glob[:, 0:8], ("glob",), self.x, tg * TC, [("x", k, tg) for k in range(NKC)])

    def proj(self, lhs_fn, nk, rhs_fn, rtoks, wtoks, M=128, N=TC, pb=None):
        s = self.s
        if pb is None:
            pb = self.next_psum()
        for k in range(nk):
            lt = lhs_fn(k)
            rt = rhs_fn(k)
            s.add("pe", lambda e: e.matmul(self.psum[pb][0:M, 0:N], lhsT=lt, rhs=rt,
                                           start=(k == 0), stop=(k == nk - 1)),
                  reads=list(wtoks) + list(rtoks), writes=[("ps", pb)])
        return pb

    def mixer_layer_setup(self, l, gwd):
        s = self.s
        s.add("pool", lambda e: e.dma_start(out=self.gw[:, :, :], in_=gwd.rearrange("p (a b) -> p a b", a=8)),
              writes=[("gw",)], dma=True, semkey=("gw",))
        lam = self.vcol(l, "lam", 0, 4)
        s.add("act", lambda e: e.activation(out=self.nsp[:, 0:4], in_=lam, func=AF.Exp, scale=-1.0),
              reads=[("vecs", l)], writes=[("nsp",)])
        s.add("act", lambda e: e.activation(out=self.nsp[:, 0:4], in_=self.nsp[:, 0:4], func=AF.Ln, bias=1.0),
              reads=[("nsp",)], writes=[("nsp",)])
        s.add("dve", lambda e: e.tensor_scalar(out=self.nsp[:, 4:8], in0=self.nsp[:, 0:4], scalar1=-16.0, scalar2=None, op0=ALU.mult),
              reads=[("nsp",)], writes=[("nsp2",)])
        s.add("dve", lambda e: e.tensor_scalar(out=self.nsp[:, 0:4], in0=self.nsp[:, 0:4], scalar1=-8.0, scalar2=None, op0=ALU.mult),
              reads=[("nsp",), ("nsp2",)], writes=[("nsp",)])
        s.add("pool", lambda e: e.memset(self.halo_a[:], 0.0), writes=[("halo_a", j) for j in range(4)])
        s.add("pool", lambda e: e.memset(self.halo_c[:], 0.0), writes=[("halo_c", j) for j in range(4)])
        s.add("pool", lambda e: e.memset(self.state[:], 0.0), writes=[("state", j) for j in range(4)])
        s.add("pool", lambda e: e.memset(self.V[:, :, :, 64:65], 1.0), writes=[("Vones",)])

    def mixer_chunk(self, l, tg, W):
        s = self.s
        t0 = tg * TC
        vt = ("vecs", l)
        hm = self.hm
        hmt = [("hm", k) for k in range(8)]
        self.rmsnorm_chunk(tg, self.vcol(l, "mix_norm", 0, 8), vt, hm, 0, hmt)
        hm_rhs = lambda k: hm[:, k, :]
        v8 = lambda b, n: b[:, 0:8 * n].rearrange("p (k n) -> p k n", k=8)
        winA = W["winA"].rearrange("(kc p) j n -> p kc j n", p=128)
        winC = W["winC"].rearrange("(kc p) j n -> p kc j n", p=128)
        winB = W["winB"].rearrange("(kc p) n -> p kc n", p=128)
        wkpe = W["wkpe"].rearrange("(kc p) n -> p kc n", p=128)
        winG = W["winG"].rearrange("(kc p) o n -> p kc o n", p=128)
        wbr = W["wbr"].rearrange("(kc p) o n -> p kc o n", p=128)
        wq = W["wq"].rearrange("(kc p) n -> p kc n", p=128)
        wukv = W["wukv"].rearrange("(kc p) n -> p kc n", p=128)
        wout = W["wout"].rearrange("(kc p) n -> p kc n", p=128)

        for j in range(4):
            buf, wt = self.wload([(lambda b: v8(b, 256), winA[:, :, j, :])])
            wv = v8(buf, 256)
            pxa = self.proj(lambda k: wv[:, k, 0:128], 8, hm_rhs, hmt, wt)
            pga = self.proj(lambda k: wv[:, k, 128:256], 8, hm_rhs, hmt, wt)
            i_in, tin, tint = self.ft()
            s.add("act", lambda e, tin=tin, pxa=pxa, j=j: e.activation(
                out=tin[:, 30:542], in_=self.psum[pxa][:, :], func=AF.Identity, bias=self.vcol(l, "b_xa", j)),
                reads=[("ps", pxa), vt], writes=[tint])
            s.add("pool", lambda e, tin=tin, j=j: e.tensor_copy(out=tin[:, 27:30], in_=self.halo_a[:, j, :]),
                  reads=[("halo_a", j)], writes=[tint])
            i_xa, xa, xat = self.ft()
            cw = lambda tap, j=j: self.vcol(l, "lru_cw", j * 4 + tap)
            s.add("dve", lambda e, tin=tin, xa=xa, j=j: e.tensor_scalar(
                out=xa[:, 0:TC], in0=tin[:, 27:27 + TC], scalar1=cw(0), scalar2=self.vcol(l, "lru_cb", j),
                op0=ALU.mult, op1=ALU.add), reads=[tint, vt], writes=[xat])
            for tap in range(1, 4):
                s.add("dve", lambda e, tin=tin, xa=xa, tap=tap: e.scalar_tensor_tensor(
                    out=xa[:, 0:TC], in0=tin[:, 27 + tap:27 + tap + TC], scalar=cw(tap), in1=xa[:, 0:TC],
                    op0=ALU.mult, op1=ALU.add), reads=[tint, xat, vt], writes=[xat])
            s.add("pool", lambda e, tin=tin, j=j: e.tensor_copy(out=self.halo_a[:, j, :], in_=tin[:, 539:542]),
                  reads=[tint], writes=[("halo_a", j)])
            self.ffr(i_in)
            ib, xab, xabt = self.bt()
            s.add("act", lambda e, xa=xa, xab=xab: e.activation(out=xab[:, :], in_=xa[:, 0:TC], func=AF.Identity),
                  reads=[xat], writes=[xabt])
            pr = self.proj(lambda k: self.gw[:, 2 * j, :], 1, lambda k: xab[:, :], [xabt], [("gw",)])
            pi_ = self.proj(lambda k: self.gw[:, 2 * j + 1, :], 1, lambda k: xab[:, :], [xabt], [("gw",)])
            self.bfr(ib)
            i_r, r, rt = self.ft()
            i_i, ii, it = self.ft()
            s.add("act", lambda e, r=r, pr=pr, j=j: e.activation(out=r[:, 0:TC], in_=self.psum[pr][:, :], func=AF.Sigmoid,
                                                              bias=self.vcol(l, "b_r", j)), reads=[("ps", pr), vt], writes=[rt])
            s.add("act", lambda e, ii=ii, pi_=pi_, j=j: e.activation(out=ii[:, 0:TC], in_=self.psum[pi_][:, :], func=AF.Sigmoid,
                                                                  bias=self.vcol(l, "b_i", j)), reads=[("ps", pi_), vt], writes=[it])
            i_a, a, at = self.ft()
            i_a2, a2, a2t = self.ft()
            s.add("act", lambda e, a=a, r=r, j=j: e.activation(out=a[:, 0:TC], in_=r[:, 0:TC], func=AF.Exp, scale=self.nsp[:, j:j + 1]),
                  reads=[rt, ("nsp",)], writes=[at])
            s.add("act", lambda e, a2=a2, r=r, j=j: e.activation(out=a2[:, 0:TC], in_=r[:, 0:TC], func=AF.Exp, scale=self.nsp[:, 4 + j:5 + j]),
                  reads=[rt, ("nsp2",)], writes=[a2t])
            self.ffr(i_r)
            s.add("dve", lambda e, a2=a2: e.tensor_scalar(out=a2[:, 0:TC], in0=a2[:, 0:TC], scalar1=-1.0, scalar2=1.0,
                                                          op0=ALU.mult, op1=ALU.add), reads=[a2t], writes=[a2t])
            s.add("act", lambda e, a2=a2: e.activation(out=a2[:, 0:TC], in_=a2[:, 0:TC], func=AF.Sqrt), reads=[a2t], writes=[a2t])
            s.add("dve", lambda e, ii=ii, xa=xa: e.tensor_tensor(out=ii[:, 0:TC], in0=ii[:, 0:TC], in1=xa[:, 0:TC], op=ALU.mult),
                  reads=[it, xat], writes=[it])
            s.add("dve", lambda e, ii=ii, a2=a2: e.tensor_tensor(out=ii[:, 0:TC], in0=ii[:, 0:TC], in1=a2[:, 0:TC], op=ALU.mult),
                  reads=[it, a2t], writes=[it])
            self.ffr(i_xa)
            self.ffr(i_a2)
            i_h, hl, hlt = self.ft()
            s.add("dve", lambda e, hl=hl, a=a, ii=ii, j=j: e.tensor_tensor_scan(
                out=hl[:, 0:TC], data0=a[:, 0:TC], data1=ii[:, 0:TC], initial=self.state[:, j:j + 1],
                op0=ALU.mult, op1=ALU.add), reads=[at, it, ("state", j)], writes=[hlt])
            s.add("act", lambda e, hl=hl, j=j: e.activation(out=self.state[:, j:j + 1], in_=hl[:, TC - 1:TC], func=AF.Identity),
                  reads=[hlt], writes=[("state", j)])
            self.ffr(i_a)
            self.ffr(i_i)
            i_g, xg, xgt = self.ft()
            i_q, qg, qgt = self.ft()
            s.add("act", lambda e, xg=xg, pga=pga, j=j: e.activation(out=xg[:, 0:TC], in_=self.psum[pga][:, :], func=AF.Identity,
                                                                  bias=self.vcol(l, "b_ga", j)), reads=[("ps", pga), vt], writes=[xgt])
            s.add("act", lambda e, xg=xg, qg=qg: e.activation(out=qg[:, 0:TC], in_=xg[:, 0:TC], func=AF.Square), reads=[xgt], writes=[qgt])
            s.add("dve", lambda e, qg=qg: e.tensor_scalar(out=qg[:, 0:TC], in0=qg[:, 0:TC], scalar1=0.044715, scalar2=1.0,
                                                          op0=ALU.mult, op1=ALU.add), reads=[qgt], writes=[qgt])
            s.add("dve", lambda e, qg=qg, xg=xg: e.tensor_tensor(out=qg[:, 0:TC], in0=qg[:, 0:TC], in1=xg[:, 0:TC], op=ALU.mult),
                  reads=[qgt, xgt], writes=[qgt])
            s.add("act", lambda e, qg=qg: e.activation(out=qg[:, 0:TC], in_=qg[:, 0:TC], func=AF.Sigmoid, scale=GELU_K), reads=[qgt], writes=[qgt])
            s.add("dve", lambda e, qg=qg, xg=xg: e.tensor_tensor(out=qg[:, 0:TC], in0=qg[:, 0:TC], in1=xg[:, 0:TC], op=ALU.mult),
                  reads=[qgt, xgt], writes=[qgt])
            s.add("dve", lambda e, qg=qg, hl=hl, j=j: e.tensor_tensor(out=self.m_a[:, j, :], in0=qg[:, 0:TC], in1=hl[:, 0:TC], op=ALU.mult),
                  reads=[qgt, hlt], writes=[("m_a", j)])
            self.ffr(i_g)
            self.ffr(i_q)
            self.ffr(i_h)

        for j in range(4):
            buf, wt = self.wload([(lambda b: v8(b, 256), winC[:, :, j, :])])
            wv = v8(buf, 256)
            pv = self.proj(lambda k: wv[:, k, 0:128], 8, hm_rhs, hmt, wt)
            pg = self.proj(lambda k: wv[:, k, 128:256], 8, hm_rhs, hmt, wt)
            i_s, sg, sgt = self.ft()
            s.add("act", lambda e, sg=sg, pg=pg, j=j: e.activation(out=sg[:, 0:TC], in_=self.psum[pg][:, :], func=AF.Sigmoid,
                                                                bias=self.vcol(l, "b_cg", j)), reads=[("ps", pg), vt], writes=[sgt])
            i_c, cb, cbt = self.ft()
            s.add("dve", lambda e, cb=cb, pv=pv, sg=sg, j=j: e.scalar_tensor_tensor(
                out=cb[:, 30:542], in0=self.psum[pv][:, :], scalar=self.vcol(l, "b_cv", j), in1=sg[:, 0:TC],
                op0=ALU.add, op1=ALU.mult), reads=[("ps", pv), sgt, vt], writes=[cbt])
            s.add("pool", lambda e, cb=cb, j=j: e.tensor_copy(out=cb[:, 0:30], in_=self.halo_c[:, j, :]),
                  reads=[("halo_c", j)], writes=[cbt])
            self.ffr(i_s)
            i_o, acc, acct = self.ft()
            dw = lambda tap, j=j: self.vcol(l, "dw_w", j * 31 + tap)
            s.add("dve", lambda e, cb=cb, acc=acc, j=j: e.tensor_scalar(
                out=acc[:, 0:TC], in0=cb[:, 0:TC], scalar1=dw(0), scalar2=self.vcol(l, "dw_b", j),
                op0=ALU.mult, op1=ALU.add), reads=[cbt, vt], writes=[acct])
            for tap in range(1, 31):
                last = (tap == 30)
                s.add("dve", lambda e, cb=cb, acc=acc, tap=tap, last=last, j=j: e.scalar_tensor_tensor(
                    out=(self.cc[:, j, :] if last else acc[:, 0:TC]), in0=cb[:, tap:tap + TC], scalar=dw(tap), in1=acc[:, 0:TC],
                    op0=ALU.mult, op1=ALU.add), reads=[cbt, acct, vt],
                    writes=[("cc", j)] if last else [acct])
            s.add("pool", lambda e, cb=cb, j=j: e.tensor_copy(out=self.halo_c[:, j, :], in_=cb[:, 512:542]),
                  reads=[cbt], writes=[("halo_c", j)])
            self.ffr(i_c)
            self.ffr(i_o)
        p1 = self.next_psum()
        p2 = self.next_psum()
        for j in range(4):
            s.add("pe", lambda e, j=j: e.matmul(self.psum[p1][:, :], lhsT=self.ones_bf[:, :], rhs=self.cc[:, j, :],
                                                 start=(j == 0), stop=(j == 3)), reads=[("cc", j), ("ones_bf",)], writes=[("ps", p1)])
        for j in range(4):
            ib, sq, sqt = self.bt()
            s.add("act", lambda e, sq=sq, j=j: e.activation(out=sq[:, :], in_=self.cc[:, j, :], func=AF.Square), reads=[("cc", j)], writes=[sqt])
            s.add("pe", lambda e, j=j, sq=sq: e.matmul(self.psum[p2][:, :], lhsT=self.ones_bf[:, :], rhs=sq[:, :],
                                                        start=(j == 0), stop=(j == 3)), reads=[sqt, ("ones_bf",)], writes=[("ps", p2)])
            self.bfr(ib)
        i_m, mean, meant = self.ft()
        i_v, var, vart = self.ft()
        s.add("act", lambda e: e.activation(out=mean[:, 0:TC], in_=self.psum[p1][:, :], func=AF.Identity, scale=1.0 / 512),
              reads=[("ps", p1)], writes=[meant])
        s.add("act", lambda e: e.activation(out=var[:, 0:TC], in_=mean[:, 0:TC], func=AF.Square), reads=[meant], writes=[vart])
        s.add("dve", lambda e: e.scalar_tensor_tensor(out=var[:, 0:TC], in0=self.psum[p2][:, :], scalar=1.0 / 512, in1=var[:, 0:TC],
                                                      op0=ALU.mult, op1=ALU.subtract), reads=[("ps", p2), vart], writes=[vart])
        s.add("act", lambda e: e.activation(out=var[:, 0:TC], in_=var[:, 0:TC], func=AF.Sqrt, bias=self.epsb[:, 0:1]),
              reads=[vart, ("epsb",)], writes=[vart])
        s.add("dve", lambda e: e.reciprocal(out=var[:, 0:TC], in_=var[:, 0:TC]), reads=[vart], writes=[vart])
        for j in range(4):
            i_t, tt, ttt = self.ft()
            s.add("dve", lambda e, tt=tt, j=j: e.tensor_tensor(out=tt[:, 0:TC], in0=self.cc[:, j, :], in1=mean[:, 0:TC], op=ALU.subtract),
                  reads=[("cc", j), meant], writes=[ttt])
            s.add("dve", lambda e, tt=tt: e.tensor_tensor(out=tt[:, 0:TC], in0=tt[:, 0:TC], in1=var[:, 0:TC], op=ALU.mult),
                  reads=[ttt, vart], writes=[ttt])
            s.add("act", lambda e, tt=tt, j=j: e.activation(out=self.cs[:, j, :], in_=tt[:, 0:TC], func=AF.Silu,
                                                          scale=self.vcol(l, "ln_g", j), bias=self.vcol(l, "ln_b", j)),
                  reads=[ttt, vt], writes=[("cs", j)])
            self.ffr(i_t)
        self.ffr(i_m)
        self.ffr(i_v)

        R = slice(64, 96)
        glob = self.glob
        i_ang, ang, angt = self.ft()
        i_cos, cos, cost = self.ft()
        i_sin, sin, sint = self.ft()
        i_msk, msk, mskt = self.ft()
        s.add("sp", lambda e: e.dma_start(out=msk[R, 0:TC].bitcast(I32), in_=self.dr_pos[R, t0:t0 + TC]),
              writes=[mskt], dma=True, semkey=("pos", i_msk))
        s.add("dve", lambda e: e.tensor_copy(out=ang[R, 0:TC], in_=msk[R, 0:TC].bitcast(I32)), reads=[mskt], writes=[angt])
        s.add("dve", lambda e: e.tensor_scalar(out=ang[R, 0:TC], in0=ang[R, 0:TC], scalar1=glob[R, 8:9], scalar2=None, op0=ALU.mult),
              reads=[angt, ("glob",)], writes=[angt])
        for dst, dtok, shift, sc in ((sin, sint, 0.0, glob[R, 9:10]), (cos, cost, PI / 2, 1.0)):
            ki = msk[R, 0:TC].bitcast(I32)
            s.add("dve", lambda e: e.tensor_scalar(out=dst[R, 0:TC], in0=ang[R, 0:TC], scalar1=shift, scalar2=1.0 / (2 * PI),
                                                   op0=ALU.add, op1=ALU.mult), reads=[angt], writes=[dtok])
            s.add("dve", lambda e: e.tensor_copy(out=ki, in_=dst[R, 0:TC]), reads=[dtok], writes=[mskt])
            s.add("dve", lambda e: e.tensor_copy(out=dst[R, 0:TC], in_=ki), reads=[mskt], writes=[dtok])
            s.add("dve", lambda e: e.scalar_tensor_tensor(out=dst[R, 0:TC], in0=dst[R, 0:TC], scalar=-2 * PI, in1=ang[R, 0:TC],
                                                          op0=ALU.mult, op1=ALU.add), reads=[dtok, angt], writes=[dtok])
            s.add("dve", lambda e: e.tensor_scalar(out=dst[R, 0:TC], in0=dst[R, 0:TC], scalar1=shift, scalar2=None, op0=ALU.add),
                  reads=[dtok], writes=[dtok])
            s.add("dve", lambda e: e.tensor_scalar(out=msk[R, 0:TC], in0=dst[R, 0:TC], scalar1=PI, scalar2=-2 * PI,
                                                   op0=ALU.is_gt, op1=ALU.mult), reads=[dtok], writes=[mskt])
            s.add("dve", lambda e: e.tensor_tensor(out=dst[R, 0:TC], in0=dst[R, 0:TC], in1=msk[R, 0:TC], op=ALU.add),
                  reads=[dtok, mskt], writes=[dtok])
            s.add("dve", lambda e: e.tensor_scalar(out=dst[R, 0:TC], in0=dst[R, 0:TC], scalar1=-PI, scalar2=PI,
                                                   op0=ALU.max, op1=ALU.min), reads=[dtok], writes=[dtok])
            s.add("act", lambda e: e.activation(out=dst[R, 0:TC], in_=dst[R, 0:TC], func=AF.Sin, scale=sc),
                  reads=[dtok, ("glob",)], writes=[dtok])
        self.ffr(i_ang)
        self.ffr(i_msk)

        def latent(src, col0, n, bname, gname, dst, dname):
            buf, wt = self.wload([(lambda b: v8(b, n * 128), src[:, :, col0:col0 + n * 128])])
            wv = v8(buf, n * 128)
            tmps = []
            for i in range(n):
                p = self.proj(lambda k: wv[:, k, i * 128:(i + 1) * 128], 8, hm_rhs, hmt, wt)
                fi, t, tt = self.ft()
                s.add("act", lambda e: e.activation(out=t[:, 0:TC], in_=self.psum[p][:, :], func=AF.Identity,
                                                    bias=self.vcol(l, bname, i)), reads=[("ps", p), vt], writes=[tt])
                tmps.append((fi, t, tt))
            ri, r, rtok = self.rstd_from([t[:, 0:TC] for _, t, _ in tmps], [tt for _, _, tt in tmps], n * 128)
            for i, (fi, t, tt) in enumerate(tmps):
                s.add("dve", lambda e: e.scalar_tensor_tensor(out=dst[:, i, :], in0=t[:, 0:TC], scalar=self.vcol(l, gname, i),
                                                              in1=r[:, 0:TC], op0=ALU.mult, op1=ALU.mult),
                      reads=[tt, rtok, vt], writes=[(dname, i)])
                self.ffr(fi)
            self.ffr(ri)
        latent(winB, 0, 3, "b_cq", "q_norm", self.cqn, "cqn")
        latent(winB, 384, 2, "b_ckv", "kv_norm", self.ckvn, "ckvn")
        cqt = [("cqn", i) for i in range(3)]
        ckvt = [("ckvn", i) for i in range(2)]

        buf, wt = self.wload([(lambda b: v8(b, 192), wkpe[:, :, :])])
        wv = v8(buf, 192)
        pn = self.proj(lambda k: wv[:, k, 0:96], 8, hm_rhs, hmt, wt, M=96)
        psw = self.proj(lambda k: wv[:, k, 96:192], 8, hm_rhs, hmt, wt, M=96)
        i_kn, kn, knt = self.ft()
        i_ks, ks, kst = self.ft()
        s.add("act", lambda e: e.activation(out=kn[R, 0:TC], in_=self.psum[pn][R, :], func=AF.Identity,
                                            bias=self.vecs[l][R, _VC["b_kpe"]:_VC["b_kpe"] + 1]), reads=[("ps", pn), vt], writes=[knt])
        s.add("act", lambda e: e.activation(out=ks[R, 0:TC], in_=self.psum[psw][R, :], func=AF.Identity,
                                            bias=self.vecs[l][R, _VC["b_kpe"] + 1:_VC["b_kpe"] + 2]), reads=[("ps", psw), vt], writes=[kst])
        s.add("dve", lambda e: e.tensor_tensor(out=kn[R, 0:TC], in0=kn[R, 0:TC], in1=cos[R, 0:TC], op=ALU.mult), reads=[knt, cost], writes=[knt])
        s.add("dve", lambda e: e.tensor_tensor(out=ks[R, 0:TC], in0=ks[R, 0:TC], in1=sin[R, 0:TC], op=ALU.mult), reads=[kst, sint], writes=[kst])
        i_kr, kr, krt = self.bt()
        s.add("dve", lambda e: e.tensor_tensor(out=kr[R, :], in0=kn[R, 0:TC], in1=ks[R, 0:TC], op=ALU.add), reads=[knt, kst], writes=[krt])
        for h in range(8):
            s.add("pool", lambda e: e.tensor_copy(out=self.K[R, h, t0:t0 + TC], in_=kr[R, :]), reads=[krt], writes=[("K", h, tg)])
        self.ffr(i_kn)
        self.ffr(i_ks)
        self.bfr(i_kr)

        buf, wt = self.wload([(lambda b: b[:, 0:2048].rearrange("p (k n) -> p k n", k=2), wukv[:, :, :])])
        wv = buf[:, 0:2048].rearrange("p (k n) -> p k n", k=2)
        for h in range(8):
            pk = self.proj(lambda k: wv[:, k, h * 128:h * 128 + 64], 2, lambda k: self.ckvn[:, k, :], ckvt, wt, M=64)
            s.add("act", lambda e: e.activation(out=self.K[0:64, h, t0:t0 + TC], in_=self.psum[pk][0:64, :], func=AF.Identity),
                  reads=[("ps", pk)], writes=[("K", h, tg)])
        wvv = buf[:, 0:2048].rearrange("p (k h two d) -> p k h two d", k=2, h=8, two=2)
        for blk in range(4):
            gb = tg * 4 + blk
            pb = self.next_psum()
            for k in range(2):
                s.add("pe", lambda e: e.matmul(self.psum[pb][:, :].rearrange("p (h d) -> p h d", h=8),
                                               lhsT=self.ckvn[:, k, blk * 128:(blk + 1) * 128], rhs=wvv[:, k, :, 1, :],
                                               start=(k == 0), stop=(k == 1)), reads=ckvt + wt, writes=[("ps", pb)])
            s.add("dve", lambda e: e.tensor_copy(out=self.V[:, gb, :, 0:64], in_=self.psum[pb][:, :].rearrange("p (h d) -> p h d", h=8)),
                  reads=[("ps", pb)], writes=[("V", gb)])

        for h in range(8):
            if h % 4 == 0:
                g = h // 4
                bufq, wtq = self.wload([(lambda b: b[:, 0:1536].rearrange("p (k n) -> p k n", k=3), wq[:, :, g * 512:(g + 1) * 512])])
                wvq = bufq[:, 0:1536].rearrange("p (k n) -> p k n", k=3)
            c0 = (h % 4) * 128
            pqn = self.proj(lambda k: wvq[:, k, c0:c0 + 96], 3, lambda k: self.cqn[:, k, :], cqt, wtq, M=96)
            pqs = self.proj(lambda k: wvq[:, k, c0 + 32:c0 + 128], 3, lambda k: self.cqn[:, k, :], cqt, wtq, M=96)
            i_q, Qh, Qt = self.bt()
            i_t1, t1, t1t = self.ft()
            i_t2, t2, t2t = self.ft()
            s.add("act", lambda e: e.activation(out=Qh[0:64, :], in_=self.psum[pqn][0:64, :], func=AF.Identity), reads=[("ps", pqn)], writes=[Qt])
            s.add("dve", lambda e: e.tensor_tensor(out=t1[R, 0:TC], in0=self.psum[pqn][R, :], in1=cos[R, 0:TC], op=ALU.mult),
                  reads=[("ps", pqn), cost], writes=[t1t])
            s.add("dve", lambda e: e.tensor_tensor(out=t2[R, 0:TC], in0=self.psum[pqs][R, :], in1=sin[R, 0:TC], op=ALU.mult),
                  reads=[("ps", pqs), sint], writes=[t2t])
            s.add("dve", lambda e: e.tensor_tensor(out=Qh[R, :], in0=t1[R, 0:TC], in1=t2[R, 0:TC], op=ALU.add), reads=[t1t, t2t, Qt], writes=[Qt])
            self.ffr(i_t1)
            self.ffr(i_t2)
            pO = 6 + (h % 2)
            for kb in range(4 * tg + 4):
                kl = kb - 4 * tg
                q0 = max(kl, 0) * 128
                psn = self.next_psum()
                s.add("pe", lambda e: e.matmul(self.psum[psn][:, q0:512], lhsT=self.K[0:96, h, kb * 128:(kb + 1) * 128],
                                               rhs=Qh[0:96, q0:512], start=True, stop=True),
                      reads=[("K", h, kb // 4), Qt], writes=[("ps", psn)])
                i_p, pT, pTt = self.bt()
                s.add("act", lambda e: e.activation(out=pT[:, q0:512], in_=self.psum[psn][:, q0:512], func=AF.Exp, scale=SCALE),
                      reads=[("ps", psn)], writes=[pTt])
                if kl >= 0:
                    s.add("pool", lambda e: e.tensor_tensor(out=pT[:, q0:q0 + 128], in0=pT[:, q0:q0 + 128], in1=self.tri[:, :], op=ALU.mult),
                          reads=[pTt, ("tri",)], writes=[pTt])
                for qi in range(max(kl, 0), 4):
                    s.add("pe", lambda e: e.matmul(self.psum[pO][:, qi * 65:(qi + 1) * 65], lhsT=pT[:, qi * 128:(qi + 1) * 128],
                                                   rhs=self.V[:, kb, h, :], start=(kb == 0 and qi == 0), stop=(kb == 4 * tg + qi),
                                                   skip_group_check=True),
                          reads=[pTt, ("V", kb), ("Vones",)], writes=[("ps", pO)])
                self.bfr(i_p)
            self.bfr(i_q)
            s.add("dve", lambda e: e.reciprocal(out=self.rec[:, 0:4], in_=self.psum[pO][:, 0:260].rearrange("p (q d) -> p q d", d=65)[:, :, 64]),
                  reads=[("ps", pO)], writes=[("rec",)])
            for qi in range(4):
                s.add("act", lambda e: e.activation(out=self.O_sb[:, qi, h * 64:(h + 1) * 64], in_=self.psum[pO][:, qi * 65:qi * 65 + 64],
                                                    func=AF.Identity, scale=self.rec[:, qi:qi + 1]),
                      reads=[("ps", pO), ("rec",)], writes=[("O_sb", qi, h)])
        self.ffr(i_cos)
        self.ffr(i_sin)
        for qi in range(4):
            pb = self.next_psum()
            pbf = self.psum[pb][:, :].bitcast(BF16)
            for fc in range(4):
                s.add("pe", lambda e: e.transpose(out=pbf[:, fc * 128:(fc + 1) * 128], in_=self.O_sb[:, qi, fc * 128:(fc + 1) * 128],
                                                  identity=self.ident[:, :]),
                      reads=[("O_sb", qi, hh) for hh in range(8)] + [("ident",)], writes=[("ps", pb)])
            s.add("dve", lambda e: e.tensor_copy(out=self.OT[:, :, qi * 128:(qi + 1) * 128],
                                                 in_=pbf[:, 0:512].rearrange("p (k t) -> p k t", k=4)),
                  reads=[("ps", pb)], writes=[("OT", qi)])

        for o in range(8):
            bufG, wtG = self.wload([(lambda b: v8(b, 384), winG[:, :, o, :])])
            wvG = v8(bufG, 384)
            bufR, wtR = self.wload([(lambda b: b[:, 0:1536].rearrange("p (k n) -> p k n", k=4), wbr[:, :, o, :])])
            wvR = bufR[:, 0:1536].rearrange("p (k n) -> p k n", k=4)
            pg = [self.proj(lambda k: wvG[:, k, br * 128:(br + 1) * 128], 8, hm_rhs, hmt, wtG) for br in range(3)]
            pya = self.proj(lambda k: wvR[:, k, 0:128], 4, lambda k: self.m_a[:, k, :], [("m_a", j) for j in range(4)], wtR)
            pyb = self.proj(lambda k: wvR[:, k, 128:256], 4, lambda k: self.OT[:, k, :], [("OT", q) for q in range(4)], wtR)
            pyc = self.proj(lambda k: wvR[:, k, 256:384], 4, lambda k: self.cs[:, k, :], [("cs", j) for j in range(4)], wtR)
            gs = []
            for br in range(3):
                fi, gt_, gtok_ = self.ft()
                s.add("act", lambda e: e.activation(out=gt_[:, 0:TC], in_=self.psum[pg[br]][:, :], func=AF.Sigmoid,
                                                    bias=self.vcol(l, "b_G", br * 8 + o)), reads=[("ps", pg[br]), vt], writes=[gtok_])
                gs.append((fi, gt_, gtok_))
            (f0, g0, g0t), (f1, g1, g1t), (f2, g2, g2t) = gs
            s.add("dve", lambda e: e.tensor_tensor(out=g0[:, 0:TC], in0=g0[:, 0:TC], in1=self.psum[pya][:, :], op=ALU.mult),
                  reads=[g0t, ("ps", pya)], writes=[g0t])
            s.add("dve", lambda e: e.tensor_tensor(out=g1[:, 0:TC], in0=g1[:, 0:TC], in1=self.psum[pyb][:, :], op=ALU.mult),
                  reads=[g1t, ("ps", pyb)], writes=[g1t])
            s.add("dve", lambda e: e.scalar_tensor_tensor(out=g2[:, 0:TC], in0=self.psum[pyc][:, :], scalar=self.vcol(l, "cb_out", o),
                                                          in1=g2[:, 0:TC], op0=ALU.add, op1=ALU.mult), reads=[g2t, ("ps", pyc), vt], writes=[g2t])
            s.add("pool", lambda e: e.tensor_tensor(out=g0[:, 0:TC], in0=g0[:, 0:TC], in1=g1[:, 0:TC], op=ALU.add), reads=[g0t, g1t], writes=[g0t])
            s.add("dve", lambda e: e.tensor_tensor(out=self.merged[:, o, :], in0=g0[:, 0:TC], in1=g2[:, 0:TC], op=ALU.add),
                  reads=[g0t, g2t], writes=[("merged", o)])
            self.ffr(f0)
            self.ffr(f1)
            self.ffr(f2)
        mt = [("merged", o) for o in range(8)]
        for op_ in range(4):
            buf, wt = self.wload([(lambda b: v8(b, 256), wout[:, :, op_ * 256:(op_ + 1) * 256])])
            wv = v8(buf, 256)
            for hf in range(2):
                o2 = op_ * 2 + hf
                py = self.proj(lambda k: wv[:, k, hf * 128:(hf + 1) * 128], 8, lambda k: self.merged[:, k, :], mt, wt)
                s.add("dve", lambda e: e.tensor_tensor(out=self.x[:, o2, t0:t0 + TC], in0=self.psum[py][:, :], in1=self.x[:, o2, t0:t0 + TC], op=ALU.add),
                      reads=[("ps", py), ("x", o2, tg)], writes=[("x", o2, tg)])


_WNAMES = ["w1r_a", "w2_a", "winA", "winC", "winB", "wkpe", "winG", "wbr", "wq", "wukv", "wout", "gwd", "w1r_b", "w2_b"]
_WSHAPES = {
    "w1r_a": [D, NFC, 256], "w2_a": [DFF, D], "w1r_b": [D, NFC, 256], "w2_b": [DFF, D],
    "winA": [D, 4, 256], "winC": [D, 4, 256], "winB": [D, 640], "wkpe": [D, 192], "winG": [D, 8, 384],
    "wbr": [512, 8, 384], "wq": [384, 1024], "wukv": [256, 1024], "wout": [D, D], "gwd": [128, 1024],
}


def build_program(S=2048, L=DEPTH, parts=("ffn1", "mixer", "ffn2"), final=True):
    b = Builder(S, L)
    nc = b.nc
    xT = b.din("xT", [D, S])
    outT = nc.dram_tensor("outT", [D, S], F32, kind="ExternalOutput")
    W = []
    for l in range(L):
        W.append({n: b.din("%s_%d" % (n, l), _WSHAPES[n]) for n in _WNAMES})
    b.setup()
    b.load_consts()
    b.load_x(xT)
    for l in range(L):
        if "ffn1" in parts:
            b.ffn(l, "ffn1_norm", W[l]["w1r_a"], W[l]["w2_a"])
        if "mixer" in parts:
            b.s.barrier()
            b.mixer_layer_setup(l, W[l]["gwd"])
            for tg in range(b.NTC):
                b.mixer_chunk(l, tg, W[l])
            b.s.barrier()
        if "ffn2" in parts:
            b.ffn(l, "ffn2_norm", W[l]["w1r_b"], W[l]["w2_b"])
    if final:
        b.final_norm()
    b.store_x(outT)
    b.s.emit(nc)
    return nc, b


def _fm(v):
    return np.ascontiguousarray(v.reshape(-1, 128).T)


def host_layout(inp, l):
    f32 = np.float32
    w = {}
    for tag, nm in (("a", "ffn1"), ("b", "ffn2")):
        w1 = inp[nm + "_w1"][l]
        w["w1r_" + tag] = np.ascontiguousarray(
            np.concatenate([w1[:, :DFF].reshape(D, NFC, 128), w1[:, DFF:].reshape(D, NFC, 128)], axis=2))
        w["w2_" + tag] = np.ascontiguousarray(inp[nm + "_w2"][l])
    win = inp["w_in"][l]
    w["winA"] = np.ascontiguousarray(np.concatenate([win[:, 0:512].reshape(D, 4, 128), win[:, 512:1024].reshape(D, 4, 128)], axis=2))
    w["winB"] = np.ascontiguousarray(win[:, 1024:1664])
    kp = win[:, 1664:1696]
    wk = np.zeros((D, 192), f32)
    wk[:, 64:96] = kp
    wk[:, 160:176] = kp[:, 16:32]
    wk[:, 176:192] = kp[:, 0:16]
    w["wkpe"] = wk
    w["winC"] = np.ascontiguousarray(np.concatenate([win[:, 1696:2208].reshape(D, 4, 128), win[:, 2208:2720].reshape(D, 4, 128)], axis=2))
    G = win[:, 2720:].reshape(D, 3, 8, 128)
    w["winG"] = np.ascontiguousarray(G.transpose(0, 2, 1, 3).reshape(D, 8, 384))
    br = np.stack([inp["lru_w_out"][l].reshape(512, 8, 128), inp["mla_w_o"][l].reshape(512, 8, 128),
                   inp["conv_w_out"][l].reshape(512, 8, 128)], axis=2)
    w["wbr"] = np.ascontiguousarray(br.reshape(512, 8, 384))
    uq = inp["w_uq"][l].reshape(384, 8, 96)
    w["wq"] = np.ascontiguousarray(np.concatenate([uq, uq[:, :, 80:96], uq[:, :, 64:80]], axis=2).reshape(384, 1024))
    w["wukv"] = np.ascontiguousarray(inp["w_ukv"][l])
    w["wout"] = np.ascontiguousarray(inp["w_out"][l])
    wg = inp["lru_w_gate"][l]
    gwd = np.zeros((128, 8, 128), f32)
    for j in range(4):
        for e in range(2):
            hd = 2 * j + e
            gwd[e * 64:(e + 1) * 64, 2 * j, e * 64:(e + 1) * 64] = wg[hd][:, 0:64]
            gwd[e * 64:(e + 1) * 64, 2 * j + 1, e * 64:(e + 1) * 64] = wg[hd][:, 64:128]
    w["gwd"] = gwd.reshape(128, 1024)
    v = np.zeros((128, NVEC), f32)
    def put(name, arr):
        c = _VC[name]
        v[:, c:c + arr.shape[1]] = arr
    put("ffn1_norm", _fm(inp["ffn1_norm"][l])); put("mix_norm", _fm(inp["mix_norm"][l])); put("ffn2_norm", _fm(inp["ffn2_norm"][l]))
    bi = inp["b_in"][l]
    put("b_xa", _fm(bi[0:512])); put("b_ga", _fm(bi[512:1024])); put("b_cq", _fm(bi[1024:1408])); put("b_ckv", _fm(bi[1408:1664]))
    bk = np.zeros((128, 2), f32)
    bk[64:96, 0] = bi[1664:1696]
    bk[64:80, 1] = bi[1680:1696]
    bk[80:96, 1] = bi[1664:1680]
    put("b_kpe", bk)
    put("b_cv", _fm(bi[1696:2208])); put("b_cg", _fm(bi[2208:2720])); put("b_G", _fm(bi[2720:]))
    cw = inp["lru_conv_w"][l]
    put("lru_cw", np.ascontiguousarray(cw.reshape(4, 4, 128).transpose(2, 1, 0).reshape(128, 16)))
    put("lru_cb", _fm(inp["lru_conv_b"][l]))
    bg = inp["lru_b_gate"][l].reshape(4, 2, 2, 64)
    put("b_r", np.ascontiguousarray(bg[:, :, 0, :].transpose(1, 2, 0).reshape(128, 4)))
    put("b_i", np.ascontiguousarray(bg[:, :, 1, :].transpose(1, 2, 0).reshape(128, 4)))
    put("lam", _fm(inp["lru_lambda"][l])); put("q_norm", _fm(inp["q_norm"][l])); put("kv_norm", _fm(inp["kv_norm"][l]))
    dw = inp["conv_dw_w"][l]
    put("dw_w", np.ascontiguousarray(dw.reshape(31, 4, 128).transpose(2, 1, 0).reshape(128, 124)))
    put("dw_b", _fm(inp["conv_dw_b"][l])); put("ln_g", _fm(inp["conv_ln_g"][l])); put("ln_b", _fm(inp["conv_ln_b"][l]))
    put("cb_out", _fm(inp["conv_b_out"][l]))
    return w, v


def host_consts(inp, S):
    f32 = np.float32
    c = {}
    c["c_ident"] = np.eye(128, dtype=f32)
    c["c_tri"] = np.triu(np.ones((128, 128), f32))
    g = np.zeros((128, NGL), f32)
    g[:, 0:8] = _fm(inp["final_norm"])
    inv = (10000.0 ** (-np.arange(0, 32, 2, dtype=np.float32) / 32)).astype(f32)
    g[64:96, 8] = np.concatenate([inv, inv])
    g[64:80, 9] = -1.0
    g[80:96, 9] = 1.0
    c["c_glob"] = g
    return c


def make_in_maps(inp, S=2048, L=DEPTH, cores=range(8)):
    shared = {}
    for l in range(L):
        w, v = host_layout(inp, l)
        for n in _WNAMES:
            shared["%s_%d" % (n, l)] = w[n]
        shared["vecs%d" % l] = v
    shared.update(host_consts(inp, S))
    maps = []
    for b in cores:
        m = dict(shared)
        m["xT"] = np.ascontiguousarray(inp["x"][b, :S].T)
        m["pos_rep"] = np.ascontiguousarray(np.broadcast_to(inp["positions"][b, :S][None, :], (96, S))).astype(np.int32)
        maps.append(m)
    return maps


_CACHE = {}


def kernel(**inputs):
    inp = {k: np.asarray(v) for k, v in inputs.items()}
    if "prog" not in _CACHE:
        _CACHE["prog"] = build_program()[0]
    nc = _CACHE["prog"]
    maps = make_in_maps(inp)
    res = run_bass_kernel_spmd(nc, maps, core_ids=list(range(8)))
    out = np.stack([np.ascontiguousarray(r["outT"].T) for r in res.results], axis=0)
    return out.astype(np.float32)
```

```python
import numpy as np
import concourse.bass as bass
import concourse.mybir as mybir
from concourse.bass_utils import run_bass_kernel_spmd

F32 = mybir.dt.float32
BF16 = mybir.dt.bfloat16
I32 = mybir.dt.int32
AF = mybir.ActivationFunctionType
ALU = mybir.AluOpType
AX = mybir.AxisListType

D = 1024
S = 2048
DFF = 2816
DEPTH = 2
EPS = 1e-6
NKC = D // 128
NFC = DFF // 128
TC = 512
NTC = S // TC


class _Op:
    __slots__ = ("eng", "fn", "idx", "waits", "signal", "sem", "val", "dma", "semkey")

    def __init__(self, eng, fn, dma, semkey):
        self.eng = eng
        self.fn = fn
        self.dma = dma
        self.semkey = semkey
        self.waits = []
        self.signal = dma
        self.sem = None
        self.val = 0
        self.idx = 0


class _Rec:
    def __getattr__(self, name):
        return lambda *a, **k: (name, a, k)


_REC = _Rec()


class Sched:
    ENGS = ("pe", "act", "dve", "pool", "sp")

    def __init__(self):
        self.ops = {e: [] for e in self.ENGS}
        self.last_w = {}
        self.readers = {}
        self.waited = {e: {} for e in self.ENGS}
        self.dma_seq = {}
        self.all_dma = []

    def add(self, eng, fn, reads=(), writes=(), dma=False, semkey=None, extra=()):
        if fn is not None:
            name_, a_, k_ = fn(_REC)
            fn = (lambda e, name_=name_, a_=a_, k_=k_: getattr(e, name_)(*a_, **k_))
        op = _Op(eng, fn, dma, semkey)
        op.idx = len(self.ops[eng])
        if dma:
            assert semkey is not None
        deps = list(extra)
        for r in reads:
            w = self.last_w.get(r)
            if w is not None:
                deps.append(w)
        for wtok in writes:
            w = self.last_w.get(wtok)
            if w is not None:
                deps.append(w)
            deps.extend(self.readers.get(wtok, ()))
        for p in deps:
            if p is op:
                continue
            if p.dma:
                key = ("dma", p.semkey)
                pos = p.val
            else:
                if p.eng == "pe" and eng == "pe":
                    continue
                key = p.eng
                pos = p.idx + 1
            if self.waited[eng].get(key, 0) >= pos:
                continue
            self.waited[eng][key] = pos
            p.signal = True
            op.waits.append(p)
        if dma:
            n = self.dma_seq.get(semkey, 0) + 1
            self.dma_seq[semkey] = n
            op.val = n
            self.all_dma.append(op)
        for r in reads:
            self.readers.setdefault(r, []).append(op)
        for wtok in writes:
            self.last_w[wtok] = op
            self.readers[wtok] = []
        self.ops[eng].append(op)
        return op

    def barrier(self):
        comp = ("pe", "act", "dve")
        last = {}
        for e in comp:
            for op in reversed(self.ops[e]):
                if not op.dma and op.fn is not None:
                    last[e] = op
                    break
        for e in comp:
            self.add(e, None, extra=[last[f] for f in comp if f != e and f in last])

    def emit(self, nc, final_wait_eng="sp"):
        from contextlib import ExitStack
        with ExitStack() as es:
            esem = {e: es.enter_context(nc.semaphore("s_" + e)) for e in self.ENGS}
            dsem = {}
            for k in self.dma_seq:
                dsem[k] = es.enter_context(nc.semaphore("d_%s" % (len(dsem),)))
            for e in self.ENGS:
                c = 0
                for op in self.ops[e]:
                    if op.dma:
                        op.sem = dsem[op.semkey]
                        op.val = op.val * 16
                    elif op.signal:
                        assert op.fn is not None
                        c += 1
                        op.sem = esem[e]
                        op.val = c
            block = es.enter_context(nc.Block())

            def run(e, engobj, extra_final=False):
                for op in self.ops[e]:
                    for p in op.waits:
                        engobj.wait_ge(p.sem, p.val)
                    if op.fn is None:
                        continue
                    ins = op.fn(engobj)
                    if op.dma:
                        ins.then_inc(op.sem, 16)
                    elif op.signal:
                        ins.then_inc(op.sem, 1)
                if extra_final:
                    for k, n in self.dma_seq.items():
                        engobj.wait_ge(dsem[k], 16 * n)

            @block.tensor
            def _(eng):
                run("pe", eng)

            @block.scalar
            def _(eng):
                run("act", eng)

            @block.vector
            def _(eng):
                run("dve", eng)

            @block.gpsimd
            def _(eng):
                run("pool", eng)

            @block.sync
            def _(eng):
                run("sp", eng, extra_final=True)


_VC = {}
def _vc_build():
    c = 0
    def put(name, n):
        nonlocal c
        _VC[name] = c
        c += n
    put("ffn1_norm", 8); put("mix_norm", 8); put("ffn2_norm", 8)
    put("b_xa", 4); put("b_ga", 4); put("b_cq", 3); put("b_ckv", 2); put("b_kpe", 2)
    put("b_cv", 4); put("b_cg", 4); put("b_G", 24)
    put("lru_cw", 16); put("lru_cb", 4); put("b_r", 4); put("b_i", 4); put("lam", 4)
    put("q_norm", 3); put("kv_norm", 2); put("dw_w", 124); put("dw_b", 4)
    put("ln_g", 4); put("ln_b", 4); put("cb_out", 8)
    return c
NVEC = _vc_build()
NFT = 9
NGL = 16

SCALE = 96.0 ** -0.5
GELU_K = 1.5957691216057308
PI = 3.141592653589793


class Builder:
    def __init__(self, S=2048, L=DEPTH):
        self.nc = bass.Bass("TRN2", target_bir_lowering=False)
        self.s = Sched()
        self.S = S
        self.L = L
        self.NTC = S // TC
        self.dram_in = {}
        self.psum_rr = 0
        self.wp_rr = 0

    def din(self, name, shape, dt=F32):
        t = self.nc.dram_tensor(name, list(shape), dt, kind="ExternalInput")
        self.dram_in[name] = t
        return t

    def sb(self, name, shape, dt):
        return self.nc.alloc_sbuf_tensor(name, list(shape), dt)

    def setup(self):
        nc, s, S = self.nc, self.s, self.S
        self.x = self.sb("x", [128, NKC, S], F32)
        self.psum = [nc.alloc_psum_tensor("ps%d" % i, [128, 512], F32) for i in range(8)]
        self.wp = [self.sb("wp%d" % i, [128, 3072], BF16) for i in range(4)]
        self.ftmp = [self.sb("ft%d" % i, [128, 544], F32) for i in range(NFT)]
        self.btmp = [self.sb("bt%d" % i, [128, 512], BF16) for i in range(5)]
        self.ffree = list(range(NFT))
        self.bfree = list(range(5))
        NB = S // 128
        mix_elems = 8 * S + NB * 8 * 65 + 4096 + 5 * 2048 + 4096 + 64
        ffn_elems = 8 * 1024 + 22 * 1024
        self.arena = self.sb("arena", [128, max(mix_elems, ffn_elems)], BF16)
        a = self.arena
        self.h = a[:, 0:8192].rearrange("p (k t) -> p k t", k=8)
        self.mid = a[:, 8192:8192 + 22528].rearrange("p (f t) -> p f t", f=22)
        o = 0
        def carve(n):
            nonlocal o
            v = a[:, o:o + n]
            o += n
            return v
        self.K = carve(8 * S).rearrange("p (h t) -> p h t", h=8)
        self.V = carve(NB * 8 * 65).rearrange("p (b h d) -> p b h d", b=NB, h=8)
        self.hm = carve(4096).rearrange("p (k t) -> p k t", k=8)
        self.m_a = carve(2048).rearrange("p (k t) -> p k t", k=4)
        self.cs = carve(2048).rearrange("p (k t) -> p k t", k=4)
        self.O_sb = carve(2048).rearrange("p (q f) -> p q f", q=4)
        self.OT = carve(2048).rearrange("p (k t) -> p k t", k=4)
        ccv = carve(2048)
        self.cc = ccv.rearrange("p (k t) -> p k t", k=4)
        self.ckvn = ccv[:, 0:1024].rearrange("p (k t) -> p k t", k=2)
        mgv = carve(4096)
        self.merged = mgv.rearrange("p (k t) -> p k t", k=8)
        self.cqn = mgv[:, 0:1536].rearrange("p (k t) -> p k t", k=3)
        self.ones_bf = self.sb("ones_bf", [128, 128], BF16)
        self.ident = self.sb("ident", [128, 128], BF16)
        self.tri = self.sb("tri", [128, 128], BF16)
        self.epsb = self.sb("epsb", [128, 1], F32)
        self.oneb = self.sb("oneb", [128, 1], F32)
        self.vecs = [self.sb("svecs%d" % l, [128, NVEC], F32) for l in range(self.L)]
        self.glob = self.sb("glob", [128, NGL], F32)
        self.gw = self.sb("gw", [128, 8, 128], BF16)
        self.halo_a = self.sb("halo_a", [128, 4, 3], F32)
        self.halo_c = self.sb("halo_c", [128, 4, 30], F32)
        self.state = self.sb("state", [128, 4], F32)
        self.nsp = self.sb("nsp", [128, 8], F32)
        self.rec = self.sb("rec", [128, 4], F32)
        s.add("pool", lambda e: e.memset(self.ones_bf[:], 1.0), writes=[("ones_bf",)])
        s.add("pool", lambda e: e.memset(self.oneb[:], 1.0), writes=[("oneb",)])
        s.add("pool", lambda e: e.memset(self.epsb[:], EPS), writes=[("epsb",)])

    def load_consts(self):
        s = self.s
        dr_ident = self.din("c_ident", [128, 128])
        dr_tri = self.din("c_tri", [128, 128])
        dr_glob = self.din("c_glob", [128, NGL])
        self.dr_pos = self.din("pos_rep", [96, self.S], I32)
        s.add("pool", lambda e: e.dma_start(out=self.ident[:, :], in_=dr_ident[:, :]),
              writes=[("ident",)], dma=True, semkey=("c", 0))
        s.add("pool", lambda e: e.dma_start(out=self.tri[:, :], in_=dr_tri[:, :]),
              writes=[("tri",)], dma=True, semkey=("c", 1))
        s.add("sp", lambda e: e.dma_start(out=self.glob[:, :], in_=dr_glob[:, :]),
              writes=[("glob",)], dma=True, semkey=("c", 2))
        self.dr_vecs = []
        for l in range(self.L):
            dv = self.din("vecs%d" % l, [128, NVEC])
            self.dr_vecs.append(dv)
            s.add("sp", lambda e, l=l, dv=dv: e.dma_start(out=self.vecs[l][:, :], in_=dv[:, :]),
                  writes=[("vecs", l)], dma=True, semkey=("vecs", l))

    def vcol(self, l, name, j=0, n=1):
        c = _VC[name] + j
        return self.vecs[l][:, c:c + n]

    def ft(self):
        i = self.ffree.pop(0)
        return i, self.ftmp[i], ("ft", i)

    def ffr(self, i):
        self.ffree.append(i)

    def bt(self):
        i = self.bfree.pop(0)
        return i, self.btmp[i], ("bt", i)

    def bfr(self, i):
        self.bfree.append(i)

    def next_psum(self):
        i = self.psum_rr
        self.psum_rr = (i + 1) % 6
        return i

    def wload(self, dmas):
        s = self.s
        b = self.wp_rr
        self.wp_rr = (b + 1) % 4
        buf = self.wp[b]
        for n, (dstf, src) in enumerate(dmas):
            s.add("pool", lambda e, dstf=dstf, src=src, buf=buf: e.dma_start(out=dstf(buf), in_=src),
                  writes=[("wp", b, n)], dma=True, semkey=("wp", b, n))
        return buf, [("wp", b, n) for n in range(len(dmas))]

    def load_x(self, xT):
        s = self.s
        for c in range(NKC):
            s.add("sp", lambda e, c=c: e.dma_start(out=self.x[:, c, :], in_=xT[c * 128:(c + 1) * 128, :]),
                  writes=[("x", c, t) for t in range(self.NTC)], dma=True, semkey=("xload", c))

    def store_x(self, outT):
        s = self.s
        for c in range(NKC):
            s.add("sp", lambda e, c=c: e.dma_start(out=outT[c * 128:(c + 1) * 128, :], in_=self.x[:, c, :]),
                  reads=[("x", c, t) for t in range(self.NTC)], dma=True, semkey=("xstore",))

    def rstd_from(self, srcs, src_toks, n_feat):
        s = self.s
        pb = self.next_psum()
        ps = self.psum[pb]
        nk = len(srcs)
        for k, (ap, tok) in enumerate(zip(srcs, src_toks)):
            bi, bq, btok = self.bt()
            s.add("act", lambda e, ap=ap, bq=bq: e.activation(out=bq[:, :], in_=ap, func=AF.Square),
                  reads=[tok], writes=[btok])
            s.add("pe", lambda e, k=k, bq=bq: e.matmul(ps[:, :], lhsT=self.ones_bf[:, :], rhs=bq[:, :],
                                                        start=(k == 0), stop=(k == nk - 1)),
                  reads=[btok, ("ones_bf",)], writes=[("ps", pb)])
            self.bfr(bi)
        fi, r, rtok = self.ft()
        s.add("act", lambda e: e.activation(out=r[:, 0:TC], in_=ps[:, :], func=AF.Sqrt,
                                            bias=self.epsb[:, 0:1], scale=1.0 / n_feat),
              reads=[("ps", pb), ("epsb",)], writes=[rtok])
        s.add("dve", lambda e: e.reciprocal(out=r[:, 0:TC], in_=r[:, 0:TC]), reads=[rtok], writes=[rtok])
        return fi, r, rtok

    def rmsnorm_chunk(self, tg, gain, gtok, h_out, hcol0, htoks):
        s = self.s
        t0 = tg * TC
        fi, r, rtok = self.rstd_from([self.x[:, k, t0:t0 + TC] for k in range(NKC)],
                                     [("x", k, tg) for k in range(NKC)], D)
        for k in range(NKC):
            s.add("dve", lambda e, k=k: e.scalar_tensor_tensor(
                out=h_out[:, k, hcol0:hcol0 + TC], in0=self.x[:, k, t0:t0 + TC], scalar=gain[:, k:k + 1],
                in1=r[:, 0:TC], op0=ALU.mult, op1=ALU.mult),
                reads=[("x", k, tg), rtok, gtok], writes=[htoks[k]])
        self.ffr(fi)

    def ffn(self, l, which, w1r, w2):
        s = self.s
        TH = min(1024, self.S)
        nloc = TH // TC
        gain = self.vcol(l, which, 0, 8)
        gtok = ("vecs", l)
        w1v = w1r.rearrange("(kc p) f n -> p kc f n", p=128)
        w2v = w2.rearrange("(fc p) n -> p fc n", p=128)
        for half in range(self.S // TH):
            for tcl in range(nloc):
                tg = half * nloc + tcl
                self.rmsnorm_chunk(tg, gain, gtok, self.h, tcl * TC, [("h", tcl)] * 8)
            for f in range(NFC):
                buf, wt = self.wload([(lambda b: b[:, 0:2048].rearrange("p (k n) -> p k n", k=8), w1v[:, :, f, :])])
                wv = buf[:, 0:2048].rearrange("p (k n) -> p k n", k=8)
                for tcl in range(nloc):
                    pg = self.next_psum()
                    pu = self.next_psum()
                    for k in range(NKC):
                        s.add("pe", lambda e, k=k, wv=wv, pg=pg, tcl=tcl: e.matmul(
                            self.psum[pg][:, :], lhsT=wv[:, k, 0:128], rhs=self.h[:, k, tcl * TC:(tcl + 1) * TC],
                            start=(k == 0), stop=(k == NKC - 1)),
                            reads=wt + [("h", tcl)], writes=[("ps", pg)])
                    for k in range(NKC):
                        s.add("pe", lambda e, k=k, wv=wv, pu=pu, tcl=tcl: e.matmul(
                            self.psum[pu][:, :], lhsT=wv[:, k, 128:256], rhs=self.h[:, k, tcl * TC:(tcl + 1) * TC],
                            start=(k == 0), stop=(k == NKC - 1)),
                            reads=wt + [("h", tcl)], writes=[("ps", pu)])
                    fi, sg, sgt = self.ft()
                    s.add("act", lambda e, sg=sg, pg=pg: e.activation(out=sg[:, 0:TC], in_=self.psum[pg][:, :], func=AF.Silu),
                          reads=[("ps", pg)], writes=[sgt])
                    s.add("dve", lambda e, sg=sg, pu=pu, f=f, tcl=tcl: e.tensor_tensor(
                        out=self.mid[:, f, tcl * TC:(tcl + 1) * TC], in0=sg[:, 0:TC], in1=self.psum[pu][:, :], op=ALU.mult),
                        reads=[sgt, ("ps", pu)], writes=[("mid", f, tcl)])
                    self.ffr(fi)
            for o in range(NKC):
                buf, wt = self.wload([(lambda b: b[:, 0:2816].rearrange("p (f n) -> p f n", f=22), w2v[:, :, o * 128:(o + 1) * 128])])
                wv = buf[:, 0:2816].rearrange("p (f n) -> p f n", f=22)
                for tcl in range(nloc):
                    tg = half * nloc + tcl
                    py = self.next_psum()
                    for f in range(NFC):
                        s.add("pe", lambda e, f=f, wv=wv, py=py, tcl=tcl: e.matmul(
                            self.psum[py][:, :], lhsT=wv[:, f, :], rhs=self.mid[:, f, tcl * TC:(tcl + 1) * TC],
                            start=(f == 0), stop=(f == NFC - 1)),
                            reads=wt + [("mid", f, tcl)], writes=[("ps", py)])
                    s.add("dve", lambda e, o=o, py=py, tg=tg: e.scalar_tensor_tensor(
                        out=self.x[:, o, tg * TC:(tg + 1) * TC], in0=self.psum[py][:, :], scalar=0.5,
                        in1=self.x[:, o, tg * TC:(tg + 1) * TC], op0=ALU.mult, op1=ALU.add),
                        reads=[("ps", py), ("x", o, tg)], writes=[("x", o, tg)])

    def final_norm(self):
        for tg in range(self.NTC):
            self.rmsnorm_chunk(tg, self.glob[:, 0:8], ("glob",), self.x, tg * TC, [("x", k, tg) for k in range(NKC)])

    def proj(self, lhs_fn, nk, rhs_fn, rtoks, wtoks, M=128, N=TC, pb=None):
        s = self.s
        if pb is None:
            pb = self.next_psum()
        for k in range(nk):
            lt = lhs_fn(k)
            rt = rhs_fn(k)
            s.add("pe", lambda e: e.matmul(self.psum[pb][0:M, 0:N], lhsT=lt, rhs=rt,
                                           start=(k == 0), stop=(k == nk - 1)),
                  reads=list(wtoks) + list(rtoks), writes=[("ps", pb)])
        return pb

    def mixer_layer_setup(self, l, gwd):
        s = self.s
        s.add("pool", lambda e: e.dma_start(out=self.gw[:, :, :], in_=gwd.rearrange("p (a b) -> p a b", a=8)),
              writes=[("gw",)], dma=True, semkey=("gw",))
        lam = self.vcol(l, "lam", 0, 4)
        s.add("act", lambda e: e.activation(out=self.nsp[:, 0:4], in_=lam, func=AF.Exp, scale=-1.0),
              reads=[("vecs", l)], writes=[("nsp",)])
        s.add("act", lambda e: e.activation(out=self.nsp[:, 0:4], in_=self.nsp[:, 0:4], func=AF.Ln, bias=1.0),
              reads=[("nsp",)], writes=[("nsp",)])
        s.add("dve", lambda e: e.tensor_scalar(out=self.nsp[:, 4:8], in0=self.nsp[:, 0:4], scalar1=-16.0, scalar2=None, op0=ALU.mult),
              reads=[("nsp",)], writes=[("nsp2",)])
        s.add("dve", lambda e: e.tensor_scalar(out=self.nsp[:, 0:4], in0=self.nsp[:, 0:4], scalar1=-8.0, scalar2=None, op0=ALU.mult),
              reads=[("nsp",), ("nsp2",)], writes=[("nsp",)])
        s.add("dve", lambda e: e.memset(self.halo_a[:], 0.0), writes=[("halo_a", j) for j in range(4)])
        s.add("dve", lambda e: e.memset(self.halo_c[:], 0.0), writes=[("halo_c", j) for j in range(4)])
        s.add("dve", lambda e: e.memset(self.state[:], 0.0), writes=[("state", j) for j in range(4)])
        s.add("dve", lambda e: e.memset(self.V[:, :, :, 64:65], 1.0), writes=[("Vones",)])

    def mixer_chunk(self, l, tg, W):
        s = self.s
        t0 = tg * TC
        vt = ("vecs", l)
        hm = self.hm
        hmt = [("hm", k) for k in range(8)]
        self.rmsnorm_chunk(tg, self.vcol(l, "mix_norm", 0, 8), vt, hm, 0, hmt)
        hm_rhs = lambda k: hm[:, k, :]
        v8 = lambda b, n: b[:, 0:8 * n].rearrange("p (k n) -> p k n", k=8)
        winA = W["winA"].rearrange("(kc p) j n -> p kc j n", p=128)
        winC = W["winC"].rearrange("(kc p) j n -> p kc j n", p=128)
        winB = W["winB"].rearrange("(kc p) n -> p kc n", p=128)
        wkpe = W["wkpe"].rearrange("(kc p) n -> p kc n", p=128)
        winG = W["winG"].rearrange("(kc p) o n -> p kc o n", p=128)
        wbr = W["wbr"].rearrange("(kc p) o n -> p kc o n", p=128)
        wq = W["wq"].rearrange("(kc p) n -> p kc n", p=128)
        wukv = W["wukv"].rearrange("(kc p) n -> p kc n", p=128)
        wout = W["wout"].rearrange("(kc p) n -> p kc n", p=128)

        def chainA(j):
            buf, wt = self.wload([(lambda b: v8(b, 256), winA[:, :, j, :])])
            wv = v8(buf, 256)
            pxa = self.proj(lambda k: wv[:, k, 0:128], 8, hm_rhs, hmt, wt)
            pga = self.proj(lambda k: wv[:, k, 128:256], 8, hm_rhs, hmt, wt, pb=6 + (j % 2))
            yield
            i_in, tin, tint = self.ft()
            s.add("act", lambda e: e.activation(out=tin[:, 30:542], in_=self.psum[pxa][:, :], func=AF.Identity, bias=self.vcol(l, "b_xa", j)),
                  reads=[("ps", pxa), vt], writes=[tint])
            s.add("dve", lambda e: e.tensor_copy(out=tin[:, 27:30], in_=self.halo_a[:, j, :]), reads=[("halo_a", j)], writes=[tint])
            yield
            i_xa, xa, xat = self.ft()
            cw = lambda tap: self.vcol(l, "lru_cw", j * 4 + tap)
            s.add("dve", lambda e: e.tensor_scalar(out=xa[:, 0:TC], in0=tin[:, 27:27 + TC], scalar1=cw(0), scalar2=self.vcol(l, "lru_cb", j),
                                                   op0=ALU.mult, op1=ALU.add), reads=[tint, vt], writes=[xat])
            yield
            for tap in range(1, 4):
                s.add("dve", lambda e: e.scalar_tensor_tensor(out=xa[:, 0:TC], in0=tin[:, 27 + tap:27 + tap + TC], scalar=cw(tap), in1=xa[:, 0:TC],
                                                              op0=ALU.mult, op1=ALU.add), reads=[tint, xat, vt], writes=[xat])
                yield
            s.add("dve", lambda e: e.tensor_copy(out=self.halo_a[:, j, :], in_=tin[:, 539:542]), reads=[tint], writes=[("halo_a", j)])
            self.ffr(i_in)
            ib, xab, xabt = self.bt()
            s.add("act", lambda e: e.activation(out=xab[:, :], in_=xa[:, 0:TC], func=AF.Identity), reads=[xat], writes=[xabt])
            pr = self.proj(lambda k: self.gw[:, 2 * j, :], 1, lambda k: xab[:, :], [xabt], [("gw",)])
            pi_ = self.proj(lambda k: self.gw[:, 2 * j + 1, :], 1, lambda k: xab[:, :], [xabt], [("gw",)])
            self.bfr(ib)
            yield
            i_r, r, rt = self.ft()
            s.add("act", lambda e: e.activation(out=r[:, 0:TC], in_=self.psum[pr][:, :], func=AF.Sigmoid, bias=self.vcol(l, "b_r", j)),
                  reads=[("ps", pr), vt], writes=[rt])
            yield
            i_a, a, at = self.ft()
            i_a2, a2, a2t = self.ft()
            s.add("act", lambda e: e.activation(out=a[:, 0:TC], in_=r[:, 0:TC], func=AF.Exp, scale=self.nsp[:, j:j + 1]), reads=[rt, ("nsp",)], writes=[at])
            s.add("act", lambda e: e.activation(out=a2[:, 0:TC], in_=r[:, 0:TC], func=AF.Exp, scale=self.nsp[:, 4 + j:5 + j]), reads=[rt, ("nsp2",)], writes=[a2t])
            self.ffr(i_r)
            yield
            i_i, ii, it = self.ft()
            s.add("act", lambda e: e.activation(out=ii[:, 0:TC], in_=self.psum[pi_][:, :], func=AF.Sigmoid, bias=self.vcol(l, "b_i", j)),
                  reads=[("ps", pi_), vt], writes=[it])
            yield
            s.add("act", lambda e: e.activation(out=a2[:, 0:TC], in_=a2[:, 0:TC], func=AF.Sqrt, scale=-1.0, bias=self.oneb[:, 0:1]), reads=[a2t, ("oneb",)], writes=[a2t])
            s.add("dve", lambda e: e.tensor_tensor(out=ii[:, 0:TC], in0=ii[:, 0:TC], in1=xa[:, 0:TC], op=ALU.mult), reads=[it, xat], writes=[it])
            yield
            s.add("dve", lambda e: e.tensor_tensor(out=ii[:, 0:TC], in0=ii[:, 0:TC], in1=a2[:, 0:TC], op=ALU.mult), reads=[it, a2t], writes=[it])
            self.ffr(i_xa)
            self.ffr(i_a2)
            yield
            i_h, hl, hlt = self.ft()
            s.add("dve", lambda e: e.tensor_tensor_scan(out=hl[:, 0:TC], data0=a[:, 0:TC], data1=ii[:, 0:TC], initial=self.state[:, j:j + 1],
                                                        op0=ALU.mult, op1=ALU.add), reads=[at, it, ("state", j)], writes=[hlt])
            s.add("act", lambda e: e.activation(out=self.state[:, j:j + 1], in_=hl[:, TC - 1:TC], func=AF.Identity), reads=[hlt], writes=[("state", j)])
            self.ffr(i_a)
            self.ffr(i_i)
            yield
            i_g, xg, xgt = self.ft()
            s.add("act", lambda e: e.activation(out=xg[:, 0:TC], in_=self.psum[pga][:, :], func=AF.Identity, bias=self.vcol(l, "b_ga", j)),
                  reads=[("ps", pga), vt], writes=[xgt])
            yield
            i_q, qg, qgt = self.ft()
            s.add("act", lambda e: e.activation(out=qg[:, 0:TC], in_=xg[:, 0:TC], func=AF.Square), reads=[xgt], writes=[qgt])
            yield
            s.add("dve", lambda e: e.tensor_scalar(out=qg[:, 0:TC], in0=qg[:, 0:TC], scalar1=0.044715, scalar2=1.0, op0=ALU.mult, op1=ALU.add),
                  reads=[qgt], writes=[qgt])
            yield
            s.add("dve", lambda e: e.tensor_tensor(out=qg[:, 0:TC], in0=qg[:, 0:TC], in1=xg[:, 0:TC], op=ALU.mult), reads=[qgt, xgt], writes=[qgt])
            yield
            s.add("act", lambda e: e.activation(out=qg[:, 0:TC], in_=qg[:, 0:TC], func=AF.Sigmoid, scale=GELU_K), reads=[qgt], writes=[qgt])
            yield
            s.add("dve", lambda e: e.tensor_tensor(out=qg[:, 0:TC], in0=qg[:, 0:TC], in1=xg[:, 0:TC], op=ALU.mult), reads=[qgt, xgt], writes=[qgt])
            yield
            s.add("dve", lambda e: e.tensor_tensor(out=self.m_a[:, j, :], in0=qg[:, 0:TC], in1=hl[:, 0:TC], op=ALU.mult), reads=[qgt, hlt], writes=[("m_a", j)])
            self.ffr(i_g)
            self.ffr(i_q)
            self.ffr(i_h)

        def zip_run(gens):
            gens = list(gens)
            while gens:
                for g in list(gens):
                    try:
                        next(g)
                    except StopIteration:
                        gens.remove(g)
        zip_run([chainA(0), chainA(1)])
        zip_run([chainA(2), chainA(3)])

        R = slice(64, 96)
        glob = self.glob
        i_ang, ang, angt = self.ft()
        i_cos, cos, cost = self.ft()
        i_sin, sin, sint = self.ft()
        i_msk, msk, mskt = self.ft()
        s.add("sp", lambda e: e.dma_start(out=msk[R, 0:TC].bitcast(I32), in_=self.dr_pos[R, t0:t0 + TC]),
              writes=[mskt], dma=True, semkey=("pos", i_msk))
        s.add("dve", lambda e: e.tensor_copy(out=ang[R, 0:TC], in_=msk[R, 0:TC].bitcast(I32)), reads=[mskt], writes=[angt])
        s.add("dve", lambda e: e.tensor_scalar(out=ang[R, 0:TC], in0=ang[R, 0:TC], scalar1=glob[R, 8:9], scalar2=None, op0=ALU.mult),
              reads=[angt, ("glob",)], writes=[angt])
        for dst, dtok, shift, sc in ((sin, sint, 0.0, glob[R, 9:10]), (cos, cost, PI / 2, 1.0)):
            ki = msk[R, 0:TC].bitcast(I32)
            s.add("dve", lambda e: e.tensor_scalar(out=dst[R, 0:TC], in0=ang[R, 0:TC], scalar1=shift, scalar2=1.0 / (2 * PI),
                                                   op0=ALU.add, op1=ALU.mult), reads=[angt], writes=[dtok])
            s.add("dve", lambda e: e.tensor_copy(out=ki, in_=dst[R, 0:TC]), reads=[dtok], writes=[mskt])
            s.add("dve", lambda e: e.tensor_copy(out=dst[R, 0:TC], in_=ki), reads=[mskt], writes=[dtok])
            s.add("dve", lambda e: e.scalar_tensor_tensor(out=dst[R, 0:TC], in0=dst[R, 0:TC], scalar=-2 * PI, in1=ang[R, 0:TC],
                                                          op0=ALU.mult, op1=ALU.add), reads=[dtok, angt], writes=[dtok])
            s.add("dve", lambda e: e.tensor_scalar(out=dst[R, 0:TC], in0=dst[R, 0:TC], scalar1=shift, scalar2=None, op0=ALU.add),
                  reads=[dtok], writes=[dtok])
            s.add("dve", lambda e: e.tensor_scalar(out=msk[R, 0:TC], in0=dst[R, 0:TC], scalar1=PI, scalar2=-2 * PI,
                                                   op0=ALU.is_gt, op1=ALU.mult), reads=[dtok], writes=[mskt])
            s.add("dve", lambda e: e.tensor_tensor(out=dst[R, 0:TC], in0=dst[R, 0:TC], in1=msk[R, 0:TC], op=ALU.add),
                  reads=[dtok, mskt], writes=[dtok])
            s.add("dve", lambda e: e.tensor_scalar(out=dst[R, 0:TC], in0=dst[R, 0:TC], scalar1=-PI, scalar2=PI,
                                                   op0=ALU.max, op1=ALU.min), reads=[dtok], writes=[dtok])
            s.add("act", lambda e: e.activation(out=dst[R, 0:TC], in_=dst[R, 0:TC], func=AF.Sin, scale=sc),
                  reads=[dtok, ("glob",)], writes=[dtok])
        self.ffr(i_ang)
        self.ffr(i_msk)

        def latent(src, col0, n, bname, gname, dst, dname):
            buf, wt = self.wload([(lambda b: v8(b, n * 128), src[:, :, col0:col0 + n * 128])])
            wv = v8(buf, n * 128)
            tmps = []
            for i in range(n):
                p = self.proj(lambda k: wv[:, k, i * 128:(i + 1) * 128], 8, hm_rhs, hmt, wt)
                fi, t, tt = self.ft()
                s.add("act", lambda e: e.activation(out=t[:, 0:TC], in_=self.psum[p][:, :], func=AF.Identity,
                                                    bias=self.vcol(l, bname, i)), reads=[("ps", p), vt], writes=[tt])
                tmps.append((fi, t, tt))
            ri, r, rtok = self.rstd_from([t[:, 0:TC] for _, t, _ in tmps], [tt for _, _, tt in tmps], n * 128)
            for i, (fi, t, tt) in enumerate(tmps):
                s.add("dve", lambda e: e.scalar_tensor_tensor(out=dst[:, i, :], in0=t[:, 0:TC], scalar=self.vcol(l, gname, i),
                                                              in1=r[:, 0:TC], op0=ALU.mult, op1=ALU.mult),
                      reads=[tt, rtok, vt], writes=[(dname, i), (("merged", i) if dname == "cqn" else ("cc", i))])
                self.ffr(fi)
            self.ffr(ri)
        latent(winB, 0, 3, "b_cq", "q_norm", self.cqn, "cqn")
        latent(winB, 384, 2, "b_ckv", "kv_norm", self.ckvn, "ckvn")
        cqt = [("cqn", i) for i in range(3)]
        ckvt = [("ckvn", i) for i in range(2)]

        buf, wt = self.wload([(lambda b: v8(b, 192), wkpe[:, :, :])])
        wv = v8(buf, 192)
        pn = self.proj(lambda k: wv[:, k, 0:96], 8, hm_rhs, hmt, wt, M=96)
        psw = self.proj(lambda k: wv[:, k, 96:192], 8, hm_rhs, hmt, wt, M=96)
        i_kn, kn, knt = self.ft()
        i_ks, ks, kst = self.ft()
        s.add("act", lambda e: e.activation(out=kn[R, 0:TC], in_=self.psum[pn][R, :], func=AF.Identity,
                                            bias=self.vecs[l][R, _VC["b_kpe"]:_VC["b_kpe"] + 1]), reads=[("ps", pn), vt], writes=[knt])
        s.add("act", lambda e: e.activation(out=ks[R, 0:TC], in_=self.psum[psw][R, :], func=AF.Identity,
                                            bias=self.vecs[l][R, _VC["b_kpe"] + 1:_VC["b_kpe"] + 2]), reads=[("ps", psw), vt], writes=[kst])
        s.add("dve", lambda e: e.tensor_tensor(out=kn[R, 0:TC], in0=kn[R, 0:TC], in1=cos[R, 0:TC], op=ALU.mult), reads=[knt, cost], writes=[knt])
        s.add("dve", lambda e: e.tensor_tensor(out=ks[R, 0:TC], in0=ks[R, 0:TC], in1=sin[R, 0:TC], op=ALU.mult), reads=[kst, sint], writes=[kst])
        i_kr, kr, krt = self.bt()
        s.add("dve", lambda e: e.tensor_tensor(out=kr[R, :], in0=kn[R, 0:TC], in1=ks[R, 0:TC], op=ALU.add), reads=[knt, kst], writes=[krt])
        s.add("dve", lambda e: e.tensor_copy(out=self.K[R, :, t0:t0 + TC], in_=kr[R, :].unsqueeze(1).broadcast_to([32, 8, TC])),
              reads=[krt], writes=[("Kpe", tg)])
        self.ffr(i_kn)
        self.ffr(i_ks)
        self.bfr(i_kr)

        buf, wt = self.wload([(lambda b: b[:, 0:2048].rearrange("p (k n) -> p k n", k=2), wukv[:, :, :])])
        wv = buf[:, 0:2048].rearrange("p (k n) -> p k n", k=2)
        for h in range(8):
            pk = self.proj(lambda k: wv[:, k, h * 128:h * 128 + 64], 2, lambda k: self.ckvn[:, k, :], ckvt, wt, M=64)
            eng = "act" if h % 2 == 0 else "dve"
            if eng == "act":
                s.add("act", lambda e: e.activation(out=self.K[0:64, h, t0:t0 + TC], in_=self.psum[pk][0:64, :], func=AF.Identity),
                      reads=[("ps", pk)], writes=[("K", h, tg)])
            else:
                s.add("dve", lambda e: e.tensor_copy(out=self.K[0:64, h, t0:t0 + TC], in_=self.psum[pk][0:64, :]),
                      reads=[("ps", pk)], writes=[("K", h, tg)])
        wvv = buf[:, 0:2048].rearrange("p (k h two d) -> p k h two d", k=2, h=8, two=2)
        for blk in range(4):
            gb = tg * 4 + blk
            pb = self.next_psum()
            for k in range(2):
                s.add("pe", lambda e: e.matmul(self.psum[pb][:, :].rearrange("p (h d) -> p h d", h=8),
                                               lhsT=self.ckvn[:, k, blk * 128:(blk + 1) * 128], rhs=wvv[:, k, :, 1, :],
                                               start=(k == 0), stop=(k == 1)), reads=ckvt + wt, writes=[("ps", pb)])
            s.add("act", lambda e: e.activation(out=self.V[:, gb, :, 0:64], in_=self.psum[pb][:, :].rearrange("p (h d) -> p h d", h=8), func=AF.Identity),
                  reads=[("ps", pb)], writes=[("V", gb)])

        cbs = []
        for j in range(4):
            buf, wt = self.wload([(lambda b: v8(b, 256), winC[:, :, j, :])])
            wv = v8(buf, 256)
            pv = self.proj(lambda k: wv[:, k, 0:128], 8, hm_rhs, hmt, wt)
            pg = self.proj(lambda k: wv[:, k, 128:256], 8, hm_rhs, hmt, wt)
            i_s, sg, sgt = self.ft()
            s.add("act", lambda e: e.activation(out=sg[:, 0:TC], in_=self.psum[pg][:, :], func=AF.Sigmoid, bias=self.vcol(l, "b_cg", j)),
                  reads=[("ps", pg), vt], writes=[sgt])
            i_c, cb, cbt = self.ft()
            s.add("dve", lambda e: e.scalar_tensor_tensor(out=cb[:, 30:542], in0=self.psum[pv][:, :], scalar=self.vcol(l, "b_cv", j), in1=sg[:, 0:TC],
                                                          op0=ALU.add, op1=ALU.mult), reads=[("ps", pv), sgt, vt], writes=[cbt])
            s.add("dve", lambda e: e.tensor_copy(out=cb[:, 0:30], in_=self.halo_c[:, j, :]), reads=[("halo_c", j)], writes=[cbt])
            s.add("dve", lambda e: e.tensor_copy(out=self.halo_c[:, j, :], in_=cb[:, 512:542]), reads=[cbt], writes=[("halo_c", j)])
            self.ffr(i_s)
            cbs.append((i_c, cb, cbt))

        def conv_gen():
            for j in range(4):
                i_c, cb, cbt = cbs[j]
                i_o, acc, acct = self.ft()
                dw = lambda tap: self.vcol(l, "dw_w", j * 31 + tap)
                s.add("dve", lambda e: e.tensor_scalar(out=acc[:, 0:TC], in0=cb[:, 0:TC], scalar1=dw(0), scalar2=self.vcol(l, "dw_b", j),
                                                       op0=ALU.mult, op1=ALU.add), reads=[cbt, vt], writes=[acct])
                yield
                for tap in range(1, 31):
                    last = (tap == 30)
                    s.add("dve", lambda e: e.scalar_tensor_tensor(
                        out=(self.cc[:, j, :] if last else acc[:, 0:TC]), in0=cb[:, tap:tap + TC], scalar=dw(tap), in1=acc[:, 0:TC],
                        op0=ALU.mult, op1=ALU.add), reads=[cbt, acct, vt],
                        writes=([("cc", j)] + ([("ckvn", j)] if j < 2 else [])) if last else [acct])
                    yield
                self.ffr(i_c)
                self.ffr(i_o)
        cg = conv_gen()

        def conv_steps(n):
            for _ in range(n):
                try:
                    next(cg)
                except StopIteration:
                    return

        wq_state = {}
        for g in range(2):
            bufq, wtq = self.wload([(lambda b: b[:, 0:1536].rearrange("p (k n) -> p k n", k=3), wq[:, :, g * 512:(g + 1) * 512])])
            wq_state[g] = (bufq[:, 0:1536].rearrange("p (k n) -> p k n", k=3), wtq)

        def prologue(h):
            wvq, wtq = wq_state[h // 4]
            c0 = (h % 4) * 128
            pqn = self.proj(lambda k: wvq[:, k, c0:c0 + 96], 3, lambda k: self.cqn[:, k, :], cqt, wtq, M=96)
            pqs = self.proj(lambda k: wvq[:, k, c0 + 32:c0 + 128], 3, lambda k: self.cqn[:, k, :], cqt, wtq, M=96)
            i_q, Qh, Qt = self.bt()
            i_t1, t1, t1t = self.ft()
            i_t2, t2, t2t = self.ft()
            s.add("act", lambda e: e.activation(out=Qh[0:64, :], in_=self.psum[pqn][0:64, :], func=AF.Identity), reads=[("ps", pqn)], writes=[Qt])
            s.add("dve", lambda e: e.tensor_tensor(out=t1[R, 0:TC], in0=self.psum[pqn][R, :], in1=cos[R, 0:TC], op=ALU.mult),
                  reads=[("ps", pqn), cost], writes=[t1t])
            s.add("dve", lambda e: e.tensor_tensor(out=t2[R, 0:TC], in0=self.psum[pqs][R, :], in1=sin[R, 0:TC], op=ALU.mult),
                  reads=[("ps", pqs), sint], writes=[t2t])
            s.add("dve", lambda e: e.tensor_tensor(out=Qh[R, :], in0=t1[R, 0:TC], in1=t2[R, 0:TC], op=ALU.add), reads=[t1t, t2t, Qt], writes=[Qt])
            self.ffr(i_t1)
            self.ffr(i_t2)
            return i_q, Qh, Qt

        def attention(h, i_q, Qh, Qt):
            pO = 6 + (h % 2)
            nkb = 4 * tg + 4
            live = {}

            def emit_scores(kb):
                kl = kb - 4 * tg
                q0 = max(kl, 0) * 128
                psn = self.next_psum()
                s.add("pe", lambda e: e.matmul(self.psum[psn][:, q0:512], lhsT=self.K[0:96, h, kb * 128:(kb + 1) * 128],
                                               rhs=Qh[0:96, q0:512], start=True, stop=True),
                      reads=[("K", h, kb // 4), ("Kpe", kb // 4), Qt], writes=[("ps", psn)])
                i_p, pT, pTt = self.bt()
                s.add("act", lambda e: e.activation(out=pT[:, q0:512], in_=self.psum[psn][:, q0:512], func=AF.Exp, scale=SCALE),
                      reads=[("ps", psn)], writes=[pTt])
                if kl >= 0:
                    s.add("pool", lambda e: e.tensor_tensor(out=pT[:, q0:q0 + 128], in0=pT[:, q0:q0 + 128], in1=self.tri[:, :], op=ALU.mult),
                          reads=[pTt, ("tri",)], writes=[pTt])
                live[kb] = (i_p, pT, pTt)

            def emit_pv(kb):
                kl = kb - 4 * tg
                i_p, pT, pTt = live.pop(kb)
                for qi in range(max(kl, 0), 4):
                    s.add("pe", lambda e: e.matmul(self.psum[pO][:, qi * 65:(qi + 1) * 65], lhsT=pT[:, qi * 128:(qi + 1) * 128],
                                                   rhs=self.V[:, kb, h, :], start=(kb == 0 and qi == 0), stop=(kb == 4 * tg + qi),
                                                   skip_group_check=True),
                          reads=[pTt, ("V", kb), ("Vones",)], writes=[("ps", pO)])
                self.bfr(i_p)
            LA = 2
            for kb in range(min(LA, nkb)):
                emit_scores(kb)
            for kb in range(nkb):
                if kb + LA < nkb:
                    emit_scores(kb + LA)
                emit_pv(kb)
            self.bfr(i_q)
            s.add("dve", lambda e: e.reciprocal(out=self.rec[:, 0:4], in_=self.psum[pO][:, 0:260].rearrange("p (q d) -> p q d", d=65)[:, :, 64]),
                  reads=[("ps", pO)], writes=[("rec",)])
            for qi in range(4):
                s.add("act", lambda e: e.activation(out=self.O_sb[:, qi, h * 64:(h + 1) * 64], in_=self.psum[pO][:, qi * 65:qi * 65 + 64],
                                                    func=AF.Identity, scale=self.rec[:, qi:qi + 1]),
                      reads=[("ps", pO), ("rec",)], writes=[("O_sb", qi, h)])

        nxt = prologue(0)
        for h in range(8):
            cur = nxt
            if h + 1 < 8:
                nxt = prologue(h + 1)
            attention(h, *cur)
            conv_steps(16)
        conv_steps(1000)
        self.ffr(i_cos)
        self.ffr(i_sin)

        p1 = self.next_psum()
        p2 = self.next_psum()
        for j in range(4):
            s.add("pe", lambda e: e.matmul(self.psum[p1][:, :], lhsT=self.ones_bf[:, :], rhs=self.cc[:, j, :],
                                           start=(j == 0), stop=(j == 3)), reads=[("cc", j), ("ones_bf",)], writes=[("ps", p1)])
        for j in range(4):
            ib, sq, sqt = self.bt()
            s.add("act", lambda e: e.activation(out=sq[:, :], in_=self.cc[:, j, :], func=AF.Square), reads=[("cc", j)], writes=[sqt])
            s.add("pe", lambda e: e.matmul(self.psum[p2][:, :], lhsT=self.ones_bf[:, :], rhs=sq[:, :],
                                           start=(j == 0), stop=(j == 3)), reads=[sqt, ("ones_bf",)], writes=[("ps", p2)])
            self.bfr(ib)
        i_m, mean, meant = self.ft()
        i_v, var, vart = self.ft()
        s.add("act", lambda e: e.activation(out=mean[:, 0:TC], in_=self.psum[p1][:, :], func=AF.Identity, scale=1.0 / 512),
              reads=[("ps", p1)], writes=[meant])
        s.add("act", lambda e: e.activation(out=var[:, 0:TC], in_=mean[:, 0:TC], func=AF.Square), reads=[meant], writes=[vart])
        s.add("dve", lambda e: e.scalar_tensor_tensor(out=var[:, 0:TC], in0=self.psum[p2][:, :], scalar=1.0 / 512, in1=var[:, 0:TC],
                                                      op0=ALU.mult, op1=ALU.subtract), reads=[("ps", p2), vart], writes=[vart])
        s.add("act", lambda e: e.activation(out=var[:, 0:TC], in_=var[:, 0:TC], func=AF.Sqrt, bias=self.epsb[:, 0:1]),
              reads=[vart, ("epsb",)], writes=[vart])
        s.add("dve", lambda e: e.reciprocal(out=var[:, 0:TC], in_=var[:, 0:TC]), reads=[vart], writes=[vart])
        for j in range(4):
            i_t, tt, ttt = self.ft()
            s.add("dve", lambda e: e.tensor_tensor(out=tt[:, 0:TC], in0=self.cc[:, j, :], in1=mean[:, 0:TC], op=ALU.subtract),
                  reads=[("cc", j), meant], writes=[ttt])
            s.add("dve", lambda e: e.tensor_tensor(out=tt[:, 0:TC], in0=tt[:, 0:TC], in1=var[:, 0:TC], op=ALU.mult),
                  reads=[ttt, vart], writes=[ttt])
            s.add("act", lambda e: e.activation(out=self.cs[:, j, :], in_=tt[:, 0:TC], func=AF.Silu,
                                                scale=self.vcol(l, "ln_g", j), bias=self.vcol(l, "ln_b", j)),
                  reads=[ttt, vt], writes=[("cs", j)])
            self.ffr(i_t)
        self.ffr(i_m)
        self.ffr(i_v)

        for qi in range(4):
            pb = self.next_psum()
            pbf = self.psum[pb][:, :].bitcast(BF16)
            for fc in range(4):
                s.add("pe", lambda e: e.transpose(out=pbf[:, fc * 128:(fc + 1) * 128], in_=self.O_sb[:, qi, fc * 128:(fc + 1) * 128],
                                                  identity=self.ident[:, :]),
                      reads=[("O_sb", qi, hh) for hh in range(8)] + [("ident",)], writes=[("ps", pb)])
            s.add("dve", lambda e: e.tensor_copy(out=self.OT[:, :, qi * 128:(qi + 1) * 128],
                                                 in_=pbf[:, 0:512].rearrange("p (k t) -> p k t", k=4)),
                  reads=[("ps", pb)], writes=[("OT", qi)])

        for o in range(8):
            bufG, wtG = self.wload([(lambda b: v8(b, 384), winG[:, :, o, :])])
            wvG = v8(bufG, 384)
            bufR, wtR = self.wload([(lambda b: b[:, 0:1536].rearrange("p (k n) -> p k n", k=4), wbr[:, :, o, :])])
            wvR = bufR[:, 0:1536].rearrange("p (k n) -> p k n", k=4)
            pg = [self.proj(lambda k: wvG[:, k, br * 128:(br + 1) * 128], 8, hm_rhs, hmt, wtG) for br in range(3)]
            pya = self.proj(lambda k: wvR[:, k, 0:128], 4, lambda k: self.m_a[:, k, :], [("m_a", j) for j in range(4)], wtR)
            pyb = self.proj(lambda k: wvR[:, k, 128:256], 4, lambda k: self.OT[:, k, :], [("OT", q) for q in range(4)], wtR)
            pyc = self.proj(lambda k: wvR[:, k, 256:384], 4, lambda k: self.cs[:, k, :], [("cs", j) for j in range(4)], wtR)
            gs = []
            for br in range(3):
                fi, gt_, gtok_ = self.ft()
                s.add("act", lambda e: e.activation(out=gt_[:, 0:TC], in_=self.psum[pg[br]][:, :], func=AF.Sigmoid,
                                                    bias=self.vcol(l, "b_G", br * 8 + o)), reads=[("ps", pg[br]), vt], writes=[gtok_])
                gs.append((fi, gt_, gtok_))
            (f0, g0, g0t), (f1, g1, g1t), (f2, g2, g2t) = gs
            s.add("dve", lambda e: e.tensor_tensor(out=g0[:, 0:TC], in0=g0[:, 0:TC], in1=self.psum[pya][:, :], op=ALU.mult),
                  reads=[g0t, ("ps", pya)], writes=[g0t])
            s.add("dve", lambda e: e.tensor_tensor(out=g1[:, 0:TC], in0=g1[:, 0:TC], in1=self.psum[pyb][:, :], op=ALU.mult),
                  reads=[g1t, ("ps", pyb)], writes=[g1t])
            s.add("dve", lambda e: e.scalar_tensor_tensor(out=g2[:, 0:TC], in0=self.psum[pyc][:, :], scalar=self.vcol(l, "cb_out", o),
                                                          in1=g2[:, 0:TC], op0=ALU.add, op1=ALU.mult), reads=[g2t, ("ps", pyc), vt], writes=[g2t])
            s.add("dve", lambda e: e.tensor_tensor(out=g0[:, 0:TC], in0=g0[:, 0:TC], in1=g1[:, 0:TC], op=ALU.add), reads=[g0t, g1t], writes=[g0t])
            s.add("dve", lambda e: e.tensor_tensor(out=self.merged[:, o, :], in0=g0[:, 0:TC], in1=g2[:, 0:TC], op=ALU.add),
                  reads=[g0t, g2t], writes=[("merged", o)] + ([("cqn", o)] if o < 3 else []))
            self.ffr(f0)
            self.ffr(f1)
            self.ffr(f2)
        mt = [("merged", o) for o in range(8)]
        for op_ in range(4):
            buf, wt = self.wload([(lambda b: v8(b, 256), wout[:, :, op_ * 256:(op_ + 1) * 256])])
            wv = v8(buf, 256)
            for hf in range(2):
                o2 = op_ * 2 + hf
                py = self.proj(lambda k: wv[:, k, hf * 128:(hf + 1) * 128], 8, lambda k: self.merged[:, k, :], mt, wt)
                s.add("dve", lambda e: e.tensor_tensor(out=self.x[:, o2, t0:t0 + TC], in0=self.psum[py][:, :], in1=self.x[:, o2, t0:t0 + TC], op=ALU.add),
                      reads=[("ps", py), ("x", o2, tg)], writes=[("x", o2, tg)])


_WNAMES = ["w1r_a", "w2_a", "winA", "winC", "winB", "wkpe", "winG", "wbr", "wq", "wukv", "wout", "gwd", "w1r_b", "w2_b"]
_WSHAPES = {
    "w1r_a": [D, NFC, 256], "w2_a": [DFF, D], "w1r_b": [D, NFC, 256], "w2_b": [DFF, D],
    "winA": [D, 4, 256], "winC": [D, 4, 256], "winB": [D, 640], "wkpe": [D, 192], "winG": [D, 8, 384],
    "wbr": [512, 8, 384], "wq": [384, 1024], "wukv": [256, 1024], "wout": [D, D], "gwd": [128, 1024],
}


def build_program(S=2048, L=DEPTH, parts=("ffn1", "mixer", "ffn2"), final=True):
    b = Builder(S, L)
    nc = b.nc
    xT = b.din("xT", [D, S])
    outT = nc.dram_tensor("outT", [D, S], F32, kind="ExternalOutput")
    W = []
    for l in range(L):
        W.append({n: b.din("%s_%d" % (n, l), _WSHAPES[n]) for n in _WNAMES})
    b.setup()
    b.load_consts()
    b.load_x(xT)
    for l in range(L):
        if "ffn1" in parts:
            b.ffn(l, "ffn1_norm", W[l]["w1r_a"], W[l]["w2_a"])
        if "mixer" in parts:
            b.s.barrier()
            b.mixer_layer_setup(l, W[l]["gwd"])
            for tg in range(b.NTC):
                b.mixer_chunk(l, tg, W[l])
            b.s.barrier()
        if "ffn2" in parts:
            b.ffn(l, "ffn2_norm", W[l]["w1r_b"], W[l]["w2_b"])
    if final:
        b.final_norm()
    b.store_x(outT)
    b.s.emit(nc)
    return nc, b


def _fm(v):
    return np.ascontiguousarray(v.reshape(-1, 128).T)


def host_layout(inp, l):
    f32 = np.float32
    w = {}
    for tag, nm in (("a", "ffn1"), ("b", "ffn2")):
        w1 = inp[nm + "_w1"][l]
        w["w1r_" + tag] = np.ascontiguousarray(
            np.concatenate([w1[:, :DFF].reshape(D, NFC, 128), w1[:, DFF:].reshape(D, NFC, 128)], axis=2))
        w["w2_" + tag] = np.ascontiguousarray(inp[nm + "_w2"][l])
    win = inp["w_in"][l]
    w["winA"] = np.ascontiguousarray(np.concatenate([win[:, 0:512].reshape(D, 4, 128), win[:, 512:1024].reshape(D, 4, 128)], axis=2))
    w["winB"] = np.ascontiguousarray(win[:, 1024:1664])
    kp = win[:, 1664:1696]
    wk = np.zeros((D, 192), f32)
    wk[:, 64:96] = kp
    wk[:, 160:176] = kp[:, 16:32]
    wk[:, 176:192] = kp[:, 0:16]
    w["wkpe"] = wk
    w["winC"] = np.ascontiguousarray(np.concatenate([win[:, 1696:2208].reshape(D, 4, 128), win[:, 2208:2720].reshape(D, 4, 128)], axis=2))
    G = win[:, 2720:].reshape(D, 3, 8, 128)
    w["winG"] = np.ascontiguousarray(G.transpose(0, 2, 1, 3).reshape(D, 8, 384))
    br = np.stack([inp["lru_w_out"][l].reshape(512, 8, 128), inp["mla_w_o"][l].reshape(512, 8, 128),
                   inp["conv_w_out"][l].reshape(512, 8, 128)], axis=2)
    w["wbr"] = np.ascontiguousarray(br.reshape(512, 8, 384))
    uq = inp["w_uq"][l].reshape(384, 8, 96)
    w["wq"] = np.ascontiguousarray(np.concatenate([uq, uq[:, :, 80:96], uq[:, :, 64:80]], axis=2).reshape(384, 1024))
    w["wukv"] = np.ascontiguousarray(inp["w_ukv"][l])
    w["wout"] = np.ascontiguousarray(inp["w_out"][l])
    wg = inp["lru_w_gate"][l]
    gwd = np.zeros((128, 8, 128), f32)
    for j in range(4):
        for e in range(2):
            hd = 2 * j + e
            gwd[e * 64:(e + 1) * 64, 2 * j, e * 64:(e + 1) * 64] = wg[hd][:, 0:64]
            gwd[e * 64:(e + 1) * 64, 2 * j + 1, e * 64:(e + 1) * 64] = wg[hd][:, 64:128]
    w["gwd"] = gwd.reshape(128, 1024)
    v = np.zeros((128, NVEC), f32)
    def put(name, arr):
        c = _VC[name]
        v[:, c:c + arr.shape[1]] = arr
    put("ffn1_norm", _fm(inp["ffn1_norm"][l])); put("mix_norm", _fm(inp["mix_norm"][l])); put("ffn2_norm", _fm(inp["ffn2_norm"][l]))
    bi = inp["b_in"][l]
    put("b_xa", _fm(bi[0:512])); put("b_ga", _fm(bi[512:1024])); put("b_cq", _fm(bi[1024:1408])); put("b_ckv", _fm(bi[1408:1664]))
    bk = np.zeros((128, 2), f32)
    bk[64:96, 0] = bi[1664:1696]
    bk[64:80, 1] = bi[1680:1696]
    bk[80:96, 1] = bi[1664:1680]
    put("b_kpe", bk)
    put("b_cv", _fm(bi[1696:2208])); put("b_cg", _fm(bi[2208:2720])); put("b_G", _fm(bi[2720:]))
    cw = inp["lru_conv_w"][l]
    put("lru_cw", np.ascontiguousarray(cw.reshape(4, 4, 128).transpose(2, 1, 0).reshape(128, 16)))
    put("lru_cb", _fm(inp["lru_conv_b"][l]))
    bg = inp["lru_b_gate"][l].reshape(4, 2, 2, 64)
    put("b_r", np.ascontiguousarray(bg[:, :, 0, :].transpose(1, 2, 0).reshape(128, 4)))
    put("b_i", np.ascontiguousarray(bg[:, :, 1, :].transpose(1, 2, 0).reshape(128, 4)))
    put("lam", _fm(inp["lru_lambda"][l])); put("q_norm", _fm(inp["q_norm"][l])); put("kv_norm", _fm(inp["kv_norm"][l]))
    dw = inp["conv_dw_w"][l]
    put("dw_w", np.ascontiguousarray(dw.reshape(31, 4, 128).transpose(2, 1, 0).reshape(128, 124)))
    put("dw_b", _fm(inp["conv_dw_b"][l])); put("ln_g", _fm(inp["conv_ln_g"][l])); put("ln_b", _fm(inp["conv_ln_b"][l]))
    put("cb_out", _fm(inp["conv_b_out"][l]))
    return w, v


def host_consts(inp, S):
    f32 = np.float32
    c = {}
    c["c_ident"] = np.eye(128, dtype=f32)
    c["c_tri"] = np.triu(np.ones((128, 128), f32))
    g = np.zeros((128, NGL), f32)
    g[:, 0:8] = _fm(inp["final_norm"])
    inv = (10000.0 ** (-np.arange(0, 32, 2, dtype=np.float32) / 32)).astype(f32)
    g[64:96, 8] = np.concatenate([inv, inv])
    g[64:80, 9] = -1.0
    g[80:96, 9] = 1.0
    c["c_glob"] = g
    return c


def make_in_maps(inp, S=2048, L=DEPTH, cores=range(8)):
    shared = {}
    for l in range(L):
        w, v = host_layout(inp, l)
        for n in _WNAMES:
            shared["%s_%d" % (n, l)] = w[n]
        shared["vecs%d" % l] = v
    shared.update(host_consts(inp, S))
    maps = []
    for b in cores:
        m = dict(shared)
        m["xT"] = np.ascontiguousarray(inp["x"][b, :S].T)
        m["pos_rep"] = np.ascontiguousarray(np.broadcast_to(inp["positions"][b, :S][None, :], (96, S))).astype(np.int32)
        maps.append(m)
    return maps


_CACHE = {}


def kernel(**inputs):
    inp = {k: np.asarray(v) for k, v in inputs.items()}
    if "prog" not in _CACHE:
        _CACHE["prog"] = build_program()[0]
    nc = _CACHE["prog"]
    maps = make_in_maps(inp)
    res = run_bass_kernel_spmd(nc, maps, core_ids=list(range(8)))
    out = np.stack([np.ascontiguousarray(r["outT"].T) for r in res.results], axis=0)
    return out.astype(np.float32)
```

```python
import numpy as np
import concourse.bass as bass
import concourse.mybir as mybir
from concourse.bass_utils import run_bass_kernel_spmd

F32 = mybir.dt.float32
BF16 = mybir.dt.bfloat16
I32 = mybir.dt.int32
AF = mybir.ActivationFunctionType
ALU = mybir.AluOpType
AX = mybir.AxisListType

D = 1024
S = 2048
DFF = 2816
DEPTH = 2
EPS = 1e-6
NKC = D // 128
NFC = DFF // 128
TC = 512
NTC = S // TC


class _Op:
    __slots__ = ("eng", "fn", "idx", "waits", "signal", "sem", "val", "dma", "semkey")

    def __init__(self, eng, fn, dma, semkey):
        self.eng = eng
        self.fn = fn
        self.dma = dma
        self.semkey = semkey
        self.waits = []
        self.signal = dma
        self.sem = None
        self.val = 0
        self.idx = 0


class _Rec:
    def __getattr__(self, name):
        return lambda *a, **k: (name, a, k)


_REC = _Rec()


class Sched:
    ENGS = ("pe", "act", "dve", "pool", "sp")

    def __init__(self):
        self.ops = {e: [] for e in self.ENGS}
        self.last_w = {}
        self.readers = {}
        self.waited = {e: {} for e in self.ENGS}
        self.dma_seq = {}
        self.all_dma = []

    def add(self, eng, fn, reads=(), writes=(), dma=False, semkey=None, extra=()):
        if fn is not None:
            name_, a_, k_ = fn(_REC)
            fn = (lambda e, name_=name_, a_=a_, k_=k_: getattr(e, name_)(*a_, **k_))
        op = _Op(eng, fn, dma, semkey)
        op.idx = len(self.ops[eng])
        if dma:
            assert semkey is not None
        deps = list(extra)
        for r in reads:
            w = self.last_w.get(r)
            if w is not None:
                deps.append(w)
        for wtok in writes:
            w = self.last_w.get(wtok)
            if w is not None:
                deps.append(w)
            deps.extend(self.readers.get(wtok, ()))
        for p in deps:
            if p is op:
                continue
            if p.dma:
                key = ("dma", p.semkey)
                pos = p.val
            else:
                if p.eng == "pe" and eng == "pe":
                    continue
                key = p.eng
                pos = p.idx + 1
            if self.waited[eng].get(key, 0) >= pos:
                continue
            self.waited[eng][key] = pos
            p.signal = True
            op.waits.append(p)
        if dma:
            n = self.dma_seq.get(semkey, 0) + 1
            self.dma_seq[semkey] = n
            op.val = n
            self.all_dma.append(op)
        for r in reads:
            self.readers.setdefault(r, []).append(op)
        for wtok in writes:
            self.last_w[wtok] = op
            self.readers[wtok] = []
        self.ops[eng].append(op)
        return op

    def barrier(self):
        comp = ("pe", "act", "dve")
        last = {}
        for e in comp:
            for op in reversed(self.ops[e]):
                if not op.dma and op.fn is not None:
                    last[e] = op
                    break
        for e in comp:
            self.add(e, None, extra=[last[f] for f in comp if f != e and f in last])

    def emit(self, nc, final_wait_eng="sp"):
        from contextlib import ExitStack
        with ExitStack() as es:
            esem = {e: es.enter_context(nc.semaphore("s_" + e)) for e in self.ENGS}
            dsem = {}
            for k in self.dma_seq:
                dsem[k] = es.enter_context(nc.semaphore("d_%s" % (len(dsem),)))
            for e in self.ENGS:
                c = 0
                for op in self.ops[e]:
                    if op.dma:
                        op.sem = dsem[op.semkey]
                        op.val = op.val * 16
                    elif op.signal:
                        assert op.fn is not None
                        c += 1
                        op.sem = esem[e]
                        op.val = c
            block = es.enter_context(nc.Block())

            def run(e, engobj, extra_final=False):
                for op in self.ops[e]:
                    for p in op.waits:
                        engobj.wait_ge(p.sem, p.val)
                    if op.fn is None:
                        continue
                    ins = op.fn(engobj)
                    if op.dma:
                        ins.then_inc(op.sem, 16)
                    elif op.signal:
                        ins.then_inc(op.sem, 1)
                if extra_final:
                    for k, n in self.dma_seq.items():
                        engobj.wait_ge(dsem[k], 16 * n)

            @block.tensor
            def _(eng):
                run("pe", eng)

            @block.scalar
            def _(eng):
                run("act", eng)

            @block.vector
            def _(eng):
                run("dve", eng)

            @block.gpsimd
            def _(eng):
                run("pool", eng)

            @block.sync
            def _(eng):
                run("sp", eng, extra_final=True)


_VC = {}
def _vc_build():
    c = 0
    def put(name, n):
        nonlocal c
        _VC[name] = c
        c += n
    put("ffn1_norm", 8); put("mix_norm", 8); put("ffn2_norm", 8)
    put("b_xa", 4); put("b_ga", 4); put("b_cq", 3); put("b_ckv", 2); put("b_kpe", 2)
    put("b_cv", 4); put("b_cg", 4); put("b_G", 24)
    put("lru_cw", 16); put("lru_cb", 4); put("b_r", 4); put("b_i", 4); put("lam", 4)
    put("q_norm", 3); put("kv_norm", 2); put("dw_w", 124); put("dw_b", 4)
    put("ln_g", 4); put("ln_b", 4); put("cb_out", 8)
    return c
NVEC = _vc_build()
NFT = 9
NGL = 16

SCALE = 96.0 ** -0.5
GELU_K = 1.5957691216057308
PI = 3.141592653589793


class Builder:
    def __init__(self, S=2048, L=DEPTH):
        self.nc = bass.Bass("TRN2", target_bir_lowering=False)
        self.s = Sched()
        self.S = S
        self.L = L
        self.NTC = S // TC
        self.dram_in = {}
        self.psum_rr = 0
        self.psum_n = 6
        self.wp_rr = 0
        self.wp_pinned = set()

    def din(self, name, shape, dt=F32):
        t = self.nc.dram_tensor(name, list(shape), dt, kind="ExternalInput")
        self.dram_in[name] = t
        return t

    def sb(self, name, shape, dt):
        return self.nc.alloc_sbuf_tensor(name, list(shape), dt)

    def setup(self):
        nc, s, S = self.nc, self.s, self.S
        self.x = self.sb("x", [128, NKC, S], F32)
        self.psum = [nc.alloc_psum_tensor("ps%d" % i, [128, 512], F32) for i in range(8)]
        self.wp = [self.sb("wp%d" % i, [128, 3072], BF16) for i in range(4)]
        self.ftmp = [self.sb("ft%d" % i, [128, 544], F32) for i in range(NFT)]
        self.btmp = [self.sb("bt%d" % i, [128, 512], BF16) for i in range(6)]
        self.ffree = list(range(NFT))
        self.bfree = list(range(6))
        NB = S // 128
        mix_elems = 8 * S + NB * 8 * 65 + 4096 + 5 * 2048 + 4096 + 4 * 544 + 64
        ffn_elems = 8 * 1024 + 22 * 1024
        self.arena = self.sb("arena", [128, max(mix_elems, ffn_elems)], BF16)
        a = self.arena
        self.h = a[:, 0:8192].rearrange("p (k t) -> p k t", k=8)
        self.mid = a[:, 8192:8192 + 22528].rearrange("p (f t) -> p f t", f=22)
        o = 0
        def carve(n):
            nonlocal o
            v = a[:, o:o + n]
            o += n
            return v
        self.K = carve(8 * S).rearrange("p (h t) -> p h t", h=8)
        self.V = carve(NB * 8 * 65).rearrange("p (b h d) -> p b h d", b=NB, h=8)
        self.hm = carve(4096).rearrange("p (k t) -> p k t", k=8)
        self.m_a = carve(2048).rearrange("p (k t) -> p k t", k=4)
        self.cs = carve(2048).rearrange("p (k t) -> p k t", k=4)
        self.O_sb = carve(2048).rearrange("p (q f) -> p q f", q=4)
        self.OT = carve(2048).rearrange("p (k t) -> p k t", k=4)
        ccv = carve(2048)
        self.cc = ccv.rearrange("p (k t) -> p k t", k=4)
        self.ckvn = ccv[:, 0:1024].rearrange("p (k t) -> p k t", k=2)
        self.cbb = carve(4 * 544).rearrange("p (k t) -> p k t", k=4)
        mgv = carve(4096)
        self.merged = mgv.rearrange("p (k t) -> p k t", k=8)
        self.cqn = mgv[:, 0:1536].rearrange("p (k t) -> p k t", k=3)
        self.ones_bf = self.sb("ones_bf", [128, 128], BF16)
        self.ident = self.sb("ident", [128, 128], BF16)
        self.tri = self.sb("tri", [128, 128], BF16)
        self.epsb = self.sb("epsb", [128, 1], F32)
        self.oneb = self.sb("oneb", [128, 1], F32)
        self.vecs = [self.sb("svecs%d" % l, [128, NVEC], F32) for l in range(self.L)]
        self.glob = self.sb("glob", [128, NGL], F32)
        self.gw = self.sb("gw", [128, 8, 128], BF16)
        self.halo_a = self.sb("halo_a", [128, 4, 3], F32)
        self.state = self.sb("state", [128, 4], F32)
        self.nsp = self.sb("nsp", [128, 8], F32)
        self.rec = self.sb("rec", [128, 4], F32)
        s.add("pool", lambda e: e.memset(self.ones_bf[:], 1.0), writes=[("ones_bf",)])
        s.add("pool", lambda e: e.memset(self.oneb[:], 1.0), writes=[("oneb",)])
        s.add("pool", lambda e: e.memset(self.epsb[:], EPS), writes=[("epsb",)])

    def load_consts(self):
        s = self.s
        dr_ident = self.din("c_ident", [128, 128])
        dr_tri = self.din("c_tri", [128, 128])
        dr_glob = self.din("c_glob", [128, NGL])
        self.dr_pos = self.din("pos_rep", [96, self.S], I32)
        s.add("pool", lambda e: e.dma_start(out=self.ident[:, :], in_=dr_ident[:, :]),
              writes=[("ident",)], dma=True, semkey=("c", 0))
        s.add("pool", lambda e: e.dma_start(out=self.tri[:, :], in_=dr_tri[:, :]),
              writes=[("tri",)], dma=True, semkey=("c", 1))
        s.add("sp", lambda e: e.dma_start(out=self.glob[:, :], in_=dr_glob[:, :]),
              writes=[("glob",)], dma=True, semkey=("c", 2))
        self.dr_vecs = []
        for l in range(self.L):
            dv = self.din("vecs%d" % l, [128, NVEC])
            self.dr_vecs.append(dv)
            s.add("sp", lambda e, l=l, dv=dv: e.dma_start(out=self.vecs[l][:, :], in_=dv[:, :]),
                  writes=[("vecs", l)], dma=True, semkey=("vecs", l))

    def vcol(self, l, name, j=0, n=1):
        c = _VC[name] + j
        return self.vecs[l][:, c:c + n]

    def ft(self):
        i = self.ffree.pop(0)
        return i, self.ftmp[i], ("ft", i)

    def ffr(self, i):
        self.ffree.append(i)

    def bt(self):
        i = self.bfree.pop(0)
        return i, self.btmp[i], ("bt", i)

    def bfr(self, i):
        self.bfree.append(i)

    def next_psum(self):
        i = self.psum_rr % self.psum_n
        self.psum_rr = (i + 1) % self.psum_n
        return i

    def wload(self, dmas):
        s = self.s
        b = self.wp_rr
        while b in self.wp_pinned:
            b = (b + 1) % 4
        self.wp_rr = (b + 1) % 4
        self.wp_last = b
        buf = self.wp[b]
        for n, (dstf, src) in enumerate(dmas):
            s.add("pool", lambda e, dstf=dstf, src=src, buf=buf: e.dma_start(out=dstf(buf), in_=src),
                  writes=[("wp", b, n)], dma=True, semkey=("wp", b, n))
        return buf, [("wp", b, n) for n in range(len(dmas))]

    def load_x(self, xT):
        s = self.s
        for c in range(NKC):
            s.add("sp", lambda e, c=c: e.dma_start(out=self.x[:, c, :], in_=xT[c * 128:(c + 1) * 128, :]),
                  writes=[("x", c, t) for t in range(self.NTC)], dma=True, semkey=("xload", c))

    def store_x(self, outT):
        s = self.s
        for c in range(NKC):
            s.add("sp", lambda e, c=c: e.dma_start(out=outT[c * 128:(c + 1) * 128, :], in_=self.x[:, c, :]),
                  reads=[("x", c, t) for t in range(self.NTC)], dma=True, semkey=("xstore",))

    def rstd_from(self, srcs, src_toks, n_feat):
        s = self.s
        pb = self.next_psum()
        ps = self.psum[pb]
        nk = len(srcs)
        for k, (ap, tok) in enumerate(zip(srcs, src_toks)):
            bi, bq, btok = self.bt()
            s.add("act", lambda e, ap=ap, bq=bq: e.activation(out=bq[:, :], in_=ap, func=AF.Square),
                  reads=[tok], writes=[btok])
            s.add("pe", lambda e, k=k, bq=bq: e.matmul(ps[:, :], lhsT=self.ones_bf[:, :], rhs=bq[:, :],
                                                        start=(k == 0), stop=(k == nk - 1)),
                  reads=[btok, ("ones_bf",)], writes=[("ps", pb)])
            self.bfr(bi)
        fi, r, rtok = self.ft()
        s.add("act", lambda e: e.activation(out=r[:, 0:TC], in_=ps[:, :], func=AF.Sqrt,
                                            bias=self.epsb[:, 0:1], scale=1.0 / n_feat),
              reads=[("ps", pb), ("epsb",)], writes=[rtok])
        s.add("dve", lambda e: e.reciprocal(out=r[:, 0:TC], in_=r[:, 0:TC]), reads=[rtok], writes=[rtok])
        return fi, r, rtok

    def rmsnorm_chunk(self, tg, gain, gtok, h_out, hcol0, htoks):
        s = self.s
        t0 = tg * TC
        fi, r, rtok = self.rstd_from([self.x[:, k, t0:t0 + TC] for k in range(NKC)],
                                     [("x", k, tg) for k in range(NKC)], D)
        for k in range(NKC):
            s.add("dve", lambda e, k=k: e.scalar_tensor_tensor(
                out=h_out[:, k, hcol0:hcol0 + TC], in0=self.x[:, k, t0:t0 + TC], scalar=gain[:, k:k + 1],
                in1=r[:, 0:TC], op0=ALU.mult, op1=ALU.mult),
                reads=[("x", k, tg), rtok, gtok], writes=[htoks[k]])
        self.ffr(fi)

    def ffn(self, l, which, w1r, w2):
        s = self.s
        TH = min(1024, self.S)
        nloc = TH // TC
        gain = self.vcol(l, which, 0, 8)
        gtok = ("vecs", l)
        w1v = w1r.rearrange("(kc p) f n -> p kc f n", p=128)
        w2v = w2.rearrange("(fc p) n -> p fc n", p=128)
        for half in range(self.S // TH):
            for tcl in range(nloc):
                tg = half * nloc + tcl
                self.rmsnorm_chunk(tg, gain, gtok, self.h, tcl * TC, [("h", tcl)] * 8)
            for f in range(NFC):
                buf, wt = self.wload([(lambda b: b[:, 0:2048].rearrange("p (k n) -> p k n", k=8), w1v[:, :, f, :])])
                wv = buf[:, 0:2048].rearrange("p (k n) -> p k n", k=8)
                for tcl in range(nloc):
                    pg = self.next_psum()
                    pu = self.next_psum()
                    for k in range(NKC):
                        s.add("pe", lambda e, k=k, wv=wv, pg=pg, tcl=tcl: e.matmul(
                            self.psum[pg][:, :], lhsT=wv[:, k, 0:128], rhs=self.h[:, k, tcl * TC:(tcl + 1) * TC],
                            start=(k == 0), stop=(k == NKC - 1)),
                            reads=wt + [("h", tcl)], writes=[("ps", pg)])
                    for k in range(NKC):
                        s.add("pe", lambda e, k=k, wv=wv, pu=pu, tcl=tcl: e.matmul(
                            self.psum[pu][:, :], lhsT=wv[:, k, 128:256], rhs=self.h[:, k, tcl * TC:(tcl + 1) * TC],
                            start=(k == 0), stop=(k == NKC - 1)),
                            reads=wt + [("h", tcl)], writes=[("ps", pu)])
                    fi, sg, sgt = self.ft()
                    s.add("act", lambda e, sg=sg, pg=pg: e.activation(out=sg[:, 0:TC], in_=self.psum[pg][:, :], func=AF.Silu),
                          reads=[("ps", pg)], writes=[sgt])
                    s.add("dve", lambda e, sg=sg, pu=pu, f=f, tcl=tcl: e.tensor_tensor(
                        out=self.mid[:, f, tcl * TC:(tcl + 1) * TC], in0=sg[:, 0:TC], in1=self.psum[pu][:, :], op=ALU.mult),
                        reads=[sgt, ("ps", pu)], writes=[("mid", f, tcl)])
                    self.ffr(fi)
            for o in range(NKC):
                buf, wt = self.wload([(lambda b: b[:, 0:2816].rearrange("p (f n) -> p f n", f=22), w2v[:, :, o * 128:(o + 1) * 128])])
                wv = buf[:, 0:2816].rearrange("p (f n) -> p f n", f=22)
                for tcl in range(nloc):
                    tg = half * nloc + tcl
                    py = self.next_psum()
                    for f in range(NFC):
                        s.add("pe", lambda e, f=f, wv=wv, py=py, tcl=tcl: e.matmul(
                            self.psum[py][:, :], lhsT=wv[:, f, :], rhs=self.mid[:, f, tcl * TC:(tcl + 1) * TC],
                            start=(f == 0), stop=(f == NFC - 1)),
                            reads=wt + [("mid", f, tcl)], writes=[("ps", py)])
                    s.add("dve", lambda e, o=o, py=py, tg=tg: e.scalar_tensor_tensor(
                        out=self.x[:, o, tg * TC:(tg + 1) * TC], in0=self.psum[py][:, :], scalar=0.5,
                        in1=self.x[:, o, tg * TC:(tg + 1) * TC], op0=ALU.mult, op1=ALU.add),
                        reads=[("ps", py), ("x", o, tg)], writes=[("x", o, tg)])

    def final_norm(self):
        for tg in range(self.NTC):
            self.rmsnorm_chunk(tg, self.glob[:, 0:8], ("glob",), self.x, tg * TC, [("x", k, tg) for k in range(NKC)])

    def proj(self, lhs_fn, nk, rhs_fn, rtoks, wtoks, M=128, N=TC, pb=None):
        s = self.s
        if pb is None:
            pb = self.next_psum()
        for k in range(nk):
            lt = lhs_fn(k)
            rt = rhs_fn(k)
            s.add("pe", lambda e: e.matmul(self.psum[pb][0:M, 0:N], lhsT=lt, rhs=rt,
                                           start=(k == 0), stop=(k == nk - 1)),
                  reads=list(wtoks) + list(rtoks), writes=[("ps", pb)])
        return pb

    def mixer_layer_setup(self, l, gwd):
        s = self.s
        s.add("pool", lambda e: e.dma_start(out=self.gw[:, :, :], in_=gwd.rearrange("p (a b) -> p a b", a=8)),
              writes=[("gw",)], dma=True, semkey=("gw",))
        lam = self.vcol(l, "lam", 0, 4)
        s.add("act", lambda e: e.activation(out=self.nsp[:, 0:4], in_=lam, func=AF.Exp, scale=-1.0),
              reads=[("vecs", l)], writes=[("nsp",)])
        s.add("act", lambda e: e.activation(out=self.nsp[:, 0:4], in_=self.nsp[:, 0:4], func=AF.Ln, bias=1.0),
              reads=[("nsp",)], writes=[("nsp",)])
        s.add("dve", lambda e: e.tensor_scalar(out=self.nsp[:, 4:8], in0=self.nsp[:, 0:4], scalar1=-16.0, scalar2=None, op0=ALU.mult),
              reads=[("nsp",)], writes=[("nsp2",)])
        s.add("dve", lambda e: e.tensor_scalar(out=self.nsp[:, 0:4], in0=self.nsp[:, 0:4], scalar1=-8.0, scalar2=None, op0=ALU.mult),
              reads=[("nsp",), ("nsp2",)], writes=[("nsp",)])
        s.add("dve", lambda e: e.memset(self.halo_a[:], 0.0), writes=[("halo_a", j) for j in range(4)])
        s.add("dve", lambda e: e.memset(self.cbb[:, :, :], 0.0), writes=[("cbb", j) for j in range(4)])
        s.add("dve", lambda e: e.memset(self.state[:], 0.0), writes=[("state", j) for j in range(4)])
        s.add("dve", lambda e: e.memset(self.V[:, :, :, 64:65], 1.0), writes=[("Vones",)])

    def mixer_chunk(self, l, tg, W):
        s = self.s
        t0 = tg * TC
        vt = ("vecs", l)
        hm = self.hm
        hmt = [("hm", k) for k in range(8)]
        self.rmsnorm_chunk(tg, self.vcol(l, "mix_norm", 0, 8), vt, hm, 0, hmt)
        hm_rhs = lambda k: hm[:, k, :]
        v8 = lambda b, n: b[:, 0:8 * n].rearrange("p (k n) -> p k n", k=8)
        winA = W["winA"].rearrange("(kc p) j n -> p kc j n", p=128)
        winC = W["winC"].rearrange("(kc p) j n -> p kc j n", p=128)
        winB = W["winB"].rearrange("(kc p) n -> p kc n", p=128)
        wkpe = W["wkpe"].rearrange("(kc p) n -> p kc n", p=128)
        winG = W["winG"].rearrange("(kc p) o n -> p kc o n", p=128)
        wbr = W["wbr"].rearrange("(kc p) o n -> p kc o n", p=128)
        wq = W["wq"].rearrange("(kc p) n -> p kc n", p=128)
        wukv = W["wukv"].rearrange("(kc p) n -> p kc n", p=128)
        wout = W["wout"].rearrange("(kc p) n -> p kc n", p=128)

        def chainA(j):
            buf, wt = self.wload([(lambda b: v8(b, 256), winA[:, :, j, :])])
            wv = v8(buf, 256)
            pxa = self.proj(lambda k: wv[:, k, 0:128], 8, hm_rhs, hmt, wt)
            i_in, tin, tint = self.ft()
            s.add("act", lambda e: e.activation(out=tin[:, 30:542], in_=self.psum[pxa][:, :], func=AF.Identity, bias=self.vcol(l, "b_xa", j)),
                  reads=[("ps", pxa), vt], writes=[tint])
            pga = self.proj(lambda k: wv[:, k, 128:256], 8, hm_rhs, hmt, wt)
            i_g, xg, xgt = self.ft()
            s.add("act", lambda e: e.activation(out=xg[:, 0:TC], in_=self.psum[pga][:, :], func=AF.Identity, bias=self.vcol(l, "b_ga", j)),
                  reads=[("ps", pga), vt], writes=[xgt])
            s.add("dve", lambda e: e.tensor_copy(out=tin[:, 27:30], in_=self.halo_a[:, j, :]), reads=[("halo_a", j)], writes=[tint])
            yield
            i_xa, xa, xat = self.ft()
            cw = lambda tap: self.vcol(l, "lru_cw", j * 4 + tap)
            s.add("dve", lambda e: e.tensor_scalar(out=xa[:, 0:TC], in0=tin[:, 27:27 + TC], scalar1=cw(0), scalar2=self.vcol(l, "lru_cb", j),
                                                   op0=ALU.mult, op1=ALU.add), reads=[tint, vt], writes=[xat])
            yield
            for tap in range(1, 4):
                s.add("dve", lambda e: e.scalar_tensor_tensor(out=xa[:, 0:TC], in0=tin[:, 27 + tap:27 + tap + TC], scalar=cw(tap), in1=xa[:, 0:TC],
                                                              op0=ALU.mult, op1=ALU.add), reads=[tint, xat, vt], writes=[xat])
                yield
            s.add("dve", lambda e: e.tensor_copy(out=self.halo_a[:, j, :], in_=tin[:, 539:542]), reads=[tint], writes=[("halo_a", j)])
            self.ffr(i_in)
            ib, xab, xabt = self.bt()
            s.add("act", lambda e: e.activation(out=xab[:, :], in_=xa[:, 0:TC], func=AF.Identity), reads=[xat], writes=[xabt])
            pr = self.proj(lambda k: self.gw[:, 2 * j, :], 1, lambda k: xab[:, :], [xabt], [("gw",)])
            i_r, r, rt = self.ft()
            s.add("act", lambda e: e.activation(out=r[:, 0:TC], in_=self.psum[pr][:, :], func=AF.Sigmoid, bias=self.vcol(l, "b_r", j)),
                  reads=[("ps", pr), vt], writes=[rt])
            pi_ = self.proj(lambda k: self.gw[:, 2 * j + 1, :], 1, lambda k: xab[:, :], [xabt], [("gw",)])
            i_i, ii, it = self.ft()
            s.add("act", lambda e: e.activation(out=ii[:, 0:TC], in_=self.psum[pi_][:, :], func=AF.Sigmoid, bias=self.vcol(l, "b_i", j)),
                  reads=[("ps", pi_), vt], writes=[it])
            self.bfr(ib)
            yield
            i_a2, a2, a2t = self.ft()
            s.add("act", lambda e: e.activation(out=a2[:, 0:TC], in_=r[:, 0:TC], func=AF.Exp, scale=self.nsp[:, 4 + j:5 + j]), reads=[rt, ("nsp2",)], writes=[a2t])
            s.add("act", lambda e: e.activation(out=r[:, 0:TC], in_=r[:, 0:TC], func=AF.Exp, scale=self.nsp[:, j:j + 1]), reads=[rt, a2t, ("nsp",)], writes=[rt])
            a, at, i_a = r, rt, i_r
            s.add("dve", lambda e: e.tensor_tensor(out=ii[:, 0:TC], in0=ii[:, 0:TC], in1=xa[:, 0:TC], op=ALU.mult), reads=[it, xat], writes=[it])
            yield
            s.add("act", lambda e: e.activation(out=a2[:, 0:TC], in_=a2[:, 0:TC], func=AF.Sqrt, scale=-1.0, bias=self.oneb[:, 0:1]), reads=[a2t, ("oneb",)], writes=[a2t])
            yield
            s.add("dve", lambda e: e.tensor_tensor(out=ii[:, 0:TC], in0=ii[:, 0:TC], in1=a2[:, 0:TC], op=ALU.mult), reads=[it, a2t], writes=[it])
            self.ffr(i_xa)
            self.ffr(i_a2)
            yield
            i_h, hl, hlt = self.ft()
            s.add("dve", lambda e: e.tensor_tensor_scan(out=hl[:, 0:TC], data0=a[:, 0:TC], data1=ii[:, 0:TC], initial=self.state[:, j:j + 1],
                                                        op0=ALU.mult, op1=ALU.add), reads=[at, it, ("state", j)], writes=[hlt])
            s.add("act", lambda e: e.activation(out=self.state[:, j:j + 1], in_=hl[:, TC - 1:TC], func=AF.Identity), reads=[hlt], writes=[("state", j)])
            self.ffr(i_a)
            self.ffr(i_i)
            yield
            i_q, qg, qgt = self.ft()
            s.add("act", lambda e: e.activation(out=qg[:, 0:TC], in_=xg[:, 0:TC], func=AF.Square), reads=[xgt], writes=[qgt])
            yield
            s.add("dve", lambda e: e.tensor_scalar(out=qg[:, 0:TC], in0=qg[:, 0:TC], scalar1=0.044715, scalar2=1.0, op0=ALU.mult, op1=ALU.add),
                  reads=[qgt], writes=[qgt])
            yield
            s.add("dve", lambda e: e.tensor_tensor(out=qg[:, 0:TC], in0=qg[:, 0:TC], in1=xg[:, 0:TC], op=ALU.mult), reads=[qgt, xgt], writes=[qgt])
            yield
            s.add("act", lambda e: e.activation(out=qg[:, 0:TC], in_=qg[:, 0:TC], func=AF.Sigmoid, scale=GELU_K), reads=[qgt], writes=[qgt])
            yield
            s.add("dve", lambda e: e.tensor_tensor(out=qg[:, 0:TC], in0=qg[:, 0:TC], in1=xg[:, 0:TC], op=ALU.mult), reads=[qgt, xgt], writes=[qgt])
            yield
            s.add("dve", lambda e: e.tensor_tensor(out=self.m_a[:, j, :], in0=qg[:, 0:TC], in1=hl[:, 0:TC], op=ALU.mult), reads=[qgt, hlt], writes=[("m_a", j)])
            self.ffr(i_g)
            self.ffr(i_q)
            self.ffr(i_h)

        def all_chains():
            for j in range(4):
                yield from chainA(j)
        agen = all_chains()

        def a_steps(n):
            for _ in range(n):
                try:
                    next(agen)
                except StopIteration:
                    return

        R = slice(64, 96)
        glob = self.glob
        i_ang, ang, angt = self.ft()
        i_cos, cos, cost = self.ft()
        i_sin, sin, sint = self.ft()
        i_msk, msk, mskt = self.ft()
        s.add("sp", lambda e: e.dma_start(out=msk[R, 0:TC].bitcast(I32), in_=self.dr_pos[R, t0:t0 + TC]),
              writes=[mskt], dma=True, semkey=("pos", i_msk))
        s.add("dve", lambda e: e.tensor_copy(out=ang[R, 0:TC], in_=msk[R, 0:TC].bitcast(I32)), reads=[mskt], writes=[angt])
        s.add("dve", lambda e: e.tensor_scalar(out=ang[R, 0:TC], in0=ang[R, 0:TC], scalar1=glob[R, 8:9], scalar2=None, op0=ALU.mult),
              reads=[angt, ("glob",)], writes=[angt])
        for dst, dtok, shift, sc in ((sin, sint, 0.0, glob[R, 9:10]), (cos, cost, PI / 2, 1.0)):
            ki = msk[R, 0:TC].bitcast(I32)
            s.add("dve", lambda e: e.tensor_scalar(out=dst[R, 0:TC], in0=ang[R, 0:TC], scalar1=shift, scalar2=1.0 / (2 * PI),
                                                   op0=ALU.add, op1=ALU.mult), reads=[angt], writes=[dtok])
            s.add("dve", lambda e: e.tensor_copy(out=ki, in_=dst[R, 0:TC]), reads=[dtok], writes=[mskt])
            s.add("dve", lambda e: e.tensor_copy(out=dst[R, 0:TC], in_=ki), reads=[mskt], writes=[dtok])
            s.add("dve", lambda e: e.scalar_tensor_tensor(out=dst[R, 0:TC], in0=dst[R, 0:TC], scalar=-2 * PI, in1=ang[R, 0:TC],
                                                          op0=ALU.mult, op1=ALU.add), reads=[dtok, angt], writes=[dtok])
            s.add("dve", lambda e: e.tensor_scalar(out=dst[R, 0:TC], in0=dst[R, 0:TC], scalar1=shift, scalar2=None, op0=ALU.add),
                  reads=[dtok], writes=[dtok])
            s.add("dve", lambda e: e.tensor_scalar(out=msk[R, 0:TC], in0=dst[R, 0:TC], scalar1=PI, scalar2=-2 * PI,
                                                   op0=ALU.is_gt, op1=ALU.mult), reads=[dtok], writes=[mskt])
            s.add("dve", lambda e: e.tensor_tensor(out=dst[R, 0:TC], in0=dst[R, 0:TC], in1=msk[R, 0:TC], op=ALU.add),
                  reads=[dtok, mskt], writes=[dtok])
            s.add("dve", lambda e: e.tensor_scalar(out=dst[R, 0:TC], in0=dst[R, 0:TC], scalar1=-PI, scalar2=PI,
                                                   op0=ALU.max, op1=ALU.min), reads=[dtok], writes=[dtok])
            s.add("act", lambda e: e.activation(out=dst[R, 0:TC], in_=dst[R, 0:TC], func=AF.Sin, scale=sc),
                  reads=[dtok, ("glob",)], writes=[dtok])
        self.ffr(i_ang)
        self.ffr(i_msk)

        def latent(src, col0, n, bname, gname, dst, dname):
            buf, wt = self.wload([(lambda b: v8(b, n * 128), src[:, :, col0:col0 + n * 128])])
            wv = v8(buf, n * 128)
            tmps = []
            for i in range(n):
                p = self.proj(lambda k: wv[:, k, i * 128:(i + 1) * 128], 8, hm_rhs, hmt, wt)
                fi, t, tt = self.ft()
                s.add("act", lambda e: e.activation(out=t[:, 0:TC], in_=self.psum[p][:, :], func=AF.Identity,
                                                    bias=self.vcol(l, bname, i)), reads=[("ps", p), vt], writes=[tt])
                tmps.append((fi, t, tt))
            ri, r, rtok = self.rstd_from([t[:, 0:TC] for _, t, _ in tmps], [tt for _, _, tt in tmps], n * 128)
            for i, (fi, t, tt) in enumerate(tmps):
                s.add("dve", lambda e: e.scalar_tensor_tensor(out=dst[:, i, :], in0=t[:, 0:TC], scalar=self.vcol(l, gname, i),
                                                              in1=r[:, 0:TC], op0=ALU.mult, op1=ALU.mult),
                      reads=[tt, rtok, vt], writes=[(dname, i), (("merged", i) if dname == "cqn" else ("cc", i))])
                self.ffr(fi)
            self.ffr(ri)
        latent(winB, 0, 3, "b_cq", "q_norm", self.cqn, "cqn")
        latent(winB, 384, 2, "b_ckv", "kv_norm", self.ckvn, "ckvn")
        cqt = [("cqn", i) for i in range(3)]
        ckvt = [("ckvn", i) for i in range(2)]

        buf, wt = self.wload([(lambda b: v8(b, 192), wkpe[:, :, :])])
        wv = v8(buf, 192)
        pn = self.proj(lambda k: wv[:, k, 0:96], 8, hm_rhs, hmt, wt, M=96)
        psw = self.proj(lambda k: wv[:, k, 96:192], 8, hm_rhs, hmt, wt, M=96)
        i_kn, kn, knt = self.ft()
        i_ks, ks, kst = self.ft()
        s.add("act", lambda e: e.activation(out=kn[R, 0:TC], in_=self.psum[pn][R, :], func=AF.Identity,
                                            bias=self.vecs[l][R, _VC["b_kpe"]:_VC["b_kpe"] + 1]), reads=[("ps", pn), vt], writes=[knt])
        s.add("act", lambda e: e.activation(out=ks[R, 0:TC], in_=self.psum[psw][R, :], func=AF.Identity,
                                            bias=self.vecs[l][R, _VC["b_kpe"] + 1:_VC["b_kpe"] + 2]), reads=[("ps", psw), vt], writes=[kst])
        s.add("dve", lambda e: e.tensor_tensor(out=kn[R, 0:TC], in0=kn[R, 0:TC], in1=cos[R, 0:TC], op=ALU.mult), reads=[knt, cost], writes=[knt])
        s.add("dve", lambda e: e.tensor_tensor(out=ks[R, 0:TC], in0=ks[R, 0:TC], in1=sin[R, 0:TC], op=ALU.mult), reads=[kst, sint], writes=[kst])
        i_kr, kr, krt = self.bt()
        s.add("dve", lambda e: e.tensor_tensor(out=kr[R, :], in0=kn[R, 0:TC], in1=ks[R, 0:TC], op=ALU.add), reads=[knt, kst], writes=[krt])
        s.add("dve", lambda e: e.tensor_copy(out=self.K[R, :, t0:t0 + TC], in_=kr[R, :].unsqueeze(1).broadcast_to([32, 8, TC])),
              reads=[krt], writes=[("Kpe", tg)])
        self.ffr(i_kn)
        self.ffr(i_ks)
        self.bfr(i_kr)

        buf, wt = self.wload([(lambda b: b[:, 0:2048].rearrange("p (k n) -> p k n", k=2), wukv[:, :, :])])
        wv = buf[:, 0:2048].rearrange("p (k n) -> p k n", k=2)
        for h in range(8):
            pk = self.proj(lambda k: wv[:, k, h * 128:h * 128 + 64], 2, lambda k: self.ckvn[:, k, :], ckvt, wt, M=64)
            eng = "act" if h % 2 == 0 else "dve"
            if eng == "act":
                s.add("act", lambda e: e.activation(out=self.K[0:64, h, t0:t0 + TC], in_=self.psum[pk][0:64, :], func=AF.Identity),
                      reads=[("ps", pk)], writes=[("K", h, tg)])
            else:
                s.add("dve", lambda e: e.tensor_copy(out=self.K[0:64, h, t0:t0 + TC], in_=self.psum[pk][0:64, :]),
                      reads=[("ps", pk)], writes=[("K", h, tg)])
        wvv = buf[:, 0:2048].rearrange("p (k h two d) -> p k h two d", k=2, h=8, two=2)
        for blk in range(4):
            gb = tg * 4 + blk
            pb = self.next_psum()
            for k in range(2):
                s.add("pe", lambda e: e.matmul(self.psum[pb][:, :].rearrange("p (h d) -> p h d", h=8),
                                               lhsT=self.ckvn[:, k, blk * 128:(blk + 1) * 128], rhs=wvv[:, k, :, 1, :],
                                               start=(k == 0), stop=(k == 1)), reads=ckvt + wt, writes=[("ps", pb)])
            s.add("act", lambda e: e.activation(out=self.V[:, gb, :, 0:64], in_=self.psum[pb][:, :].rearrange("p (h d) -> p h d", h=8), func=AF.Identity),
                  reads=[("ps", pb)], writes=[("V", gb)])

        cbb = self.cbb
        for j in range(4):
            buf, wt = self.wload([(lambda b: v8(b, 256), winC[:, :, j, :])])
            wv = v8(buf, 256)
            pv = self.proj(lambda k: wv[:, k, 0:128], 8, hm_rhs, hmt, wt)
            pg = self.proj(lambda k: wv[:, k, 128:256], 8, hm_rhs, hmt, wt)
            i_s, sg, sgt = self.ft()
            s.add("act", lambda e: e.activation(out=sg[:, 0:TC], in_=self.psum[pg][:, :], func=AF.Sigmoid, bias=self.vcol(l, "b_cg", j)),
                  reads=[("ps", pg), vt], writes=[sgt])
            s.add("dve", lambda e: e.tensor_copy(out=cbb[:, j, 0:30], in_=cbb[:, j, 512:542]), reads=[("cbb", j)], writes=[("cbb", j)])
            s.add("dve", lambda e: e.scalar_tensor_tensor(out=cbb[:, j, 30:542], in0=self.psum[pv][:, :], scalar=self.vcol(l, "b_cv", j), in1=sg[:, 0:TC],
                                                          op0=ALU.add, op1=ALU.mult), reads=[("ps", pv), sgt, vt, ("cbb", j)], writes=[("cbb", j)])
            self.ffr(i_s)

        dwd = W["dwd"]
        conv_state = {}

        def conv_load(pc_):
            j, half = pc_ // 2, pc_ % 2
            c0, c1 = (0, 2048) if half == 0 else (2048, 3968)
            buf, wt = self.wload([(lambda b: b[:, 0:c1 - c0], dwd[:, j, c0:c1])])
            conv_state[pc_] = (buf, wt, self.wp_last)
            self.wp_pinned.add(self.wp_last)

        def conv_mm(pc_):
            j, half = pc_ // 2, pc_ % 2
            buf, wt, wpi = conv_state.pop(pc_)
            self.wp_pinned.discard(wpi)
            taps = range(0, 16) if half == 0 else range(16, 31)
            for tap in taps:
                lt = buf[:, (tap - taps[0]) * 128:(tap - taps[0] + 1) * 128]
                s.add("pe", lambda e: e.matmul(self.psum[5][:, :], lhsT=lt, rhs=cbb[:, j, tap:tap + TC], start=(tap == 0), stop=(tap == 30)),
                      reads=wt + [("cbb", j)], writes=[("ps", 5)])
            if half == 1:
                s.add("act", lambda e: e.activation(out=self.cc[:, j, :], in_=self.psum[5][:, :], func=AF.Identity, bias=self.vcol(l, "dw_b", j)),
                      reads=[("ps", 5), vt], writes=[("cc", j)] + ([("ckvn", j)] if j < 2 else []))

        wq_state = {}

        def prologue(h):
            if h % 4 == 0:
                g = h // 4
                bufq, wtq = self.wload([(lambda b: b[:, 0:1536].rearrange("p (k n) -> p k n", k=3), wq[:, :, g * 512:(g + 1) * 512])])
                wq_state[g] = (bufq[:, 0:1536].rearrange("p (k n) -> p k n", k=3), wtq, self.wp_last)
                self.wp_pinned.add(self.wp_last)
            wvq, wtq, wpi = wq_state[h // 4]
            c0 = (h % 4) * 128
            pqn = self.proj(lambda k: wvq[:, k, c0:c0 + 96], 3, lambda k: self.cqn[:, k, :], cqt, wtq, M=96)
            pqs = self.proj(lambda k: wvq[:, k, c0 + 32:c0 + 128], 3, lambda k: self.cqn[:, k, :], cqt, wtq, M=96)
            if h % 4 == 3:
                self.wp_pinned.discard(wpi)
            i_q, Qh, Qt = self.bt()
            i_t1, t1, t1t = self.ft()
            i_t2, t2, t2t = self.ft()
            s.add("act", lambda e: e.activation(out=Qh[0:64, :], in_=self.psum[pqn][0:64, :], func=AF.Identity), reads=[("ps", pqn)], writes=[Qt])
            s.add("dve", lambda e: e.tensor_tensor(out=t1[R, 0:TC], in0=self.psum[pqn][R, :], in1=cos[R, 0:TC], op=ALU.mult),
                  reads=[("ps", pqn), cost], writes=[t1t])
            s.add("dve", lambda e: e.tensor_tensor(out=t2[R, 0:TC], in0=self.psum[pqs][R, :], in1=sin[R, 0:TC], op=ALU.mult),
                  reads=[("ps", pqs), sint], writes=[t2t])
            s.add("dve", lambda e: e.tensor_tensor(out=Qh[R, :], in0=t1[R, 0:TC], in1=t2[R, 0:TC], op=ALU.add), reads=[t1t, t2t, Qt], writes=[Qt])
            self.ffr(i_t1)
            self.ffr(i_t2)
            return i_q, Qh, Qt

        def attention(h, i_q, Qh, Qt):
            pO = 6 + (h % 2)
            nkb = 4 * tg + 4
            live = {}

            def emit_scores(kb):
                kl = kb - 4 * tg
                q0 = max(kl, 0) * 128
                psn = self.next_psum()
                s.add("pe", lambda e: e.matmul(self.psum[psn][:, q0:512], lhsT=self.K[0:96, h, kb * 128:(kb + 1) * 128],
                                               rhs=Qh[0:96, q0:512], start=True, stop=True),
                      reads=[("K", h, kb // 4), ("Kpe", kb // 4), Qt], writes=[("ps", psn)])
                i_p, pT, pTt = self.bt()
                s.add("act", lambda e: e.activation(out=pT[:, q0:512], in_=self.psum[psn][:, q0:512], func=AF.Exp, scale=SCALE),
                      reads=[("ps", psn)], writes=[pTt])
                if kl >= 0:
                    s.add("pool", lambda e: e.tensor_tensor(out=pT[:, q0:q0 + 128], in0=pT[:, q0:q0 + 128], in1=self.tri[:, :], op=ALU.mult),
                          reads=[pTt, ("tri",)], writes=[pTt])
                live[kb] = (i_p, pT, pTt)

            def emit_pv(kb):
                kl = kb - 4 * tg
                i_p, pT, pTt = live.pop(kb)
                for qi in range(max(kl, 0), 4):
                    s.add("pe", lambda e: e.matmul(self.psum[pO][:, qi * 65:(qi + 1) * 65], lhsT=pT[:, qi * 128:(qi + 1) * 128],
                                                   rhs=self.V[:, kb, h, :], start=(kb == 0 and qi == 0), stop=(kb == 4 * tg + qi),
                                                   skip_group_check=True),
                          reads=[pTt, ("V", kb), ("Vones",)], writes=[("ps", pO)])
                self.bfr(i_p)
            LA = 2
            for kb in range(min(LA, nkb)):
                emit_scores(kb)
            for kb in range(nkb):
                if kb + LA < nkb:
                    emit_scores(kb + LA)
                emit_pv(kb)
            self.bfr(i_q)
            s.add("dve", lambda e: e.reciprocal(out=self.rec[:, 0:4], in_=self.psum[pO][:, 0:260].rearrange("p (q d) -> p q d", d=65)[:, :, 64]),
                  reads=[("ps", pO)], writes=[("rec",)])
            for qi in range(4):
                s.add("act", lambda e: e.activation(out=self.O_sb[:, qi, h * 64:(h + 1) * 64], in_=self.psum[pO][:, qi * 65:qi * 65 + 64],
                                                    func=AF.Identity, scale=self.rec[:, qi:qi + 1]),
                      reads=[("ps", pO), ("rec",)], writes=[("O_sb", qi, h)])

        self.psum_n = 5
        self.psum_rr %= 5
        conv_load(0)
        nxt = prologue(0)
        for h in range(8):
            cur = nxt
            if h + 1 < 8:
                conv_load(h + 1)
                nxt = prologue(h + 1)
            attention(h, *cur)
            conv_mm(h)
            a_steps(9)
        a_steps(1000)
        self.psum_n = 6
        self.ffr(i_cos)
        self.ffr(i_sin)

        p1 = self.next_psum()
        p2 = self.next_psum()
        for j in range(4):
            s.add("pe", lambda e: e.matmul(self.psum[p1][:, :], lhsT=self.ones_bf[:, :], rhs=self.cc[:, j, :],
                                           start=(j == 0), stop=(j == 3)), reads=[("cc", j), ("ones_bf",)], writes=[("ps", p1)])
        for j in range(4):
            ib, sq, sqt = self.bt()
            s.add("act", lambda e: e.activation(out=sq[:, :], in_=self.cc[:, j, :], func=AF.Square), reads=[("cc", j)], writes=[sqt])
            s.add("pe", lambda e: e.matmul(self.psum[p2][:, :], lhsT=self.ones_bf[:, :], rhs=sq[:, :],
                                           start=(j == 0), stop=(j == 3)), reads=[sqt, ("ones_bf",)], writes=[("ps", p2)])
            self.bfr(ib)
        i_m, mean, meant = self.ft()
        i_v, var, vart = self.ft()
        s.add("act", lambda e: e.activation(out=mean[:, 0:TC], in_=self.psum[p1][:, :], func=AF.Identity, scale=1.0 / 512),
              reads=[("ps", p1)], writes=[meant])
        s.add("act", lambda e: e.activation(out=var[:, 0:TC], in_=mean[:, 0:TC], func=AF.Square), reads=[meant], writes=[vart])
        s.add("dve", lambda e: e.scalar_tensor_tensor(out=var[:, 0:TC], in0=self.psum[p2][:, :], scalar=1.0 / 512, in1=var[:, 0:TC],
                                                      op0=ALU.mult, op1=ALU.subtract), reads=[("ps", p2), vart], writes=[vart])
        s.add("act", lambda e: e.activation(out=var[:, 0:TC], in_=var[:, 0:TC], func=AF.Sqrt, bias=self.epsb[:, 0:1]),
              reads=[vart, ("epsb",)], writes=[vart])
        s.add("dve", lambda e: e.reciprocal(out=var[:, 0:TC], in_=var[:, 0:TC]), reads=[vart], writes=[vart])
        for j in range(4):
            i_t, tt, ttt = self.ft()
            s.add("dve", lambda e: e.tensor_tensor(out=tt[:, 0:TC], in0=self.cc[:, j, :], in1=mean[:, 0:TC], op=ALU.subtract),
                  reads=[("cc", j), meant], writes=[ttt])
            s.add("dve", lambda e: e.tensor_tensor(out=tt[:, 0:TC], in0=tt[:, 0:TC], in1=var[:, 0:TC], op=ALU.mult),
                  reads=[ttt, vart], writes=[ttt])
            s.add("act", lambda e: e.activation(out=self.cs[:, j, :], in_=tt[:, 0:TC], func=AF.Silu,
                                                scale=self.vcol(l, "ln_g", j), bias=self.vcol(l, "ln_b", j)),
                  reads=[ttt, vt], writes=[("cs", j)])
            self.ffr(i_t)
        self.ffr(i_m)
        self.ffr(i_v)

        for qi in range(4):
            pb = self.next_psum()
            pbf = self.psum[pb][:, :].bitcast(BF16)
            for fc in range(4):
                s.add("pe", lambda e: e.transpose(out=pbf[:, fc * 128:(fc + 1) * 128], in_=self.O_sb[:, qi, fc * 128:(fc + 1) * 128],
                                                  identity=self.ident[:, :]),
                      reads=[("O_sb", qi, hh) for hh in range(8)] + [("ident",)], writes=[("ps", pb)])
            s.add("dve", lambda e: e.tensor_copy(out=self.OT[:, :, qi * 128:(qi + 1) * 128],
                                                 in_=pbf[:, 0:512].rearrange("p (k t) -> p k t", k=4)),
                  reads=[("ps", pb)], writes=[("OT", qi)])

        for o in range(8):
            bufG, wtG = self.wload([(lambda b: v8(b, 384), winG[:, :, o, :])])
            wvG = v8(bufG, 384)
            bufR, wtR = self.wload([(lambda b: b[:, 0:1536].rearrange("p (k n) -> p k n", k=4), wbr[:, :, o, :])])
            wvR = bufR[:, 0:1536].rearrange("p (k n) -> p k n", k=4)
            pg = [self.proj(lambda k: wvG[:, k, br * 128:(br + 1) * 128], 8, hm_rhs, hmt, wtG) for br in range(3)]
            pya = self.proj(lambda k: wvR[:, k, 0:128], 4, lambda k: self.m_a[:, k, :], [("m_a", j) for j in range(4)], wtR)
            pyb = self.proj(lambda k: wvR[:, k, 128:256], 4, lambda k: self.OT[:, k, :], [("OT", q) for q in range(4)], wtR)
            pyc = self.proj(lambda k: wvR[:, k, 256:384], 4, lambda k: self.cs[:, k, :], [("cs", j) for j in range(4)], wtR)
            gs = []
            for br in range(3):
                fi, gt_, gtok_ = self.ft()
                s.add("act", lambda e: e.activation(out=gt_[:, 0:TC], in_=self.psum[pg[br]][:, :], func=AF.Sigmoid,
                                                    bias=self.vcol(l, "b_G", br * 8 + o)), reads=[("ps", pg[br]), vt], writes=[gtok_])
                gs.append((fi, gt_, gtok_))
            (f0, g0, g0t), (f1, g1, g1t), (f2, g2, g2t) = gs
            s.add("dve", lambda e: e.tensor_tensor(out=g0[:, 0:TC], in0=g0[:, 0:TC], in1=self.psum[pya][:, :], op=ALU.mult),
                  reads=[g0t, ("ps", pya)], writes=[g0t])
            s.add("dve", lambda e: e.tensor_tensor(out=g1[:, 0:TC], in0=g1[:, 0:TC], in1=self.psum[pyb][:, :], op=ALU.mult),
                  reads=[g1t, ("ps", pyb)], writes=[g1t])
            s.add("dve", lambda e: e.scalar_tensor_tensor(out=g2[:, 0:TC], in0=self.psum[pyc][:, :], scalar=self.vcol(l, "cb_out", o),
                                                          in1=g2[:, 0:TC], op0=ALU.add, op1=ALU.mult), reads=[g2t, ("ps", pyc), vt], writes=[g2t])
            s.add("dve", lambda e: e.tensor_tensor(out=g0[:, 0:TC], in0=g0[:, 0:TC], in1=g1[:, 0:TC], op=ALU.add), reads=[g0t, g1t], writes=[g0t])
            s.add("dve", lambda e: e.tensor_tensor(out=self.merged[:, o, :], in0=g0[:, 0:TC], in1=g2[:, 0:TC], op=ALU.add),
                  reads=[g0t, g2t], writes=[("merged", o)] + ([("cqn", o)] if o < 3 else []))
            self.ffr(f0)
            self.ffr(f1)
            self.ffr(f2)
        mt = [("merged", o) for o in range(8)]
        for op_ in range(4):
            buf, wt = self.wload([(lambda b: v8(b, 256), wout[:, :, op_ * 256:(op_ + 1) * 256])])
            wv = v8(buf, 256)
            for hf in range(2):
                o2 = op_ * 2 + hf
                py = self.proj(lambda k: wv[:, k, hf * 128:(hf + 1) * 128], 8, lambda k: self.merged[:, k, :], mt, wt)
                s.add("dve", lambda e: e.tensor_tensor(out=self.x[:, o2, t0:t0 + TC], in0=self.psum[py][:, :], in1=self.x[:, o2, t0:t0 + TC], op=ALU.add),
                      reads=[("ps", py), ("x", o2, tg)], writes=[("x", o2, tg)])


_WNAMES = ["w1r_a", "w2_a", "winA", "winC", "winB", "wkpe", "winG", "wbr", "wq", "wukv", "wout", "gwd", "dwd", "w1r_b", "w2_b"]
_WSHAPES = {
    "w1r_a": [D, NFC, 256], "w2_a": [DFF, D], "w1r_b": [D, NFC, 256], "w2_b": [DFF, D],
    "winA": [D, 4, 256], "winC": [D, 4, 256], "winB": [D, 640], "wkpe": [D, 192], "winG": [D, 8, 384],
    "wbr": [512, 8, 384], "wq": [384, 1024], "wukv": [256, 1024], "wout": [D, D], "gwd": [128, 1024], "dwd": [128, 4, 31 * 128],
}


def build_program(S=2048, L=DEPTH, parts=("ffn1", "mixer", "ffn2"), final=True):
    b = Builder(S, L)
    nc = b.nc
    xT = b.din("xT", [D, S])
    outT = nc.dram_tensor("outT", [D, S], F32, kind="ExternalOutput")
    W = []
    for l in range(L):
        W.append({n: b.din("%s_%d" % (n, l), _WSHAPES[n]) for n in _WNAMES})
    b.setup()
    b.load_consts()
    b.load_x(xT)
    for l in range(L):
        if "ffn1" in parts:
            b.ffn(l, "ffn1_norm", W[l]["w1r_a"], W[l]["w2_a"])
        if "mixer" in parts:
            b.s.barrier()
            b.mixer_layer_setup(l, W[l]["gwd"])
            for tg in range(b.NTC):
                b.mixer_chunk(l, tg, W[l])
            b.s.barrier()
        if "ffn2" in parts:
            b.ffn(l, "ffn2_norm", W[l]["w1r_b"], W[l]["w2_b"])
    if final:
        b.final_norm()
    b.store_x(outT)
    b.s.emit(nc)
    return nc, b


def _fm(v):
    return np.ascontiguousarray(v.reshape(-1, 128).T)


def host_layout(inp, l):
    f32 = np.float32
    w = {}
    for tag, nm in (("a", "ffn1"), ("b", "ffn2")):
        w1 = inp[nm + "_w1"][l]
        w["w1r_" + tag] = np.ascontiguousarray(
            np.concatenate([w1[:, :DFF].reshape(D, NFC, 128), w1[:, DFF:].reshape(D, NFC, 128)], axis=2))
        w["w2_" + tag] = np.ascontiguousarray(inp[nm + "_w2"][l])
    win = inp["w_in"][l]
    w["winA"] = np.ascontiguousarray(np.concatenate([win[:, 0:512].reshape(D, 4, 128), win[:, 512:1024].reshape(D, 4, 128)], axis=2))
    w["winB"] = np.ascontiguousarray(win[:, 1024:1664])
    kp = win[:, 1664:1696]
    wk = np.zeros((D, 192), f32)
    wk[:, 64:96] = kp
    wk[:, 160:176] = kp[:, 16:32]
    wk[:, 176:192] = kp[:, 0:16]
    w["wkpe"] = wk
    w["winC"] = np.ascontiguousarray(np.concatenate([win[:, 1696:2208].reshape(D, 4, 128), win[:, 2208:2720].reshape(D, 4, 128)], axis=2))
    G = win[:, 2720:].reshape(D, 3, 8, 128)
    w["winG"] = np.ascontiguousarray(G.transpose(0, 2, 1, 3).reshape(D, 8, 384))
    br = np.stack([inp["lru_w_out"][l].reshape(512, 8, 128), inp["mla_w_o"][l].reshape(512, 8, 128),
                   inp["conv_w_out"][l].reshape(512, 8, 128)], axis=2)
    w["wbr"] = np.ascontiguousarray(br.reshape(512, 8, 384))
    uq = inp["w_uq"][l].reshape(384, 8, 96)
    w["wq"] = np.ascontiguousarray(np.concatenate([uq, uq[:, :, 80:96], uq[:, :, 64:80]], axis=2).reshape(384, 1024))
    w["wukv"] = np.ascontiguousarray(inp["w_ukv"][l])
    w["wout"] = np.ascontiguousarray(inp["w_out"][l])
    wg = inp["lru_w_gate"][l]
    gwd = np.zeros((128, 8, 128), f32)
    for j in range(4):
        for e in range(2):
            hd = 2 * j + e
            gwd[e * 64:(e + 1) * 64, 2 * j, e * 64:(e + 1) * 64] = wg[hd][:, 0:64]
            gwd[e * 64:(e + 1) * 64, 2 * j + 1, e * 64:(e + 1) * 64] = wg[hd][:, 64:128]
    w["gwd"] = gwd.reshape(128, 1024)
    dwl = inp["conv_dw_w"][l]
    dwd = np.zeros((128, 4, 31, 128), f32)
    pidx = np.arange(128)
    for j in range(4):
        dwd[pidx, j, :, pidx] = dwl[:, j * 128:(j + 1) * 128].T
    w["dwd"] = dwd.reshape(128, 4, 31 * 128)
    v = np.zeros((128, NVEC), f32)
    def put(name, arr):
        c = _VC[name]
        v[:, c:c + arr.shape[1]] = arr
    put("ffn1_norm", _fm(inp["ffn1_norm"][l])); put("mix_norm", _fm(inp["mix_norm"][l])); put("ffn2_norm", _fm(inp["ffn2_norm"][l]))
    bi = inp["b_in"][l]
    put("b_xa", _fm(bi[0:512])); put("b_ga", _fm(bi[512:1024])); put("b_cq", _fm(bi[1024:1408])); put("b_ckv", _fm(bi[1408:1664]))
    bk = np.zeros((128, 2), f32)
    bk[64:96, 0] = bi[1664:1696]
    bk[64:80, 1] = bi[1680:1696]
    bk[80:96, 1] = bi[1664:1680]
    put("b_kpe", bk)
    put("b_cv", _fm(bi[1696:2208])); put("b_cg", _fm(bi[2208:2720])); put("b_G", _fm(bi[2720:]))
    cw = inp["lru_conv_w"][l]
    put("lru_cw", np.ascontiguousarray(cw.reshape(4, 4, 128).transpose(2, 1, 0).reshape(128, 16)))
    put("lru_cb", _fm(inp["lru_conv_b"][l]))
    bg = inp["lru_b_gate"][l].reshape(4, 2, 2, 64)
    put("b_r", np.ascontiguousarray(bg[:, :, 0, :].transpose(1, 2, 0).reshape(128, 4)))
    put("b_i", np.ascontiguousarray(bg[:, :, 1, :].transpose(1, 2, 0).reshape(128, 4)))
    put("lam", _fm(inp["lru_lambda"][l])); put("q_norm", _fm(inp["q_norm"][l])); put("kv_norm", _fm(inp["kv_norm"][l]))
    dw = inp["conv_dw_w"][l]
    put("dw_w", np.ascontiguousarray(dw.reshape(31, 4, 128).transpose(2, 1, 0).reshape(128, 124)))
    put("dw_b", _fm(inp["conv_dw_b"][l])); put("ln_g", _fm(inp["conv_ln_g"][l])); put("ln_b", _fm(inp["conv_ln_b"][l]))
    put("cb_out", _fm(inp["conv_b_out"][l]))
    return w, v


def host_consts(inp, S):
    f32 = np.float32
    c = {}
    c["c_ident"] = np.eye(128, dtype=f32)
    c["c_tri"] = np.triu(np.ones((128, 128), f32))
    g = np.zeros((128, NGL), f32)
    g[:, 0:8] = _fm(inp["final_norm"])
    inv = (10000.0 ** (-np.arange(0, 32, 2, dtype=np.float32) / 32)).astype(f32)
    g[64:96, 8] = np.concatenate([inv, inv])
    g[64:80, 9] = -1.0
    g[80:96, 9] = 1.0
    c["c_glob"] = g
    return c


def make_in_maps(inp, S=2048, L=DEPTH, cores=range(8)):
    shared = {}
    for l in range(L):
        w, v = host_layout(inp, l)
        for n in _WNAMES:
            shared["%s_%d" % (n, l)] = w[n]
        shared["vecs%d" % l] = v
    shared.update(host_consts(inp, S))
    maps = []
    for b in cores:
        m = dict(shared)
        m["xT"] = np.ascontiguousarray(inp["x"][b, :S].T)
        m["pos_rep"] = np.ascontiguousarray(np.broadcast_to(inp["positions"][b, :S][None, :], (96, S))).astype(np.int32)
        maps.append(m)
    return maps


_CACHE = {}


def kernel(**inputs):
    inp = {k: np.asarray(v) for k, v in inputs.items()}
    if "prog" not in _CACHE:
        _CACHE["prog"] = build_program()[0]
    nc = _CACHE["prog"]
    maps = make_in_maps(inp)
    res = run_bass_kernel_spmd(nc, maps, core_ids=list(range(8)))
    out = np.stack([np.ascontiguousarray(r["outT"].T) for r in res.results], axis=0)
    return out.astype(np.float32)
```

```python
import numpy as np
import concourse.bass as bass
import concourse.mybir as mybir
from concourse.bass_utils import run_bass_kernel_spmd

F32 = mybir.dt.float32
BF16 = mybir.dt.bfloat16
I32 = mybir.dt.int32
AF = mybir.ActivationFunctionType
ALU = mybir.AluOpType
AX = mybir.AxisListType

D = 1024
S = 2048
DFF = 2816
DEPTH = 2
EPS = 1e-6
NKC = D // 128
NFC = DFF // 128
TC = 512
NTC = S // TC


class _Op:
    __slots__ = ("eng", "fn", "idx", "waits", "signal", "sem", "val", "dma", "semkey")

    def __init__(self, eng, fn, dma, semkey):
        self.eng = eng
        self.fn = fn
        self.dma = dma
        self.semkey = semkey
        self.waits = []
        self.signal = dma
        self.sem = None
        self.val = 0
        self.idx = 0


class _Rec:
    def __getattr__(self, name):
        return lambda *a, **k: (name, a, k)


_REC = _Rec()


class Sched:
    ENGS = ("pe", "act", "dve", "pool", "sp")

    def __init__(self):
        self.ops = {e: [] for e in self.ENGS}
        self.last_w = {}
        self.readers = {}
        self.waited = {e: {} for e in self.ENGS}
        self.dma_seq = {}
        self.all_dma = []

    def add(self, eng, fn, reads=(), writes=(), dma=False, semkey=None, extra=()):
        if fn is not None:
            name_, a_, k_ = fn(_REC)
            fn = (lambda e, name_=name_, a_=a_, k_=k_: getattr(e, name_)(*a_, **k_))
        op = _Op(eng, fn, dma, semkey)
        op.idx = len(self.ops[eng])
        if dma:
            assert semkey is not None
        deps = list(extra)
        for r in reads:
            w = self.last_w.get(r)
            if w is not None:
                deps.append(w)
        for wtok in writes:
            w = self.last_w.get(wtok)
            if w is not None:
                deps.append(w)
            deps.extend(self.readers.get(wtok, ()))
        for p in deps:
            if p is op:
                continue
            if p.dma:
                key = ("dma", p.semkey)
                pos = p.val
            else:
                if p.eng == "pe" and eng == "pe":
                    continue
                key = p.eng
                pos = p.idx + 1
            if self.waited[eng].get(key, 0) >= pos:
                continue
            self.waited[eng][key] = pos
            p.signal = True
            op.waits.append(p)
        if dma:
            n = self.dma_seq.get(semkey, 0) + 1
            self.dma_seq[semkey] = n
            op.val = n
            self.all_dma.append(op)
        for r in reads:
            self.readers.setdefault(r, []).append(op)
        for wtok in writes:
            self.last_w[wtok] = op
            self.readers[wtok] = []
        self.ops[eng].append(op)
        return op

    def barrier(self):
        comp = ("pe", "act", "dve")
        last = {}
        for e in comp:
            for op in reversed(self.ops[e]):
                if not op.dma and op.fn is not None:
                    last[e] = op
                    break
        for e in comp:
            self.add(e, None, extra=[last[f] for f in comp if f != e and f in last])

    def emit(self, nc, final_wait_eng="sp"):
        from contextlib import ExitStack
        with ExitStack() as es:
            esem = {e: es.enter_context(nc.semaphore("s_" + e)) for e in self.ENGS}
            dsem = {}
            for k in self.dma_seq:
                dsem[k] = es.enter_context(nc.semaphore("d_%s" % (len(dsem),)))
            for e in self.ENGS:
                c = 0
                for op in self.ops[e]:
                    if op.dma:
                        op.sem = dsem[op.semkey]
                        op.val = op.val * 16
                    elif op.signal:
                        assert op.fn is not None
                        c += 1
                        op.sem = esem[e]
                        op.val = c
            block = es.enter_context(nc.Block())

            def run(e, engobj, extra_final=False):
                for op in self.ops[e]:
                    for p in op.waits:
                        engobj.wait_ge(p.sem, p.val)
                    if op.fn is None:
                        continue
                    ins = op.fn(engobj)
                    if op.dma:
                        ins.then_inc(op.sem, 16)
                    elif op.signal:
                        ins.then_inc(op.sem, 1)
                if extra_final:
                    for k, n in self.dma_seq.items():
                        engobj.wait_ge(dsem[k], 16 * n)

            @block.tensor
            def _(eng):
                run("pe", eng)

            @block.scalar
            def _(eng):
                run("act", eng)

            @block.vector
            def _(eng):
                run("dve", eng)

            @block.gpsimd
            def _(eng):
                run("pool", eng)

            @block.sync
            def _(eng):
                run("sp", eng, extra_final=True)


_VC = {}
def _vc_build():
    c = 0
    def put(name, n):
        nonlocal c
        _VC[name] = c
        c += n
    put("ffn1_norm", 8); put("mix_norm", 8); put("ffn2_norm", 8)
    put("b_xa", 4); put("b_ga", 4); put("b_cq", 3); put("b_ckv", 2); put("b_kpe", 2)
    put("b_cv", 4); put("b_cg", 4); put("b_G", 24)
    put("lru_cw", 16); put("lru_cb", 4); put("b_r", 4); put("b_i", 4); put("lam", 4)
    put("q_norm", 3); put("kv_norm", 2); put("dw_w", 124); put("dw_b", 4)
    put("ln_g", 4); put("ln_b", 4); put("cb_out", 8)
    return c
NVEC = _vc_build()
NFT = 9
NGL = 16

SCALE = 96.0 ** -0.5
GELU_K = 1.5957691216057308
PI = 3.141592653589793


class Builder:
    def __init__(self, S=2048, L=DEPTH):
        self.nc = bass.Bass("TRN2", target_bir_lowering=False)
        self.s = Sched()
        self.S = S
        self.L = L
        self.NTC = S // TC
        self.dram_in = {}
        self.psum_rr = 0
        self.psum_n = 6
        self.wp_rr = 0
        self.wp_pinned = set()

    def din(self, name, shape, dt=F32):
        t = self.nc.dram_tensor(name, list(shape), dt, kind="ExternalInput")
        self.dram_in[name] = t
        return t

    def sb(self, name, shape, dt):
        return self.nc.alloc_sbuf_tensor(name, list(shape), dt)

    def setup(self):
        nc, s, S = self.nc, self.s, self.S
        self.x = self.sb("x", [128, NKC, S], F32)
        self.psum = [nc.alloc_psum_tensor("ps%d" % i, [128, 512], F32) for i in range(8)]
        self.wp = [self.sb("wp%d" % i, [128, 3072], BF16) for i in range(4)]
        self.ftmp = [self.sb("ft%d" % i, [128, 544], F32) for i in range(NFT)]
        self.btmp = [self.sb("bt%d" % i, [128, 512], BF16) for i in range(7)]
        self.ffree = list(range(NFT))
        self.bfree = list(range(7))
        NB = S // 128
        mix_elems = 8 * S + NB * 8 * 65 + 4096 + 5 * 2048 + 4096 + 4 * 544 + 64
        ffn_elems = 8 * 1024 + 22 * 1024
        self.arena = self.sb("arena", [128, max(mix_elems, ffn_elems)], BF16)
        a = self.arena
        self.h = a[:, 0:8192].rearrange("p (k t) -> p k t", k=8)
        self.mid = a[:, 8192:8192 + 22528].rearrange("p (f t) -> p f t", f=22)
        o = 0
        def carve(n):
            nonlocal o
            v = a[:, o:o + n]
            o += n
            return v
        self.K = carve(8 * S).rearrange("p (h t) -> p h t", h=8)
        self.V = carve(NB * 8 * 65).rearrange("p (b h d) -> p b h d", b=NB, h=8)
        self.hm = carve(4096).rearrange("p (k t) -> p k t", k=8)
        self.m_a = carve(2048).rearrange("p (k t) -> p k t", k=4)
        self.cs = carve(2048).rearrange("p (k t) -> p k t", k=4)
        self.O_sb = carve(2048).rearrange("p (q f) -> p q f", q=4)
        self.OT = carve(2048).rearrange("p (k t) -> p k t", k=4)
        ccv = carve(2048)
        self.cc = ccv.rearrange("p (k t) -> p k t", k=4)
        self.ckvn = ccv[:, 0:1024].rearrange("p (k t) -> p k t", k=2)
        self.cbb = carve(4 * 544).rearrange("p (k t) -> p k t", k=4)
        mgv = carve(4096)
        self.merged = mgv.rearrange("p (k t) -> p k t", k=8)
        self.cqn = mgv[:, 0:1536].rearrange("p (k t) -> p k t", k=3)
        self.ones_bf = self.sb("ones_bf", [128, 128], BF16)
        self.ident = self.sb("ident", [128, 128], BF16)
        self.tri = self.sb("tri", [128, 128], BF16)
        self.epsb = self.sb("epsb", [128, 1], F32)
        self.oneb = self.sb("oneb", [128, 1], F32)
        self.vecs = [self.sb("svecs%d" % l, [128, NVEC], F32) for l in range(self.L)]
        self.glob = self.sb("glob", [128, NGL], F32)
        self.gw = self.sb("gw", [128, 8, 128], BF16)
        self.halo_a = self.sb("halo_a", [128, 4, 3], F32)
        self.state = self.sb("state", [128, 4], F32)
        self.nsp = self.sb("nsp", [128, 8], F32)
        self.rec = self.sb("rec", [128, 4], F32)
        s.add("pool", lambda e: e.memset(self.ones_bf[:], 1.0), writes=[("ones_bf",)])
        s.add("pool", lambda e: e.memset(self.oneb[:], 1.0), writes=[("oneb",)])
        s.add("pool", lambda e: e.memset(self.epsb[:], EPS), writes=[("epsb",)])

    def load_consts(self):
        s = self.s
        dr_ident = self.din("c_ident", [128, 128])
        dr_tri = self.din("c_tri", [128, 128])
        dr_glob = self.din("c_glob", [128, NGL])
        self.dr_pos = self.din("pos_rep", [96, self.S], I32)
        s.add("pool", lambda e: e.dma_start(out=self.ident[:, :], in_=dr_ident[:, :]),
              writes=[("ident",)], dma=True, semkey=("c", 0))
        s.add("pool", lambda e: e.dma_start(out=self.tri[:, :], in_=dr_tri[:, :]),
              writes=[("tri",)], dma=True, semkey=("c", 1))
        s.add("sp", lambda e: e.dma_start(out=self.glob[:, :], in_=dr_glob[:, :]),
              writes=[("glob",)], dma=True, semkey=("c", 2))
        self.dr_vecs = []
        for l in range(self.L):
            dv = self.din("vecs%d" % l, [128, NVEC])
            self.dr_vecs.append(dv)
            s.add("sp", lambda e, l=l, dv=dv: e.dma_start(out=self.vecs[l][:, :], in_=dv[:, :]),
                  writes=[("vecs", l)], dma=True, semkey=("vecs", l))

    def vcol(self, l, name, j=0, n=1):
        c = _VC[name] + j
        return self.vecs[l][:, c:c + n]

    def ft(self):
        i = self.ffree.pop(0)
        return i, self.ftmp[i], ("ft", i)

    def ffr(self, i):
        self.ffree.append(i)

    def bt(self):
        i = self.bfree.pop(0)
        return i, self.btmp[i], ("bt", i)

    def bfr(self, i):
        self.bfree.append(i)

    def next_psum(self):
        i = self.psum_rr % self.psum_n
        self.psum_rr = (i + 1) % self.psum_n
        return i

    def wload(self, dmas):
        s = self.s
        b = self.wp_rr
        while b in self.wp_pinned:
            b = (b + 1) % 4
        self.wp_rr = (b + 1) % 4
        self.wp_last = b
        buf = self.wp[b]
        for n, (dstf, src) in enumerate(dmas):
            s.add("pool", lambda e, dstf=dstf, src=src, buf=buf: e.dma_start(out=dstf(buf), in_=src),
                  writes=[("wp", b, n)], dma=True, semkey=("wp", b, n))
        return buf, [("wp", b, n) for n in range(len(dmas))]

    def load_x(self, xT):
        s = self.s
        for c in range(NKC):
            s.add("sp", lambda e, c=c: e.dma_start(out=self.x[:, c, :], in_=xT[c * 128:(c + 1) * 128, :]),
                  writes=[("x", c, t) for t in range(self.NTC)], dma=True, semkey=("xload", c))

    def store_x(self, outT):
        s = self.s
        for c in range(NKC):
            s.add("sp", lambda e, c=c: e.dma_start(out=outT[c * 128:(c + 1) * 128, :], in_=self.x[:, c, :]),
                  reads=[("x", c, t) for t in range(self.NTC)], dma=True, semkey=("xstore",))

    def rstd_from(self, srcs, src_toks, n_feat):
        s = self.s
        pb = self.next_psum()
        ps = self.psum[pb]
        nk = len(srcs)
        for k, (ap, tok) in enumerate(zip(srcs, src_toks)):
            bi, bq, btok = self.bt()
            s.add("act", lambda e, ap=ap, bq=bq: e.activation(out=bq[:, :], in_=ap, func=AF.Square),
                  reads=[tok], writes=[btok])
            s.add("pe", lambda e, k=k, bq=bq: e.matmul(ps[:, :], lhsT=self.ones_bf[:, :], rhs=bq[:, :],
                                                        start=(k == 0), stop=(k == nk - 1)),
                  reads=[btok, ("ones_bf",)], writes=[("ps", pb)])
            self.bfr(bi)
        fi, r, rtok = self.ft()
        s.add("act", lambda e: e.activation(out=r[:, 0:TC], in_=ps[:, :], func=AF.Ln,
                                            bias=self.epsb[:, 0:1], scale=1.0 / n_feat),
              reads=[("ps", pb), ("epsb",)], writes=[rtok])
        s.add("act", lambda e: e.activation(out=r[:, 0:TC], in_=r[:, 0:TC], func=AF.Exp, scale=-0.5), reads=[rtok], writes=[rtok])
        return fi, r, rtok

    def rmsnorm_chunk(self, tg, gain, gtok, h_out, hcol0, htoks):
        s = self.s
        t0 = tg * TC
        fi, r, rtok = self.rstd_from([self.x[:, k, t0:t0 + TC] for k in range(NKC)],
                                     [("x", k, tg) for k in range(NKC)], D)
        for k in range(NKC):
            s.add("dve", lambda e, k=k: e.scalar_tensor_tensor(
                out=h_out[:, k, hcol0:hcol0 + TC], in0=self.x[:, k, t0:t0 + TC], scalar=gain[:, k:k + 1],
                in1=r[:, 0:TC], op0=ALU.mult, op1=ALU.mult),
                reads=[("x", k, tg), rtok, gtok], writes=[htoks[k]])
        self.ffr(fi)

    def ffn(self, l, which, w1r, w2):
        s = self.s
        TH = min(1024, self.S)
        nloc = TH // TC
        gain = self.vcol(l, which, 0, 8)
        gtok = ("vecs", l)
        w1v = w1r.rearrange("(kc p) f n -> p kc f n", p=128)
        w2v = w2.rearrange("(fc p) n -> p fc n", p=128)
        for half in range(self.S // TH):
            for tcl in range(nloc):
                tg = half * nloc + tcl
                self.rmsnorm_chunk(tg, gain, gtok, self.h, tcl * TC, [("h", tcl)] * 8)
            for f in range(NFC):
                buf, wt = self.wload([(lambda b: b[:, 0:2048].rearrange("p (k n) -> p k n", k=8), w1v[:, :, f, :])])
                wv = buf[:, 0:2048].rearrange("p (k n) -> p k n", k=8)
                for tcl in range(nloc):
                    pg = self.next_psum()
                    pu = self.next_psum()
                    for k in range(NKC):
                        s.add("pe", lambda e, k=k, wv=wv, pg=pg, tcl=tcl: e.matmul(
                            self.psum[pg][:, :], lhsT=wv[:, k, 0:128], rhs=self.h[:, k, tcl * TC:(tcl + 1) * TC],
                            start=(k == 0), stop=(k == NKC - 1)),
                            reads=wt + [("h", tcl)], writes=[("ps", pg)])
                    for k in range(NKC):
                        s.add("pe", lambda e, k=k, wv=wv, pu=pu, tcl=tcl: e.matmul(
                            self.psum[pu][:, :], lhsT=wv[:, k, 128:256], rhs=self.h[:, k, tcl * TC:(tcl + 1) * TC],
                            start=(k == 0), stop=(k == NKC - 1)),
                            reads=wt + [("h", tcl)], writes=[("ps", pu)])
                    fi, sg, sgt = self.ft()
                    s.add("act", lambda e, sg=sg, pg=pg: e.activation(out=sg[:, 0:TC], in_=self.psum[pg][:, :], func=AF.Silu),
                          reads=[("ps", pg)], writes=[sgt])
                    s.add("dve", lambda e, sg=sg, pu=pu, f=f, tcl=tcl: e.tensor_tensor(
                        out=self.mid[:, f, tcl * TC:(tcl + 1) * TC], in0=sg[:, 0:TC], in1=self.psum[pu][:, :], op=ALU.mult),
                        reads=[sgt, ("ps", pu)], writes=[("mid", f, tcl)])
                    self.ffr(fi)
            for o in range(NKC):
                buf, wt = self.wload([(lambda b: b[:, 0:2816].rearrange("p (f n) -> p f n", f=22), w2v[:, :, o * 128:(o + 1) * 128])])
                wv = buf[:, 0:2816].rearrange("p (f n) -> p f n", f=22)
                for tcl in range(nloc):
                    tg = half * nloc + tcl
                    py = self.next_psum()
                    for f in range(NFC):
                        s.add("pe", lambda e, f=f, wv=wv, py=py, tcl=tcl: e.matmul(
                            self.psum[py][:, :], lhsT=wv[:, f, :], rhs=self.mid[:, f, tcl * TC:(tcl + 1) * TC],
                            start=(f == 0), stop=(f == NFC - 1)),
                            reads=wt + [("mid", f, tcl)], writes=[("ps", py)])
                    s.add("dve", lambda e, o=o, py=py, tg=tg: e.scalar_tensor_tensor(
                        out=self.x[:, o, tg * TC:(tg + 1) * TC], in0=self.psum[py][:, :], scalar=0.5,
                        in1=self.x[:, o, tg * TC:(tg + 1) * TC], op0=ALU.mult, op1=ALU.add),
                        reads=[("ps", py), ("x", o, tg)], writes=[("x", o, tg)])

    def final_norm(self):
        for tg in range(self.NTC):
            self.rmsnorm_chunk(tg, self.glob[:, 0:8], ("glob",), self.x, tg * TC, [("x", k, tg) for k in range(NKC)])

    def proj(self, lhs_fn, nk, rhs_fn, rtoks, wtoks, M=128, N=TC, pb=None):
        s = self.s
        if pb is None:
            pb = self.next_psum()
        for k in range(nk):
            lt = lhs_fn(k)
            rt = rhs_fn(k)
            s.add("pe", lambda e: e.matmul(self.psum[pb][0:M, 0:N], lhsT=lt, rhs=rt,
                                           start=(k == 0), stop=(k == nk - 1)),
                  reads=list(wtoks) + list(rtoks), writes=[("ps", pb)])
        return pb

    def mixer_layer_setup(self, l, gwd):
        s = self.s
        s.add("pool", lambda e: e.dma_start(out=self.gw[:, :, :], in_=gwd.rearrange("p (a b) -> p a b", a=8)),
              writes=[("gw",)], dma=True, semkey=("gw",))
        lam = self.vcol(l, "lam", 0, 4)
        s.add("act", lambda e: e.activation(out=self.nsp[:, 0:4], in_=lam, func=AF.Exp, scale=-1.0),
              reads=[("vecs", l)], writes=[("nsp",)])
        s.add("act", lambda e: e.activation(out=self.nsp[:, 0:4], in_=self.nsp[:, 0:4], func=AF.Ln, bias=1.0),
              reads=[("nsp",)], writes=[("nsp",)])
        s.add("dve", lambda e: e.tensor_scalar(out=self.nsp[:, 4:8], in0=self.nsp[:, 0:4], scalar1=-16.0, scalar2=None, op0=ALU.mult),
              reads=[("nsp",)], writes=[("nsp2",)])
        s.add("dve", lambda e: e.tensor_scalar(out=self.nsp[:, 0:4], in0=self.nsp[:, 0:4], scalar1=-8.0, scalar2=None, op0=ALU.mult),
              reads=[("nsp",), ("nsp2",)], writes=[("nsp",)])
        s.add("dve", lambda e: e.memset(self.halo_a[:], 0.0), writes=[("halo_a", j) for j in range(4)])
        s.add("dve", lambda e: e.memset(self.cbb[:, :, :], 0.0), writes=[("cbb", j) for j in range(4)])
        s.add("dve", lambda e: e.memset(self.state[:], 0.0), writes=[("state", j) for j in range(4)])
        s.add("dve", lambda e: e.memset(self.V[:, :, :, 64:65], 1.0), writes=[("Vones",)])

    def mixer_chunk(self, l, tg, W):
        s = self.s
        t0 = tg * TC
        vt = ("vecs", l)
        hm = self.hm
        hmt = [("hm", k) for k in range(8)]
        self.rmsnorm_chunk(tg, self.vcol(l, "mix_norm", 0, 8), vt, hm, 0, hmt)
        hm_rhs = lambda k: hm[:, k, :]
        v8 = lambda b, n: b[:, 0:8 * n].rearrange("p (k n) -> p k n", k=8)
        winA = W["winA"].rearrange("(kc p) j n -> p kc j n", p=128)
        winC = W["winC"].rearrange("(kc p) j n -> p kc j n", p=128)
        winB = W["winB"].rearrange("(kc p) n -> p kc n", p=128)
        wkpe = W["wkpe"].rearrange("(kc p) n -> p kc n", p=128)
        winG = W["winG"].rearrange("(kc p) o n -> p kc o n", p=128)
        wbr = W["wbr"].rearrange("(kc p) o n -> p kc o n", p=128)
        wq = W["wq"].rearrange("(kc p) n -> p kc n", p=128)
        wukv = W["wukv"].rearrange("(kc p) n -> p kc n", p=128)
        wout = W["wout"].rearrange("(kc p) n -> p kc n", p=128)

        def chainA(j):
            buf, wt = self.wload([(lambda b: v8(b, 256), winA[:, :, j, :])])
            wv = v8(buf, 256)
            pxa = self.proj(lambda k: wv[:, k, 0:128], 8, hm_rhs, hmt, wt)
            i_in, tin, tint = self.ft()
            s.add("act", lambda e: e.activation(out=tin[:, 30:542], in_=self.psum[pxa][:, :], func=AF.Identity, bias=self.vcol(l, "b_xa", j)),
                  reads=[("ps", pxa), vt], writes=[tint])
            pga = self.proj(lambda k: wv[:, k, 128:256], 8, hm_rhs, hmt, wt)
            i_g, xg, xgt = self.ft()
            s.add("act", lambda e: e.activation(out=xg[:, 0:TC], in_=self.psum[pga][:, :], func=AF.Identity, bias=self.vcol(l, "b_ga", j)),
                  reads=[("ps", pga), vt], writes=[xgt])
            s.add("dve", lambda e: e.tensor_copy(out=tin[:, 27:30], in_=self.halo_a[:, j, :]), reads=[("halo_a", j)], writes=[tint])
            yield
            i_q, qg, qgt = self.ft()
            s.add("act", lambda e: e.activation(out=qg[:, 0:TC], in_=xg[:, 0:TC], func=AF.Square), reads=[xgt], writes=[qgt])
            yield
            s.add("dve", lambda e: e.tensor_scalar(out=qg[:, 0:TC], in0=qg[:, 0:TC], scalar1=0.044715, scalar2=1.0, op0=ALU.mult, op1=ALU.add),
                  reads=[qgt], writes=[qgt])
            yield
            s.add("dve", lambda e: e.tensor_tensor(out=qg[:, 0:TC], in0=qg[:, 0:TC], in1=xg[:, 0:TC], op=ALU.mult), reads=[qgt, xgt], writes=[qgt])
            yield
            i_xa, xa, xat = self.ft()
            cw = lambda tap: self.vcol(l, "lru_cw", j * 4 + tap)
            s.add("dve", lambda e: e.tensor_scalar(out=xa[:, 0:TC], in0=tin[:, 27:27 + TC], scalar1=cw(0), scalar2=self.vcol(l, "lru_cb", j),
                                                   op0=ALU.mult, op1=ALU.add), reads=[tint, vt], writes=[xat])
            yield
            for tap in range(1, 4):
                s.add("dve", lambda e: e.scalar_tensor_tensor(out=xa[:, 0:TC], in0=tin[:, 27 + tap:27 + tap + TC], scalar=cw(tap), in1=xa[:, 0:TC],
                                                              op0=ALU.mult, op1=ALU.add), reads=[tint, xat, vt], writes=[xat])
                yield
            s.add("dve", lambda e: e.tensor_copy(out=self.halo_a[:, j, :], in_=tin[:, 539:542]), reads=[tint], writes=[("halo_a", j)])
            self.ffr(i_in)
            ib, xab, xabt = self.bt()
            s.add("act", lambda e: e.activation(out=xab[:, :], in_=xa[:, 0:TC], func=AF.Identity), reads=[xat], writes=[xabt])
            pr = self.proj(lambda k: self.gw[:, 2 * j, :], 1, lambda k: xab[:, :], [xabt], [("gw",)])
            i_r, r, rt = self.ft()
            s.add("act", lambda e: e.activation(out=r[:, 0:TC], in_=self.psum[pr][:, :], func=AF.Sigmoid, bias=self.vcol(l, "b_r", j)),
                  reads=[("ps", pr), vt], writes=[rt])
            pi_ = self.proj(lambda k: self.gw[:, 2 * j + 1, :], 1, lambda k: xab[:, :], [xabt], [("gw",)])
            i_i, ii, it = self.ft()
            s.add("act", lambda e: e.activation(out=ii[:, 0:TC], in_=self.psum[pi_][:, :], func=AF.Sigmoid, bias=self.vcol(l, "b_i", j)),
                  reads=[("ps", pi_), vt], writes=[it])
            self.bfr(ib)
            s.add("act", lambda e: e.activation(out=qg[:, 0:TC], in_=qg[:, 0:TC], func=AF.Sigmoid, scale=GELU_K), reads=[qgt], writes=[qgt])
            yield
            s.add("dve", lambda e: e.tensor_tensor(out=qg[:, 0:TC], in0=qg[:, 0:TC], in1=xg[:, 0:TC], op=ALU.mult), reads=[qgt, xgt], writes=[qgt])
            self.ffr(i_g)
            yield
            i_a2, a2, a2t = self.ft()
            s.add("act", lambda e: e.activation(out=a2[:, 0:TC], in_=r[:, 0:TC], func=AF.Exp, scale=self.nsp[:, 4 + j:5 + j]), reads=[rt, ("nsp2",)], writes=[a2t])
            s.add("act", lambda e: e.activation(out=r[:, 0:TC], in_=r[:, 0:TC], func=AF.Exp, scale=self.nsp[:, j:j + 1]), reads=[rt, a2t, ("nsp",)], writes=[rt])
            a, at, i_a = r, rt, i_r
            s.add("dve", lambda e: e.tensor_tensor(out=ii[:, 0:TC], in0=ii[:, 0:TC], in1=xa[:, 0:TC], op=ALU.mult), reads=[it, xat], writes=[it])
            yield
            s.add("act", lambda e: e.activation(out=a2[:, 0:TC], in_=a2[:, 0:TC], func=AF.Ln, scale=-1.0, bias=self.oneb[:, 0:1]), reads=[a2t, ("oneb",)], writes=[a2t])
            s.add("act", lambda e: e.activation(out=a2[:, 0:TC], in_=a2[:, 0:TC], func=AF.Exp, scale=0.5), reads=[a2t], writes=[a2t])
            yield
            s.add("dve", lambda e: e.tensor_tensor(out=ii[:, 0:TC], in0=ii[:, 0:TC], in1=a2[:, 0:TC], op=ALU.mult), reads=[it, a2t], writes=[it])
            self.ffr(i_xa)
            self.ffr(i_a2)
            yield
            i_h, hl, hlt = self.ft()
            s.add("dve", lambda e: e.tensor_tensor_scan(out=hl[:, 0:TC], data0=a[:, 0:TC], data1=ii[:, 0:TC], initial=self.state[:, j:j + 1],
                                                        op0=ALU.mult, op1=ALU.add), reads=[at, it, ("state", j)], writes=[hlt])
            s.add("act", lambda e: e.activation(out=self.state[:, j:j + 1], in_=hl[:, TC - 1:TC], func=AF.Identity), reads=[hlt], writes=[("state", j)])
            self.ffr(i_a)
            self.ffr(i_i)
            yield
            s.add("dve", lambda e: e.tensor_tensor(out=self.m_a[:, j, :], in0=qg[:, 0:TC], in1=hl[:, 0:TC], op=ALU.mult), reads=[qgt, hlt], writes=[("m_a", j)])
            self.ffr(i_q)
            self.ffr(i_h)

        def all_chains():
            for j in range(4):
                yield from chainA(j)
        agen = all_chains()

        def a_steps(n):
            for _ in range(n):
                try:
                    next(agen)
                except StopIteration:
                    return

        R = slice(64, 96)
        glob = self.glob
        i_ang, ang, angt = self.ft()
        i_cos, cos, cost = self.ft()
        i_sin, sin, sint = self.ft()
        i_msk, msk, mskt = self.ft()
        s.add("sp", lambda e: e.dma_start(out=msk[R, 0:TC].bitcast(I32), in_=self.dr_pos[R, t0:t0 + TC]),
              writes=[mskt], dma=True, semkey=("pos", i_msk))
        s.add("dve", lambda e: e.tensor_copy(out=ang[R, 0:TC], in_=msk[R, 0:TC].bitcast(I32)), reads=[mskt], writes=[angt])
        s.add("dve", lambda e: e.tensor_scalar(out=ang[R, 0:TC], in0=ang[R, 0:TC], scalar1=glob[R, 8:9], scalar2=None, op0=ALU.mult),
              reads=[angt, ("glob",)], writes=[angt])
        for dst, dtok, shift, sc in ((sin, sint, 0.0, glob[R, 9:10]), (cos, cost, PI / 2, 1.0)):
            ki = msk[R, 0:TC].bitcast(I32)
            s.add("dve", lambda e: e.tensor_scalar(out=dst[R, 0:TC], in0=ang[R, 0:TC], scalar1=shift, scalar2=1.0 / (2 * PI),
                                                   op0=ALU.add, op1=ALU.mult), reads=[angt], writes=[dtok])
            s.add("dve", lambda e: e.tensor_copy(out=ki, in_=dst[R, 0:TC]), reads=[dtok], writes=[mskt])
            s.add("dve", lambda e: e.tensor_copy(out=dst[R, 0:TC], in_=ki), reads=[mskt], writes=[dtok])
            s.add("dve", lambda e: e.scalar_tensor_tensor(out=dst[R, 0:TC], in0=dst[R, 0:TC], scalar=-2 * PI, in1=ang[R, 0:TC],
                                                          op0=ALU.mult, op1=ALU.add), reads=[dtok, angt], writes=[dtok])
            s.add("dve", lambda e: e.tensor_scalar(out=dst[R, 0:TC], in0=dst[R, 0:TC], scalar1=shift, scalar2=None, op0=ALU.add),
                  reads=[dtok], writes=[dtok])
            s.add("dve", lambda e: e.tensor_scalar(out=msk[R, 0:TC], in0=dst[R, 0:TC], scalar1=PI, scalar2=-2 * PI,
                                                   op0=ALU.is_gt, op1=ALU.mult), reads=[dtok], writes=[mskt])
            s.add("dve", lambda e: e.tensor_tensor(out=dst[R, 0:TC], in0=dst[R, 0:TC], in1=msk[R, 0:TC], op=ALU.add),
                  reads=[dtok, mskt], writes=[dtok])
            s.add("dve", lambda e: e.tensor_scalar(out=dst[R, 0:TC], in0=dst[R, 0:TC], scalar1=-PI, scalar2=PI,
                                                   op0=ALU.max, op1=ALU.min), reads=[dtok], writes=[dtok])
            s.add("act", lambda e: e.activation(out=dst[R, 0:TC], in_=dst[R, 0:TC], func=AF.Sin, scale=sc),
                  reads=[dtok, ("glob",)], writes=[dtok])
        self.ffr(i_ang)
        self.ffr(i_msk)

        def latent(src, col0, n, bname, gname, dst, dname):
            buf, wt = self.wload([(lambda b: v8(b, n * 128), src[:, :, col0:col0 + n * 128])])
            wv = v8(buf, n * 128)
            tmps = []
            for i in range(n):
                p = self.proj(lambda k: wv[:, k, i * 128:(i + 1) * 128], 8, hm_rhs, hmt, wt)
                fi, t, tt = self.ft()
                s.add("act", lambda e: e.activation(out=t[:, 0:TC], in_=self.psum[p][:, :], func=AF.Identity,
                                                    bias=self.vcol(l, bname, i)), reads=[("ps", p), vt], writes=[tt])
                tmps.append((fi, t, tt))
            ri, r, rtok = self.rstd_from([t[:, 0:TC] for _, t, _ in tmps], [tt for _, _, tt in tmps], n * 128)
            for i, (fi, t, tt) in enumerate(tmps):
                s.add("dve", lambda e: e.scalar_tensor_tensor(out=dst[:, i, :], in0=t[:, 0:TC], scalar=self.vcol(l, gname, i),
                                                              in1=r[:, 0:TC], op0=ALU.mult, op1=ALU.mult),
                      reads=[tt, rtok, vt], writes=[(dname, i), (("merged", i) if dname == "cqn" else ("cc", i))])
                self.ffr(fi)
            self.ffr(ri)
        latent(winB, 0, 3, "b_cq", "q_norm", self.cqn, "cqn")
        latent(winB, 384, 2, "b_ckv", "kv_norm", self.ckvn, "ckvn")
        cqt = [("cqn", i) for i in range(3)]
        ckvt = [("ckvn", i) for i in range(2)]

        buf, wt = self.wload([(lambda b: v8(b, 192), wkpe[:, :, :])])
        wv = v8(buf, 192)
        pn = self.proj(lambda k: wv[:, k, 0:96], 8, hm_rhs, hmt, wt, M=96)
        psw = self.proj(lambda k: wv[:, k, 96:192], 8, hm_rhs, hmt, wt, M=96)
        i_kn, kn, knt = self.ft()
        i_ks, ks, kst = self.ft()
        s.add("act", lambda e: e.activation(out=kn[R, 0:TC], in_=self.psum[pn][R, :], func=AF.Identity,
                                            bias=self.vecs[l][R, _VC["b_kpe"]:_VC["b_kpe"] + 1]), reads=[("ps", pn), vt], writes=[knt])
        s.add("act", lambda e: e.activation(out=ks[R, 0:TC], in_=self.psum[psw][R, :], func=AF.Identity,
                                            bias=self.vecs[l][R, _VC["b_kpe"] + 1:_VC["b_kpe"] + 2]), reads=[("ps", psw), vt], writes=[kst])
        s.add("dve", lambda e: e.tensor_tensor(out=kn[R, 0:TC], in0=kn[R, 0:TC], in1=cos[R, 0:TC], op=ALU.mult), reads=[knt, cost], writes=[knt])
        s.add("dve", lambda e: e.tensor_tensor(out=ks[R, 0:TC], in0=ks[R, 0:TC], in1=sin[R, 0:TC], op=ALU.mult), reads=[kst, sint], writes=[kst])
        i_kr, kr, krt = self.bt()
        s.add("dve", lambda e: e.tensor_tensor(out=kr[R, :], in0=kn[R, 0:TC], in1=ks[R, 0:TC], op=ALU.add), reads=[knt, kst], writes=[krt])
        s.add("dve", lambda e: e.tensor_copy(out=self.K[R, :, t0:t0 + TC], in_=kr[R, :].unsqueeze(1).broadcast_to([32, 8, TC])),
              reads=[krt], writes=[("Kpe", tg)])
        self.ffr(i_kn)
        self.ffr(i_ks)
        self.bfr(i_kr)

        buf, wt = self.wload([(lambda b: b[:, 0:2048].rearrange("p (k n) -> p k n", k=2), wukv[:, :, :])])
        wv = buf[:, 0:2048].rearrange("p (k n) -> p k n", k=2)
        for h in range(8):
            pk = self.proj(lambda k: wv[:, k, h * 128:h * 128 + 64], 2, lambda k: self.ckvn[:, k, :], ckvt, wt, M=64)
            eng = "act" if h % 2 == 0 else "dve"
            if eng == "act":
                s.add("act", lambda e: e.activation(out=self.K[0:64, h, t0:t0 + TC], in_=self.psum[pk][0:64, :], func=AF.Identity),
                      reads=[("ps", pk)], writes=[("K", h, tg)])
            else:
                s.add("dve", lambda e: e.tensor_copy(out=self.K[0:64, h, t0:t0 + TC], in_=self.psum[pk][0:64, :]),
                      reads=[("ps", pk)], writes=[("K", h, tg)])
        wvv = buf[:, 0:2048].rearrange("p (k h two d) -> p k h two d", k=2, h=8, two=2)
        for blk in range(4):
            gb = tg * 4 + blk
            pb = self.next_psum()
            for k in range(2):
                s.add("pe", lambda e: e.matmul(self.psum[pb][:, :].rearrange("p (h d) -> p h d", h=8),
                                               lhsT=self.ckvn[:, k, blk * 128:(blk + 1) * 128], rhs=wvv[:, k, :, 1, :],
                                               start=(k == 0), stop=(k == 1)), reads=ckvt + wt, writes=[("ps", pb)])
            s.add("act", lambda e: e.activation(out=self.V[:, gb, :, 0:64], in_=self.psum[pb][:, :].rearrange("p (h d) -> p h d", h=8), func=AF.Identity),
                  reads=[("ps", pb)], writes=[("V", gb)])

        cbb = self.cbb
        for j in range(4):
            buf, wt = self.wload([(lambda b: v8(b, 256), winC[:, :, j, :])])
            wv = v8(buf, 256)
            pv = self.proj(lambda k: wv[:, k, 0:128], 8, hm_rhs, hmt, wt)
            pg = self.proj(lambda k: wv[:, k, 128:256], 8, hm_rhs, hmt, wt)
            i_s, sg, sgt = self.ft()
            s.add("act", lambda e: e.activation(out=sg[:, 0:TC], in_=self.psum[pg][:, :], func=AF.Sigmoid, bias=self.vcol(l, "b_cg", j)),
                  reads=[("ps", pg), vt], writes=[sgt])
            s.add("dve", lambda e: e.tensor_copy(out=cbb[:, j, 0:30], in_=cbb[:, j, 512:542]), reads=[("cbb", j)], writes=[("cbb", j)])
            s.add("dve", lambda e: e.scalar_tensor_tensor(out=cbb[:, j, 30:542], in0=self.psum[pv][:, :], scalar=self.vcol(l, "b_cv", j), in1=sg[:, 0:TC],
                                                          op0=ALU.add, op1=ALU.mult), reads=[("ps", pv), sgt, vt, ("cbb", j)], writes=[("cbb", j)])
            self.ffr(i_s)

        dwd = W["dwd"]
        conv_state = {}

        def conv_load(pc_):
            j, half = pc_ // 2, pc_ % 2
            c0, c1 = (0, 2048) if half == 0 else (2048, 3968)
            buf, wt = self.wload([(lambda b: b[:, 0:c1 - c0], dwd[:, j, c0:c1])])
            conv_state[pc_] = (buf, wt, self.wp_last)
            self.wp_pinned.add(self.wp_last)

        def conv_mm(pc_):
            j, half = pc_ // 2, pc_ % 2
            buf, wt, wpi = conv_state.pop(pc_)
            self.wp_pinned.discard(wpi)
            taps = range(0, 16) if half == 0 else range(16, 31)
            for tap in taps:
                lt = buf[:, (tap - taps[0]) * 128:(tap - taps[0] + 1) * 128]
                s.add("pe", lambda e: e.matmul(self.psum[5][:, :], lhsT=lt, rhs=cbb[:, j, tap:tap + TC], start=(tap == 0), stop=(tap == 30)),
                      reads=wt + [("cbb", j)], writes=[("ps", 5)])
            if half == 1:
                s.add("act", lambda e: e.activation(out=self.cc[:, j, :], in_=self.psum[5][:, :], func=AF.Identity, bias=self.vcol(l, "dw_b", j)),
                      reads=[("ps", 5), vt], writes=[("cc", j)] + ([("ckvn", j)] if j < 2 else []))

        wq_state = {}

        def prologue(h):
            if h % 4 == 0:
                g = h // 4
                bufq, wtq = self.wload([(lambda b: b[:, 0:1536].rearrange("p (k n) -> p k n", k=3), wq[:, :, g * 512:(g + 1) * 512])])
                wq_state[g] = (bufq[:, 0:1536].rearrange("p (k n) -> p k n", k=3), wtq, self.wp_last)
                self.wp_pinned.add(self.wp_last)
            wvq, wtq, wpi = wq_state[h // 4]
            c0 = (h % 4) * 128
            pqn = self.proj(lambda k: wvq[:, k, c0:c0 + 96], 3, lambda k: self.cqn[:, k, :], cqt, wtq, M=96)
            pqs = self.proj(lambda k: wvq[:, k, c0 + 32:c0 + 128], 3, lambda k: self.cqn[:, k, :], cqt, wtq, M=96)
            if h % 4 == 3:
                self.wp_pinned.discard(wpi)
            i_q, Qh, Qt = self.bt()
            i_t1, t1, t1t = self.ft()
            i_t2, t2, t2t = self.ft()
            s.add("act", lambda e: e.activation(out=Qh[0:64, :], in_=self.psum[pqn][0:64, :], func=AF.Identity), reads=[("ps", pqn)], writes=[Qt])
            s.add("dve", lambda e: e.tensor_tensor(out=t1[R, 0:TC], in0=self.psum[pqn][R, :], in1=cos[R, 0:TC], op=ALU.mult),
                  reads=[("ps", pqn), cost], writes=[t1t])
            s.add("dve", lambda e: e.tensor_tensor(out=t2[R, 0:TC], in0=self.psum[pqs][R, :], in1=sin[R, 0:TC], op=ALU.mult),
                  reads=[("ps", pqs), sint], writes=[t2t])
            s.add("dve", lambda e: e.tensor_tensor(out=Qh[R, :], in0=t1[R, 0:TC], in1=t2[R, 0:TC], op=ALU.add), reads=[t1t, t2t, Qt], writes=[Qt])
            self.ffr(i_t1)
            self.ffr(i_t2)
            return i_q, Qh, Qt

        def attention(h, i_q, Qh, Qt):
            pO = 6 + (h % 2)
            nkb = 4 * tg + 4
            live = {}

            def emit_scores(kb):
                kl = kb - 4 * tg
                q0 = max(kl, 0) * 128
                psn = self.next_psum()
                s.add("pe", lambda e: e.matmul(self.psum[psn][:, q0:512], lhsT=self.K[0:96, h, kb * 128:(kb + 1) * 128],
                                               rhs=Qh[0:96, q0:512], start=True, stop=True),
                      reads=[("K", h, kb // 4), ("Kpe", kb // 4), Qt], writes=[("ps", psn)])
                i_p, pT, pTt = self.bt()
                s.add("act", lambda e: e.activation(out=pT[:, q0:512], in_=self.psum[psn][:, q0:512], func=AF.Exp, scale=SCALE),
                      reads=[("ps", psn)], writes=[pTt])
                if kl >= 0:
                    s.add("pool", lambda e: e.tensor_tensor(out=pT[:, q0:q0 + 128], in0=pT[:, q0:q0 + 128], in1=self.tri[:, :], op=ALU.mult),
                          reads=[pTt, ("tri",)], writes=[pTt])
                live[kb] = (i_p, pT, pTt)

            def emit_pv(kb):
                kl = kb - 4 * tg
                i_p, pT, pTt = live.pop(kb)
                for qi in range(max(kl, 0), 4):
                    s.add("pe", lambda e: e.matmul(self.psum[pO][:, qi * 65:(qi + 1) * 65], lhsT=pT[:, qi * 128:(qi + 1) * 128],
                                                   rhs=self.V[:, kb, h, :], start=(kb == 0 and qi == 0), stop=(kb == 4 * tg + qi),
                                                   skip_group_check=True),
                          reads=[pTt, ("V", kb), ("Vones",)], writes=[("ps", pO)])
                self.bfr(i_p)
            LA = 3
            for kb in range(min(LA, nkb)):
                emit_scores(kb)
            for kb in range(nkb):
                if kb + LA < nkb:
                    emit_scores(kb + LA)
                emit_pv(kb)
            self.bfr(i_q)
            s.add("dve", lambda e: e.reciprocal(out=self.rec[:, 0:4], in_=self.psum[pO][:, 0:260].rearrange("p (q d) -> p q d", d=65)[:, :, 64]),
                  reads=[("ps", pO)], writes=[("rec",)])
            for qi in range(4):
                s.add("dve", lambda e: e.tensor_scalar(out=self.O_sb[:, qi, h * 64:(h + 1) * 64], in0=self.psum[pO][:, qi * 65:qi * 65 + 64],
                                                       scalar1=self.rec[:, qi:qi + 1], scalar2=None, op0=ALU.mult),
                      reads=[("ps", pO), ("rec",)], writes=[("O_sb", qi, h)])

        self.psum_n = 5
        self.psum_rr %= 5
        conv_load(0)
        nxt = prologue(0)
        for h in range(8):
            cur = nxt
            if h + 1 < 8:
                conv_load(h + 1)
                nxt = prologue(h + 1)
            attention(h, *cur)
            conv_mm(h)
            a_steps(9)
        a_steps(1000)
        self.psum_n = 6
        self.ffr(i_cos)
        self.ffr(i_sin)

        p1 = self.next_psum()
        p2 = self.next_psum()
        for j in range(4):
            s.add("pe", lambda e: e.matmul(self.psum[p1][:, :], lhsT=self.ones_bf[:, :], rhs=self.cc[:, j, :],
                                           start=(j == 0), stop=(j == 3)), reads=[("cc", j), ("ones_bf",)], writes=[("ps", p1)])
        for j in range(4):
            ib, sq, sqt = self.bt()
            s.add("act", lambda e: e.activation(out=sq[:, :], in_=self.cc[:, j, :], func=AF.Square), reads=[("cc", j)], writes=[sqt])
            s.add("pe", lambda e: e.matmul(self.psum[p2][:, :], lhsT=self.ones_bf[:, :], rhs=sq[:, :],
                                           start=(j == 0), stop=(j == 3)), reads=[sqt, ("ones_bf",)], writes=[("ps", p2)])
            self.bfr(ib)
        i_m, mean, meant = self.ft()
        i_v, var, vart = self.ft()
        s.add("act", lambda e: e.activation(out=mean[:, 0:TC], in_=self.psum[p1][:, :], func=AF.Identity, scale=1.0 / 512),
              reads=[("ps", p1)], writes=[meant])
        s.add("act", lambda e: e.activation(out=var[:, 0:TC], in_=mean[:, 0:TC], func=AF.Square), reads=[meant], writes=[vart])
        s.add("dve", lambda e: e.scalar_tensor_tensor(out=var[:, 0:TC], in0=self.psum[p2][:, :], scalar=1.0 / 512, in1=var[:, 0:TC],
                                                      op0=ALU.mult, op1=ALU.subtract), reads=[("ps", p2), vart], writes=[vart])
        s.add("act", lambda e: e.activation(out=var[:, 0:TC], in_=var[:, 0:TC], func=AF.Ln, bias=self.epsb[:, 0:1]),
              reads=[vart, ("epsb",)], writes=[vart])
        s.add("act", lambda e: e.activation(out=var[:, 0:TC], in_=var[:, 0:TC], func=AF.Exp, scale=-0.5), reads=[vart], writes=[vart])
        for j in range(4):
            i_t, tt, ttt = self.ft()
            s.add("dve", lambda e: e.tensor_tensor(out=tt[:, 0:TC], in0=self.cc[:, j, :], in1=mean[:, 0:TC], op=ALU.subtract),
                  reads=[("cc", j), meant], writes=[ttt])
            s.add("dve", lambda e: e.tensor_tensor(out=tt[:, 0:TC], in0=tt[:, 0:TC], in1=var[:, 0:TC], op=ALU.mult),
                  reads=[ttt, vart], writes=[ttt])
            s.add("act", lambda e: e.activation(out=self.cs[:, j, :], in_=tt[:, 0:TC], func=AF.Silu,
                                                scale=self.vcol(l, "ln_g", j), bias=self.vcol(l, "ln_b", j)),
                  reads=[ttt, vt], writes=[("cs", j)])
            self.ffr(i_t)
        self.ffr(i_m)
        self.ffr(i_v)

        for qi in range(4):
            pb = self.next_psum()
            pbf = self.psum[pb][:, :].bitcast(BF16)
            for fc in range(4):
                s.add("pe", lambda e: e.transpose(out=pbf[:, fc * 128:(fc + 1) * 128], in_=self.O_sb[:, qi, fc * 128:(fc + 1) * 128],
                                                  identity=self.ident[:, :]),
                      reads=[("O_sb", qi, hh) for hh in range(8)] + [("ident",)], writes=[("ps", pb)])
            s.add("dve", lambda e: e.tensor_copy(out=self.OT[:, :, qi * 128:(qi + 1) * 128],
                                                 in_=pbf[:, 0:512].rearrange("p (k t) -> p k t", k=4)),
                  reads=[("ps", pb)], writes=[("OT", qi)])

        for o in range(8):
            bufG, wtG = self.wload([(lambda b: v8(b, 384), winG[:, :, o, :])])
            wvG = v8(bufG, 384)
            bufR, wtR = self.wload([(lambda b: b[:, 0:1536].rearrange("p (k n) -> p k n", k=4), wbr[:, :, o, :])])
            wvR = bufR[:, 0:1536].rearrange("p (k n) -> p k n", k=4)
            pg = [self.proj(lambda k: wvG[:, k, br * 128:(br + 1) * 128], 8, hm_rhs, hmt, wtG) for br in range(3)]
            pya = self.proj(lambda k: wvR[:, k, 0:128], 4, lambda k: self.m_a[:, k, :], [("m_a", j) for j in range(4)], wtR)
            pyb = self.proj(lambda k: wvR[:, k, 128:256], 4, lambda k: self.OT[:, k, :], [("OT", q) for q in range(4)], wtR)
            pyc = self.proj(lambda k: wvR[:, k, 256:384], 4, lambda k: self.cs[:, k, :], [("cs", j) for j in range(4)], wtR)
            gs = []
            for br in range(3):
                fi, gt_, gtok_ = self.ft()
                s.add("act", lambda e: e.activation(out=gt_[:, 0:TC], in_=self.psum[pg[br]][:, :], func=AF.Sigmoid,
                                                    bias=self.vcol(l, "b_G", br * 8 + o)), reads=[("ps", pg[br]), vt], writes=[gtok_])
                gs.append((fi, gt_, gtok_))
            (f0, g0, g0t), (f1, g1, g1t), (f2, g2, g2t) = gs
            s.add("dve", lambda e: e.tensor_tensor(out=g0[:, 0:TC], in0=g0[:, 0:TC], in1=self.psum[pya][:, :], op=ALU.mult),
                  reads=[g0t, ("ps", pya)], writes=[g0t])
            s.add("dve", lambda e: e.tensor_tensor(out=g1[:, 0:TC], in0=g1[:, 0:TC], in1=self.psum[pyb][:, :], op=ALU.mult),
                  reads=[g1t, ("ps", pyb)], writes=[g1t])
            s.add("dve", lambda e: e.scalar_tensor_tensor(out=g2[:, 0:TC], in0=self.psum[pyc][:, :], scalar=self.vcol(l, "cb_out", o),
                                                          in1=g2[:, 0:TC], op0=ALU.add, op1=ALU.mult), reads=[g2t, ("ps", pyc), vt], writes=[g2t])
            s.add("dve", lambda e: e.tensor_tensor(out=g0[:, 0:TC], in0=g0[:, 0:TC], in1=g1[:, 0:TC], op=ALU.add), reads=[g0t, g1t], writes=[g0t])
            s.add("dve", lambda e: e.tensor_tensor(out=self.merged[:, o, :], in0=g0[:, 0:TC], in1=g2[:, 0:TC], op=ALU.add),
                  reads=[g0t, g2t], writes=[("merged", o)] + ([("cqn", o)] if o < 3 else []))
            self.ffr(f0)
            self.ffr(f1)
            self.ffr(f2)
        mt = [("merged", o) for o in range(8)]
        for op_ in range(4):
            buf, wt = self.wload([(lambda b: v8(b, 256), wout[:, :, op_ * 256:(op_ + 1) * 256])])
            wv = v8(buf, 256)
            for hf in range(2):
                o2 = op_ * 2 + hf
                py = self.proj(lambda k: wv[:, k, hf * 128:(hf + 1) * 128], 8, lambda k: self.merged[:, k, :], mt, wt)
                s.add("dve", lambda e: e.tensor_tensor(out=self.x[:, o2, t0:t0 + TC], in0=self.psum[py][:, :], in1=self.x[:, o2, t0:t0 + TC], op=ALU.add),
                      reads=[("ps", py), ("x", o2, tg)], writes=[("x", o2, tg)])


_WNAMES = ["w1r_a", "w2_a", "winA", "winC", "winB", "wkpe", "winG", "wbr", "wq", "wukv", "wout", "gwd", "dwd", "w1r_b", "w2_b"]
_WSHAPES = {
    "w1r_a": [D, NFC, 256], "w2_a": [DFF, D], "w1r_b": [D, NFC, 256], "w2_b": [DFF, D],
    "winA": [D, 4, 256], "winC": [D, 4, 256], "winB": [D, 640], "wkpe": [D, 192], "winG": [D, 8, 384],
    "wbr": [512, 8, 384], "wq": [384, 1024], "wukv": [256, 1024], "wout": [D, D], "gwd": [128, 1024], "dwd": [128, 4, 31 * 128],
}


def build_program(S=2048, L=DEPTH, parts=("ffn1", "mixer", "ffn2"), final=True):
    b = Builder(S, L)
    nc = b.nc
    xT = b.din("xT", [D, S])
    outT = nc.dram_tensor("outT", [D, S], F32, kind="ExternalOutput")
    W = []
    for l in range(L):
        W.append({n: b.din("%s_%d" % (n, l), _WSHAPES[n]) for n in _WNAMES})
    b.setup()
    b.load_consts()
    b.load_x(xT)
    for l in range(L):
        if "ffn1" in parts:
            b.ffn(l, "ffn1_norm", W[l]["w1r_a"], W[l]["w2_a"])
        if "mixer" in parts:
            b.s.barrier()
            b.mixer_layer_setup(l, W[l]["gwd"])
            for tg in range(b.NTC):
                b.mixer_chunk(l, tg, W[l])
            b.s.barrier()
        if "ffn2" in parts:
            b.ffn(l, "ffn2_norm", W[l]["w1r_b"], W[l]["w2_b"])
    if final:
        b.final_norm()
    b.store_x(outT)
    b.s.emit(nc)
    return nc, b


def _fm(v):
    return np.ascontiguousarray(v.reshape(-1, 128).T)


def host_layout(inp, l):
    f32 = np.float32
    w = {}
    for tag, nm in (("a", "ffn1"), ("b", "ffn2")):
        w1 = inp[nm + "_w1"][l]
        w["w1r_" + tag] = np.ascontiguousarray(
            np.concatenate([w1[:, :DFF].reshape(D, NFC, 128), w1[:, DFF:].reshape(D, NFC, 128)], axis=2))
        w["w2_" + tag] = np.ascontiguousarray(inp[nm + "_w2"][l])
    win = inp["w_in"][l]
    w["winA"] = np.ascontiguousarray(np.concatenate([win[:, 0:512].reshape(D, 4, 128), win[:, 512:1024].reshape(D, 4, 128)], axis=2))
    w["winB"] = np.ascontiguousarray(win[:, 1024:1664])
    kp = win[:, 1664:1696]
    wk = np.zeros((D, 192), f32)
    wk[:, 64:96] = kp
    wk[:, 160:176] = kp[:, 16:32]
    wk[:, 176:192] = kp[:, 0:16]
    w["wkpe"] = wk
    w["winC"] = np.ascontiguousarray(np.concatenate([win[:, 1696:2208].reshape(D, 4, 128), win[:, 2208:2720].reshape(D, 4, 128)], axis=2))
    G = win[:, 2720:].reshape(D, 3, 8, 128)
    w["winG"] = np.ascontiguousarray(G.transpose(0, 2, 1, 3).reshape(D, 8, 384))
    br = np.stack([inp["lru_w_out"][l].reshape(512, 8, 128), inp["mla_w_o"][l].reshape(512, 8, 128),
                   inp["conv_w_out"][l].reshape(512, 8, 128)], axis=2)
    w["wbr"] = np.ascontiguousarray(br.reshape(512, 8, 384))
    uq = inp["w_uq"][l].reshape(384, 8, 96)
    w["wq"] = np.ascontiguousarray(np.concatenate([uq, uq[:, :, 80:96], uq[:, :, 64:80]], axis=2).reshape(384, 1024))
    w["wukv"] = np.ascontiguousarray(inp["w_ukv"][l])
    w["wout"] = np.ascontiguousarray(inp["w_out"][l])
    wg = inp["lru_w_gate"][l]
    gwd = np.zeros((128, 8, 128), f32)
    for j in range(4):
        for e in range(2):
            hd = 2 * j + e
            gwd[e * 64:(e + 1) * 64, 2 * j, e * 64:(e + 1) * 64] = wg[hd][:, 0:64]
            gwd[e * 64:(e + 1) * 64, 2 * j + 1, e * 64:(e + 1) * 64] = wg[hd][:, 64:128]
    w["gwd"] = gwd.reshape(128, 1024)
    dwl = inp["conv_dw_w"][l]
    dwd = np.zeros((128, 4, 31, 128), f32)
    pidx = np.arange(128)
    for j in range(4):
        dwd[pidx, j, :, pidx] = dwl[:, j * 128:(j + 1) * 128].T
    w["dwd"] = dwd.reshape(128, 4, 31 * 128)
    v = np.zeros((128, NVEC), f32)
    def put(name, arr):
        c = _VC[name]
        v[:, c:c + arr.shape[1]] = arr
    put("ffn1_norm", _fm(inp["ffn1_norm"][l])); put("mix_norm", _fm(inp["mix_norm"][l])); put("ffn2_norm", _fm(inp["ffn2_norm"][l]))
    bi = inp["b_in"][l]
    put("b_xa", _fm(bi[0:512])); put("b_ga", _fm(bi[512:1024])); put("b_cq", _fm(bi[1024:1408])); put("b_ckv", _fm(bi[1408:1664]))
    bk = np.zeros((128, 2), f32)
    bk[64:96, 0] = bi[1664:1696]
    bk[64:80, 1] = bi[1680:1696]
    bk[80:96, 1] = bi[1664:1680]
    put("b_kpe", bk)
    put("b_cv", _fm(bi[1696:2208])); put("b_cg", _fm(bi[2208:2720])); put("b_G", _fm(bi[2720:]))
    cw = inp["lru_conv_w"][l]
    put("lru_cw", np.ascontiguousarray(cw.reshape(4, 4, 128).transpose(2, 1, 0).reshape(128, 16)))
    put("lru_cb", _fm(inp["lru_conv_b"][l]))
    bg = inp["lru_b_gate"][l].reshape(4, 2, 2, 64)
    put("b_r", np.ascontiguousarray(bg[:, :, 0, :].transpose(1, 2, 0).reshape(128, 4)))
    put("b_i", np.ascontiguousarray(bg[:, :, 1, :].transpose(1, 2, 0).reshape(128, 4)))
    put("lam", _fm(inp["lru_lambda"][l])); put("q_norm", _fm(inp["q_norm"][l])); put("kv_norm", _fm(inp["kv_norm"][l]))
    dw = inp["conv_dw_w"][l]
    put("dw_w", np.ascontiguousarray(dw.reshape(31, 4, 128).transpose(2, 1, 0).reshape(128, 124)))
    put("dw_b", _fm(inp["conv_dw_b"][l])); put("ln_g", _fm(inp["conv_ln_g"][l])); put("ln_b", _fm(inp["conv_ln_b"][l]))
    put("cb_out", _fm(inp["conv_b_out"][l]))
    return w, v


def host_consts(inp, S):
    f32 = np.float32
    c = {}
    c["c_ident"] = np.eye(128, dtype=f32)
    c["c_tri"] = np.triu(np.ones((128, 128), f32))
    g = np.zeros((128, NGL), f32)
    g[:, 0:8] = _fm(inp["final_norm"])
    inv = (10000.0 ** (-np.arange(0, 32, 2, dtype=np.float32) / 32)).astype(f32)
    g[64:96, 8] = np.concatenate([inv, inv])
    g[64:80, 9] = -1.0
    g[80:96, 9] = 1.0
    c["c_glob"] = g
    return c


def make_in_maps(inp, S=2048, L=DEPTH, cores=range(8)):
    shared = {}
    for l in range(L):
        w, v = host_layout(inp, l)
        for n in _WNAMES:
            shared["%s_%d" % (n, l)] = w[n]
        shared["vecs%d" % l] = v
    shared.update(host_consts(inp, S))
    maps = []
    for b in cores:
        m = dict(shared)
        m["xT"] = np.ascontiguousarray(inp["x"][b, :S].T)
        m["pos_rep"] = np.ascontiguousarray(np.broadcast_to(inp["positions"][b, :S][None, :], (96, S))).astype(np.int32)
        maps.append(m)
    return maps


_CACHE = {}


def kernel(**inputs):
    inp = {k: np.asarray(v) for k, v in inputs.items()}
    if "prog" not in _CACHE:
        _CACHE["prog"] = build_program()[0]
    nc = _CACHE["prog"]
    maps = make_in_maps(inp)
    res = run_bass_kernel_spmd(nc, maps, core_ids=list(range(8)))
    out = np.stack([np.ascontiguousarray(r["outT"].T) for r in res.results], axis=0)
    return out.astype(np.float32)
```
